# Optimizing a Trainium2 kernel written in Bass

```python
import math
import jax, jax.numpy as jnp
from jax import lax
import numpy as np

D_MODEL = 1024
BATCH = 8
SEQ = 4096
DEPTH = 2

N_A = DEPTH // 2
N_B = DEPTH - N_A

EXPAND = 2
D_INNER = EXPAND * D_MODEL
SSM_HEAD_DIM = 64
SSM_HEADS = D_INNER // SSM_HEAD_DIM
N_GROUPS = 4
HEADS_PER_GROUP = SSM_HEADS // N_GROUPS
D_STATE = 128
CONV_K = 4
CONV_DIM = D_INNER + 2 * N_GROUPS * D_STATE
IN_PROJ_DIM = 2 * D_INNER + 2 * N_GROUPS * D_STATE + SSM_HEADS
CHUNK = 128
DT_MIN = 0.001
DT_MAX = 0.1

SB_HEAD_DIM = 64
SB_HEADS = D_MODEL // SB_HEAD_DIM
SB_WIDTH = SB_HEADS * SB_HEAD_DIM
SB_BLOCK = 128

PLE_DIM = 256

NORM_EPS = 1e-6

kernel_name = "yoco_mamba2_stickbreaking_hybrid"


def rms(x):
    xf = x.astype(jnp.float32)
    return (xf * lax.rsqrt(jnp.mean(xf * xf, axis=-1, keepdims=True) + NORM_EPS)).astype(x.dtype)


def rmsnorm(x, g):
    return rms(x) * g.astype(x.dtype)


def causal_depthwise_conv(u, w, b):
    k = w.shape[0]
    s = u.shape[1]
    up = jnp.pad(u, ((0, 0), (k - 1, 0), (0, 0)))
    out = b
    for j in range(k):
        out = out + w[j] * up[:, j:j + s]
    return out


def ssd_chunked_scan(xs, dt, A, Bm, Cm):
    bsz, s, _, _ = xs.shape
    nc = s // CHUNK
    G, HG, P, N, L = N_GROUPS, HEADS_PER_GROUP, SSM_HEAD_DIM, D_STATE, CHUNK
    xc = xs.astype(jnp.float32).reshape(bsz, nc, L, G, HG, P).transpose(1, 0, 2, 3, 4, 5)
    dtc = dt.reshape(bsz, nc, L, G, HG).transpose(1, 0, 2, 3, 4)
    Bc = Bm.astype(jnp.float32).reshape(bsz, nc, L, G, N).transpose(1, 0, 2, 3, 4)
    Cc = Cm.astype(jnp.float32).reshape(bsz, nc, L, G, N).transpose(1, 0, 2, 3, 4)
    Ag = A.reshape(G, HG)
    causal = jnp.tril(jnp.ones((L, L), dtype=bool))

    def step(state, inp):
        x_k, dt_k, B_k, C_k = inp
        acum = jnp.cumsum(dt_k * Ag, axis=1)
        seg = acum[:, :, None] - acum[:, None, :]
        decay = jnp.exp(jnp.where(causal[None, :, :, None, None], seg, -jnp.inf))
        xdt = x_k * dt_k[..., None]
        cb = jnp.einsum('blgn,bsgn->blsg', C_k, B_k)
        y_intra = jnp.einsum('blsgh,bsghp->blghp', cb[..., None] * decay, xdt)
        y_inter = jnp.einsum('blgn,bghpn->blghp', C_k, state) * jnp.exp(acum)[..., None]
        to_end = jnp.exp(acum[:, -1:] - acum)
        new_state = state * jnp.exp(acum[:, -1])[..., None, None] + jnp.einsum(
            'bsgn,bsghp->bghpn', B_k, xdt * to_end[..., None])
        return new_state, y_intra + y_inter

    state0 = jnp.zeros((bsz, G, HG, P, N), jnp.float32)
    _, y = lax.scan(step, state0, (xc, dtc, Bc, Cc))
    y = y.transpose(1, 0, 2, 3, 4, 5).reshape(bsz, s, SSM_HEADS, P)
    return y.astype(xs.dtype)


def mamba2_mixer(h, norm_g, w_in, conv_w, conv_b, dt_bias, A_log, d_skip, y_g, w_out):
    bsz, s, _ = h.shape
    u = rmsnorm(h, norm_g)
    zxbcdt = u @ w_in
    z = zxbcdt[..., :D_INNER]
    xbc = zxbcdt[..., D_INNER:D_INNER + CONV_DIM]
    dt = zxbcdt[..., D_INNER + CONV_DIM:]
    xbc = jax.nn.silu(causal_depthwise_conv(xbc, conv_w, conv_b))
    xs = xbc[..., :D_INNER].reshape(bsz, s, SSM_HEADS, SSM_HEAD_DIM)
    Bm = xbc[..., D_INNER:D_INNER + N_GROUPS * D_STATE].reshape(bsz, s, N_GROUPS, D_STATE)
    Cm = xbc[..., D_INNER + N_GROUPS * D_STATE:].reshape(bsz, s, N_GROUPS, D_STATE)
    dt = jax.nn.softplus(dt.astype(jnp.float32) + dt_bias.astype(jnp.float32))
    A = -jnp.exp(A_log.astype(jnp.float32))
    y = ssd_chunked_scan(xs, dt, A, Bm, Cm)
    y = y + d_skip.astype(y.dtype)[:, None] * xs
    y = y.reshape(bsz, s, D_INNER) * jax.nn.silu(z)
    y = rms(y.reshape(bsz, s, N_GROUPS, D_INNER // N_GROUPS)).reshape(bsz, s, D_INNER) * y_g
    return y @ w_out


def shared_kv(h, norm_g, w_kv, k_g):
    bsz, s, _ = h.shape
    kv = rmsnorm(h, norm_g) @ w_kv
    k = rmsnorm(kv[..., :SB_WIDTH].reshape(bsz, s, SB_HEADS, SB_HEAD_DIM), k_g)
    v = kv[..., SB_WIDTH:].reshape(bsz, s, SB_HEADS, SB_HEAD_DIM)
    return k.transpose(0, 2, 1, 3), v.transpose(0, 2, 1, 3)


def stick_breaking_mixer(h, k, v, norm_g, w_in, q_g, w_out):
    bsz, s, _ = h.shape
    qg = rmsnorm(h, norm_g) @ w_in
    gate = qg[..., SB_WIDTH:]
    q = rmsnorm(qg[..., :SB_WIDTH].reshape(bsz, s, SB_HEADS, SB_HEAD_DIM), q_g)
    q = (q * (1.0 / math.sqrt(SB_HEAD_DIM))).transpose(0, 2, 1, 3)
    outs = []
    for blk in range(s // SB_BLOCK):
        q0 = blk * SB_BLOCK
        kend = q0 + SB_BLOCK
        z = jnp.einsum('bhtd,bhsd->bhts', q[:, :, q0:kend], k[:, :, :kend]).astype(jnp.float32)
        t_idx = q0 + jnp.arange(SB_BLOCK)[:, None]
        s_idx = jnp.arange(kend)[None, :]
        strict = s_idx < t_idx
        log_keep = jnp.where(strict, jax.nn.log_sigmoid(-z), 0.0)
        suffix = lax.cumsum(log_keep, axis=3, reverse=True) - log_keep
        weights = jnp.where(strict, jnp.exp(jax.nn.log_sigmoid(z) + suffix), 0.0)
        outs.append(jnp.einsum('bhts,bhsd->bhtd', weights.astype(v.dtype), v[:, :, :kend]))
    o = jnp.concatenate(outs, axis=2).transpose(0, 2, 1, 3).reshape(bsz, s, SB_WIDTH)
    return (o * jax.nn.silu(gate)) @ w_out


def per_layer_embedding(h, p_i, norm_g, w_gate, w_proj):
    return h + (p_i @ w_proj) * jax.nn.sigmoid(rmsnorm(h, norm_g) @ w_gate)


def setup_inputs(seed: int = 0) -> dict:
    key = jax.random.key(seed)
    ks = jax.random.split(key, 24)

    def nrm(k, shape, scale):
        return jax.random.normal(k, shape, jnp.float32) * scale

    def gain(k, shape):
        return 1.0 + 0.05 * jax.random.normal(k, shape, jnp.float32)

    u = jax.random.uniform(ks[6], (N_A, SSM_HEADS), jnp.float32)
    dt = jnp.exp(u * (math.log(DT_MAX) - math.log(DT_MIN)) + math.log(DT_MIN))
    dt_bias = dt + jnp.log(-jnp.expm1(-dt))
    A_log = jnp.log(jax.random.uniform(ks[7], (N_A, SSM_HEADS), jnp.float32, minval=1.0, maxval=16.0))
    return {
        "x": nrm(ks[0], (BATCH, SEQ, D_MODEL), 1.0),
        "p": nrm(ks[1], (DEPTH, BATCH, SEQ, PLE_DIM), 1.0),
        "m_norm": gain(ks[2], (N_A, D_MODEL)),
        "m_in": nrm(ks[3], (N_A, D_MODEL, IN_PROJ_DIM), D_MODEL ** -0.5),
        "m_conv_w": nrm(ks[4], (N_A, CONV_K, CONV_DIM), CONV_K ** -0.5),
        "m_conv_b": nrm(ks[5], (N_A, CONV_DIM), 0.01),
        "m_dt_bias": dt_bias,
        "m_A_log": A_log,
        "m_D": gain(ks[8], (N_A, SSM_HEADS)),
        "m_ynorm": gain(ks[9], (N_A, D_INNER)),
        "m_out": nrm(ks[10], (N_A, D_INNER, D_MODEL), D_INNER ** -0.5),
        "kv_norm": gain(ks[11], (D_MODEL,)),
        "w_kv": nrm(ks[12], (D_MODEL, 2 * SB_WIDTH), D_MODEL ** -0.5),
        "k_norm": gain(ks[13], (SB_HEAD_DIM,)),
        "s_norm": gain(ks[14], (N_B, D_MODEL)),
        "s_in": nrm(ks[15], (N_B, D_MODEL, 2 * SB_WIDTH), D_MODEL ** -0.5),
        "q_norm": gain(ks[16], (N_B, SB_HEAD_DIM)),
        "s_out": nrm(ks[17], (N_B, SB_WIDTH, D_MODEL), SB_WIDTH ** -0.5),
        "ple_norm": gain(ks[18], (DEPTH, D_MODEL)),
        "ple_gate": nrm(ks[19], (DEPTH, D_MODEL, D_MODEL), D_MODEL ** -0.5),
        "ple_proj": nrm(ks[20], (DEPTH, PLE_DIM, D_MODEL), PLE_DIM ** -0.5),
    }


def reference(x, p, m_norm, m_in, m_conv_w, m_conv_b, m_dt_bias, m_A_log, m_D, m_ynorm, m_out,
              kv_norm, w_kv, k_norm, s_norm, s_in, q_norm, s_out, ple_norm, ple_gate, ple_proj):
    h = x
    k = v = None
    for i in range(DEPTH):
        if i < N_A:
            h = h + mamba2_mixer(h, m_norm[i], m_in[i], m_conv_w[i], m_conv_b[i], m_dt_bias[i],
                                 m_A_log[i], m_D[i], m_ynorm[i], m_out[i])
        else:
            if i == N_A:
                k, v = shared_kv(h, kv_norm, w_kv, k_norm)
            j = i - N_A
            h = h + stick_breaking_mixer(h, k, v, s_norm[j], s_in[j], q_norm[j], s_out[j])
        h = per_layer_embedding(h, p[i], ple_norm[i], ple_gate[i], ple_proj[i])
    return h
```

```python
from bisect import bisect_left
from contextlib import ExitStack
import os
import numpy as np
import concourse.bass as bass
import concourse.mybir as mybir
from concourse.alu_op_type import AluOpType as ALU
from concourse.bass_utils import run_bass_kernel_spmd

F32 = mybir.dt.float32
BF16 = mybir.dt.bfloat16
AF = mybir.ActivationFunctionType

S = 4096
D = 1024
_DBG_STOP = int(os.environ.get('P1B_STOP', '99'))
NCORES = 8
EPS = 1e-6


class Eng:
    def __init__(self, name, eng, sem, eager):
        self.name, self.eng, self.sem, self.eager = name, eng, sem, eager
        self.n = 0
        self.count = 0
        self.sig_idx = []
        self.sig_cnt = []
        self.last = None
        self.last_signaled = True
        self.known = {}

    def signal_last(self):
        if not self.last_signaled:
            self.last.then_inc(self.sem, 1)
            self.count += 1
            self.sig_idx.append(self.n)
            self.sig_cnt.append(self.count)
            self.last_signaled = True

    def count_for(self, idx):
        i = bisect_left(self.sig_idx, idx)
        if i == len(self.sig_idx):
            self.signal_last()
            i = len(self.sig_idx) - 1
        assert self.sig_idx[i] >= idx
        return self.sig_cnt[i]


class DSem:
    def __init__(self, sem):
        self.sem = sem
        self.issued = 0


class T:
    def __init__(self, name, ap, space):
        self.name, self.ap, self.space = name, ap, space
        self.w = None
        self.r = {}
        self.dsem = None

    def __getitem__(self, k):
        return self.ap[k]


class Ring:
    def __init__(self, tiles):
        self.t = tiles
        self.i = 0

    def next(self):
        t = self.t[self.i % len(self.t)]
        self.i += 1
        return t


class Ctx:
    def __init__(self, nc, es):
        self.nc, self.es = nc, es
        mk = lambda n: es.enter_context(nc.semaphore(n))
        self.pe = Eng("pe", nc.tensor, mk("s_pe"), False)
        self.act = Eng("act", nc.scalar, mk("s_act"), True)
        self.dve = Eng("dve", nc.vector, mk("s_dve"), True)
        self.pool = Eng("pool", nc.gpsimd, mk("s_pool"), True)
        self.sp = Eng("sp", nc.sync, mk("s_sp"), True)
        self.engs = [self.pe, self.act, self.dve, self.pool, self.sp]
        self.dsems = []
        self.free_dsems = []
        self.nsem = 0
        self.stack = es

    def sb(self, name, shape, dtype, es=None):
        t = (es or self.stack).enter_context(self.nc.sbuf_tensor(name, shape, dtype))
        return T(name, t.ap(), "sb")

    def ps(self, name, shape, dtype, es=None):
        t = (es or self.stack).enter_context(self.nc.psum_tensor(name, shape, dtype))
        return T(name, t.ap(), "ps")

    def dram(self, name, ap):
        return T(name, ap, "dram")

    def view(self, name, ap, space="sb"):
        return T(name, ap, space)

    def begin_phase(self):
        self.phase_dsems = []

    def end_phase(self):
        self.barrier()
        self.free_dsems.extend(self.phase_dsems)
        self.phase_dsems = None

    def new_dsem(self):
        if self.free_dsems:
            d = self.free_dsems.pop()
        else:
            self.nsem += 1
            d = DSem(self.es.enter_context(self.nc.semaphore("s_d%d" % self.nsem)))
            self.dsems.append(d)
        if getattr(self, "phase_dsems", None) is not None:
            self.phase_dsems.append(d)
        return d

    def _waits(self, E, reads, writes):
        need = {}

        def add(ev, same_ok):
            if ev is None:
                return
            if ev[0] == "e":
                Dn, idx = ev[1], ev[2]
                if Dn is E and same_ok:
                    return
                c = Dn.count_for(idx)
                sem = Dn.sem
            else:
                c = ev[1].issued * 16
                sem = ev[1].sem
            key = id(sem)
            if need.get(key, (None, 0))[1] < c:
                need[key] = (sem, c)

        for t in reads:
            add(t.w, E is self.pe)
        for t in writes:
            add(t.w, True)
            for ev in t.r.values():
                add(ev, True)
        for key, (sem, c) in need.items():
            if E.known.get(key, 0) >= c:
                continue
            E.eng.wait_ge(sem, c)
            E.known[key] = c

    def op(self, E, make, reads=(), writes=()):
        writes = list(writes) + [t for t in reads if t.space == "ps"]
        reads = [t for t in reads if t.space != "ps"]
        self._waits(E, reads, writes)
        inst = make()
        E.n += 1
        E.last = inst
        E.last_signaled = False
        if E.eager:
            E.signal_last()
        ev = ("e", E, E.n)
        for t in reads:
            t.r[id(E)] = ev
        for t in writes:
            t.w = ev
            t.r = {}
        return inst

    def dma(self, Q, out_ap, in_ap, reads=(), writes=(), dsem=None, **kw):
        self._waits(Q, reads, writes)
        if dsem is None:
            cand = ([t for t in writes if t.space != "dram"] or [t for t in reads if t.space != "dram"]
                    or list(writes) or list(reads))
            t0 = cand[0]
            if t0.dsem is None:
                t0.dsem = self.new_dsem()
            dsem = t0.dsem
        inst = Q.eng.dma_start(out=out_ap, in_=in_ap, **kw)
        inst.then_inc(dsem.sem, 16)
        dsem.issued += 1
        ev = ("d", dsem)
        for t in reads:
            t.r[id(dsem)] = ev
        for t in writes:
            t.w = ev
            t.r = {}
        return inst

    def barrier(self):
        for E in self.engs:
            E.signal_last()
        for E in self.engs:
            for Dn in self.engs:
                if Dn is E or Dn.count == 0:
                    continue
                key = id(Dn.sem)
                if E.known.get(key, 0) < Dn.count:
                    E.eng.wait_ge(Dn.sem, Dn.count)
                    E.known[key] = Dn.count
            for d in self.dsems:
                if d.issued:
                    key = id(d.sem)
                    if E.known.get(key, 0) < d.issued * 16:
                        E.eng.wait_ge(d.sem, d.issued * 16)
                        E.known[key] = d.issued * 16

    def finish(self):
        for d in self.dsems:
            if d.issued:
                self.sp.eng.wait_ge(d.sem, d.issued * 16)


class Consts:
    pass


def make_consts(K, nc, prm):
    C = Consts()
    onesf = K.sb("c_onesf", [128, 128], F32)
    K.op(K.pool, lambda: nc.gpsimd.memset(onesf.ap, 1.0), writes=[onesf])
    C.onesf = onesf
    identf = K.sb("c_identf", [128, 128], F32)
    K.op(K.pool, lambda: nc.gpsimd.affine_select(identf.ap, onesf.ap, pattern=[[1, 128]], compare_op=ALU.is_equal,
                                                 fill=0.0, base=0, channel_multiplier=-1), reads=[onesf], writes=[identf])
    C.identf = identf
    identb = K.sb("c_identb", [128, 128], BF16)
    K.op(K.pool, lambda: nc.gpsimd.tensor_copy(identb.ap, identf.ap), reads=[identf], writes=[identb])
    C.identb = identb
    onesb = K.sb("c_onesb", [128, 128], BF16)
    K.op(K.pool, lambda: nc.gpsimd.memset(onesb.ap, 1.0), writes=[onesb])
    C.onesb = onesb
    tri = K.sb("c_tri", [128, 128], F32)
    K.op(K.pool, lambda: nc.gpsimd.affine_select(tri.ap, onesf.ap, pattern=[[1, 128]], compare_op=ALU.is_ge,
                                                 fill=0.0, base=0, channel_multiplier=-1), reads=[onesf], writes=[tri])
    C.tri = tri
    epsc = K.sb("c_eps", [128, 1], F32)
    K.op(K.pool, lambda: nc.gpsimd.memset(epsc.ap, EPS), writes=[epsc])
    C.eps = epsc
    onec = K.sb("c_one", [128, 1], F32)
    K.op(K.pool, lambda: nc.gpsimd.memset(onec.ap, 1.0), writes=[onec])
    C.one = onec
    blk = K.sb("c_blk", [128, 128], BF16)
    K.op(K.pool, lambda: nc.gpsimd.memset(blk.ap, 0.0), writes=[blk])
    K.op(K.pool, lambda: nc.gpsimd.memset(blk.ap[0:64, 0:64], 1.0), writes=[blk])
    K.op(K.pool, lambda: nc.gpsimd.memset(blk.ap[64:128, 64:128], 1.0), writes=[blk])
    C.blk = blk

    def vec_cols(name, items):
        R = sum(n for _, n in items)
        st = K.sb("vst_" + name, [R, 128], F32)
        r0 = 0
        for ap1, n in items:
            K.dma(K.sp, st.ap[r0:r0 + n, :], ap1.rearrange("(c p) -> c p", p=128), writes=[st])
            r0 += n
        pt = K.ps("vps_" + name, [128, 512], F32)
        K.op(K.pe, lambda: nc.tensor.transpose(pt.ap[:, 0:R], st.ap, identf.ap[0:R, 0:R]), reads=[st, identf], writes=[pt])
        out = K.sb("vec_" + name, [128, R], F32)
        K.op(K.dve, lambda: nc.vector.tensor_copy(out.ap, pt.ap[:, 0:R]), reads=[pt], writes=[out])
        return out

    veca = K.sb("c_veca", [128, 80], F32)
    vecb = K.sb("c_vecb", [128, 96], F32)
    with ExitStack() as es2:
        K.stack = es2
        va = vec_cols("a", [(prm["m_norm"][0], 8), (prm["kv_norm"], 8), (prm["s_norm"][0], 8),
                            (prm["ple_norm"][0], 8), (prm["ple_norm"][1], 8), (prm["m_ynorm"][0], 16),
                            (prm["m_conv_b"][0], 24)])
        vb = vec_cols("b", [(prm["m_conv_w"][0, j], 24) for j in range(4)])
        K.op(K.dve, lambda: nc.vector.tensor_copy(veca.ap, va.ap), reads=[va], writes=[veca])
        K.op(K.dve, lambda: nc.vector.tensor_copy(vecb.ap, vb.ap), reads=[vb], writes=[vecb])
        K.barrier()
        K.stack = K.es
    C.g_m, C.g_kv, C.g_s = veca.ap[:, 0:8], veca.ap[:, 8:16], veca.ap[:, 16:24]
    C.g_p0, C.g_p1 = veca.ap[:, 24:32], veca.ap[:, 32:40]
    C.g_y, C.conv_b = veca.ap[:, 40:56], veca.ap[:, 56:80]
    C.conv_w = vecb.ap
    C.veca, C.vecb = veca, vecb

    kq = K.sb("c_kq", [128, 2], F32)
    for half in range(2):
        K.dma(K.sp, kq.ap[half * 64:(half + 1) * 64, 0:1], prm["k_norm"].rearrange("(p o) -> p o", o=1), writes=[kq])
        K.dma(K.sp, kq.ap[half * 64:(half + 1) * 64, 1:2], prm["q_norm"][0].rearrange("(p o) -> p o", o=1), writes=[kq])
    kq2 = K.sb("c_kq2", [128, 2], F32)
    K.op(K.dve, lambda: nc.vector.tensor_copy(kq2.ap[:, 0:1], kq.ap[:, 0:1]), reads=[kq], writes=[kq2])
    K.op(K.dve, lambda: nc.vector.tensor_scalar(kq2.ap[:, 1:2], kq.ap[:, 1:2], 0.125, None, op0=ALU.mult), reads=[kq], writes=[kq2])
    C.kq = kq2
    bc = K.sb("c_bc", [128, 3, 32], F32)
    K.dma(K.sp, bc.ap[:, 0, :], prm["m_dt_bias"][0:1, :].to_broadcast([128, 32]), writes=[bc])
    K.dma(K.sp, bc.ap[:, 1, :], prm["m_A_log"][0:1, :].to_broadcast([128, 32]), writes=[bc])
    K.dma(K.sp, bc.ap[:, 2, :], prm["m_D"][0:1, :].to_broadcast([128, 32]), writes=[bc])
    C.bc = bc
    Abc = K.sb("c_A", [128, 32], F32)
    K.op(K.act, lambda: nc.scalar.activation(Abc.ap, bc.ap[:, 1, :], AF.Exp), reads=[bc], writes=[Abc])
    K.op(K.dve, lambda: nc.vector.tensor_scalar(Abc.ap, Abc.ap, -1.0, None, op0=ALU.mult), reads=[Abc], writes=[Abc])
    C.A = Abc
    Dcol = K.sb("c_Dcol", [128, 16], F32)
    dv = bc.ap[:, 2, :].rearrange("p (q r) -> p q r", r=2)
    K.op(K.dve, lambda: nc.vector.tensor_copy(Dcol.ap[0:64, :], dv[0:64, :, 0]), reads=[bc], writes=[Dcol])
    K.op(K.dve, lambda: nc.vector.tensor_copy(Dcol.ap[64:128, :], dv[64:128, :, 1]), reads=[bc], writes=[Dcol])
    C.Dcol = Dcol
    return C


def load_weight(K, nc, es, name, src, C_, N, stage_ring, cast_engs):
    w = K.sb(name, [128, C_, N], BF16, es)
    views = [K.view("%s_%d" % (name, c), w.ap[:, c, :]) for c in range(C_)]
    SW = stage_ring.t[0].ap.shape[1]
    i = 0
    for c in range(C_):
        for n0 in range(0, N, SW):
            n1 = min(N, n0 + SW)
            st = stage_ring.next()
            K.dma(K.sp, st.ap[:, 0:n1 - n0], src[c * 128:(c + 1) * 128, n0:n1], writes=[st])
            E = cast_engs[i % len(cast_engs)]
            i += 1
            if E is K.act:
                K.op(E, lambda: nc.scalar.copy(views[c].ap[:, n0:n1], st.ap[:, 0:n1 - n0]), reads=[st], writes=[views[c]])
            else:
                K.op(E, lambda: E.eng.tensor_copy(views[c].ap[:, n0:n1], st.ap[:, 0:n1 - n0]), reads=[st], writes=[views[c]])
    return views


def rms_to_xT(K, nc, C, ht, na, ss, lnv, rstd, junk, xh, ptr_ring, outs):
    for a in range(na):
        K.op(K.act, lambda: nc.scalar.activation(junk.ap, ht.ap[:, a, :], AF.Square, accum_out=ss.ap[:, a:a + 1]),
             reads=[ht], writes=[junk, ss])
    K.op(K.act, lambda: nc.scalar.activation(lnv.ap[:, 0:na], ss.ap[:, 0:na], AF.Ln, bias=C.eps.ap, scale=1.0 / D),
         reads=[ss, C.eps], writes=[lnv])
    K.op(K.act, lambda: nc.scalar.activation(rstd.ap[:, 0:na], lnv.ap[:, 0:na], AF.Exp, scale=-0.5), reads=[lnv], writes=[rstd])
    for a in range(na):
        K.op(K.dve, lambda: nc.vector.tensor_scalar(xh.ap[:, a, :], ht.ap[:, a, :], rstd.ap[:, a:a + 1], None, op0=ALU.mult),
             reads=[ht, rstd], writes=[xh])
    k = 0
    for c in range(8):
        pt = ptr_ring.next()
        for a in range(na):
            K.op(K.pe, lambda: nc.tensor.transpose(pt.ap[:, a * 128:(a + 1) * 128], xh.ap[:, a, c * 128:(c + 1) * 128], C.identb.ap),
                 reads=[xh, C.identb], writes=[pt])
        for (xT, g) in outs:
            if k % 2 == 0:
                K.op(K.act, lambda: nc.scalar.activation(xT.ap[:, c, :], pt.ap[:, 0:na * 128], AF.Identity, scale=g[:, c:c + 1]),
                     reads=[pt, C.veca], writes=[xT])
            else:
                K.op(K.dve, lambda: nc.vector.tensor_scalar(xT.ap[:, c, :], pt.ap[:, 0:na * 128], g[:, c:c + 1], None, op0=ALU.mult),
                     reads=[pt, C.veca], writes=[xT])
            k += 1


def phase_p1a(K, nc, C, x_d, w_in, sz_scr, xbc_scr, dt_scr):
    with ExitStack() as es:
        K.stack = es
        K.begin_phase()
        stage = Ring([K.sb("p1a_st%d" % i, [128, 1288], F32) for i in range(2)])
        W = load_weight(K, nc, es, "p1a_w", w_in, 8, 5152, stage, [K.pool, K.dve, K.act])
        xt_ring = Ring([K.sb("p1a_x%d" % i, [128, 4, 1024], F32) for i in range(1)])
        ss = K.sb("p1a_ss", [128, 4], F32)
        lnv = K.sb("p1a_ln", [128, 4], F32)
        rstd = K.sb("p1a_rstd", [128, 4], F32)
        junk = K.sb("p1a_junk", [128, 1024], BF16)
        xh = K.sb("p1a_xh", [128, 4, 1024], BF16)
        xT = K.sb("p1a_xT", [128, 8, 512], BF16)
        ptr = Ring([K.ps("p1a_pt%d" % i, [128, 1024], BF16) for i in range(2)])
        pmm = Ring([K.ps("p1a_pm%d" % i, [128, 512], F32) for i in range(4)])
        pdt = K.ps("p1a_pdt", [128, 512], F32)
        ost_ring = Ring([K.sb("p1a_ost%d" % i, [128, 512], BF16) for i in range(6)])
        raw_ring = Ring([K.sb("p1a_raw%d" % i, [128, 515], F32) for i in range(3)])
        acc_ring = Ring([K.sb("p1a_acc%d" % i, [128, 512], F32) for i in range(2)])
        halo = K.sb("p1a_halo", [128, 24, 3], F32)
        halos = [K.view("p1a_halo%d" % i, halo.ap[:, i, :]) for i in range(24)]
        K.op(K.pool, lambda: nc.gpsimd.memset(halo.ap, 0.0), writes=halos)
        dtx = K.sb("p1a_dtx", [128, 4, 32], F32)
        dta = K.sb("p1a_dta", [128, 4, 32], F32)
        dte = K.sb("p1a_dte", [128, 4, 32], F32)
        dtm = K.sb("p1a_dtm", [128, 4, 32], F32)
        dto = K.sb("p1a_dto", [128, 4, 32], F32)
        szd = K.dram("sz_scr", sz_scr)
        xbd = K.dram("xbc_scr", xbc_scr)
        dtd = K.dram("dt_scr", dt_scr)
        xv = x_d.rearrange("(t a p) d -> t p a d", a=4, p=128)
        for ts in range(8):
            xt = xt_ring.next()
            K.dma(K.sp, xt.ap, xv[ts], writes=[xt])
            rms_to_xT(K, nc, C, xt, 4, ss, lnv, rstd, junk, xh, ptr, [(xT, C.g_m)])
            tok = slice(ts * 512, (ts + 1) * 512)
            for ft in range(40):
                pm = pmm.next()
                for c in range(8):
                    K.op(K.pe, lambda: nc.tensor.matmul(pm.ap, W[c].ap[:, ft * 128:(ft + 1) * 128], xT.ap[:, c, :],
                                                        start=(c == 0), stop=(c == 7)), reads=[W[c], xT], writes=[pm])
                ost = ost_ring.next()
                if ft < 16:
                    K.op(K.act, lambda: nc.scalar.activation(ost.ap, pm.ap, AF.Silu), reads=[pm], writes=[ost])
                    K.dma(K.sp, sz_scr[ft * 128:(ft + 1) * 128, tok], ost.ap, reads=[ost], writes=[szd])
                else:
                    ci = ft - 16
                    raw = raw_ring.next()
                    acc = acc_ring.next()
                    K.op(K.pool, lambda: nc.gpsimd.tensor_copy(raw.ap[:, 0:3], halos[ci].ap), reads=[halos[ci]], writes=[raw])
                    K.op(K.act, lambda: nc.scalar.copy(raw.ap[:, 3:515], pm.ap), reads=[pm], writes=[raw])
                    K.op(K.pool, lambda: nc.gpsimd.tensor_copy(halos[ci].ap, raw.ap[:, 512:515]), reads=[raw], writes=[halos[ci]])
                    cw = lambda j: C.conv_w[:, j * 24 + ci:j * 24 + ci + 1]
                    K.op(K.dve, lambda: nc.vector.tensor_scalar(acc.ap, raw.ap[:, 0:512], cw(0), None, op0=ALU.mult),
                         reads=[raw, C.vecb], writes=[acc])
                    for j in range(1, 4):
                        K.op(K.dve, lambda: nc.vector.scalar_tensor_tensor(acc.ap, raw.ap[:, j:j + 512], cw(j), acc.ap,
                                                                           op0=ALU.mult, op1=ALU.add),
                             reads=[raw, acc, C.vecb], writes=[acc])
                    K.op(K.act, lambda: nc.scalar.activation(ost.ap, acc.ap, AF.Silu, bias=C.conv_b[:, ci:ci + 1]),
                         reads=[acc, C.veca], writes=[ost])
                    K.dma(K.sp, xbc_scr[ci * 128:(ci + 1) * 128, tok], ost.ap, reads=[ost], writes=[xbd])
            for a in range(4):
                for c in range(8):
                    K.op(K.pe, lambda: nc.tensor.matmul(pdt.ap[:, a * 32:(a + 1) * 32], xT.ap[:, c, a * 128:(a + 1) * 128],
                                                        W[c].ap[:, 5120:5152], start=(c == 0), stop=(c == 7)),
                         reads=[W[c], xT], writes=[pdt])
            pv = pdt.ap[:, 0:128].rearrange("p (a h) -> p a h", a=4)
            K.op(K.dve, lambda: nc.vector.tensor_tensor(dtx.ap, pv, C.bc.ap[:, 0:1, :].to_broadcast([128, 4, 32]), op=ALU.add),
                 reads=[pdt, C.bc], writes=[dtx])
            K.op(K.act, lambda: nc.scalar.activation(dta.ap, dtx.ap, AF.Abs), reads=[dtx], writes=[dta])
            K.op(K.act, lambda: nc.scalar.activation(dte.ap, dta.ap, AF.Exp, scale=-1.0), reads=[dta], writes=[dte])
            K.op(K.act, lambda: nc.scalar.activation(dte.ap, dte.ap, AF.Ln, bias=C.one.ap, scale=1.0), reads=[dte, C.one], writes=[dte])
            K.op(K.dve, lambda: nc.vector.tensor_scalar(dtm.ap, dtx.ap, 0.0, None, op0=ALU.max), reads=[dtx], writes=[dtm])
            K.op(K.dve, lambda: nc.vector.tensor_tensor(dto.ap, dtm.ap, dte.ap, op=ALU.add), reads=[dtm, dte], writes=[dto])
            K.dma(K.sp, dt_scr.rearrange("(t a p) h -> t p a h", a=4, p=128)[ts], dto.ap, reads=[dto], writes=[dtd])
        K.end_phase()
        K.stack = K.es


def phase_p1b(K, nc, C, sz_scr, xbc_scr, dt_scr, yn_scr, nchunks=32):
    with ExitStack() as es:
        K.stack = es
        K.begin_phase()
        xb_ring = Ring([K.sb("p1b_xb%d" % i, [128, 24, 128], BF16) for i in range(2)])
        sz_ring = Ring([K.sb("p1b_sz%d" % i, [128, 16, 128], BF16) for i in range(2)])
        dt_ring = Ring([K.sb("p1b_dt%d" % i, [128, 32], F32) for i in range(2)])
        a_ch = K.sb("p1b_a", [128, 32], F32)
        acum = K.sb("p1b_acum", [128, 32], F32)
        tmpw = K.sb("p1b_tmpw", [128, 32], F32)
        wend = K.sb("p1b_wend", [128, 32], F32)
        dtw = K.sb("p1b_dtw", [128, 32], F32)
        dA = K.sb("p1b_dA", [128, 32], F32)
        xdt_pad = K.sb("p1b_xdtp", [128, 32, 128], BF16)
        xdtw = K.sb("p1b_xdtw", [128, 32, 64], BF16)
        btok = K.sb("p1b_btok", [128, 4, 128], BF16)
        state = K.sb("p1b_state", [128, 32, 64], F32)
        st_pad = K.sb("p1b_stp", [128, 32, 128], BF16)
        cbm = K.sb("p1b_cbm", [128, 4, 128], F32)
        seg_ring = Ring([K.sb("p1b_seg%d" % i, [128, 4, 128], F32) for i in range(2)])
        dec_ring = Ring([K.sb("p1b_dec%d" % i, [128, 4, 128], F32) for i in range(2)])
        ea_ring = Ring([K.sb("p1b_ea%d" % i, [128, 4, 128], F32) for i in range(2)])
        mt_ring = Ring([K.sb("p1b_mt%d" % i, [128, 4, 128], BF16) for i in range(3)])
        cs_ring = Ring([K.sb("p1b_cs%d" % i, [128, 4, 128], BF16) for i in range(3)])
        ytmp_ring = Ring([K.sb("p1b_yt%d" % i, [128, 128], F32) for i in range(2)])
        ygate = K.sb("p1b_yg", [128, 16, 128], F32)
        sq = K.sb("p1b_sq", [128, 16, 128], BF16)
        grs = K.sb("p1b_grs", [128, 4, 128], F32)
        yn_ring = Ring([K.sb("p1b_yn%d" % i, [128, 16, 128], BF16) for i in range(2)])
        pxs = K.ps("p1b_pxs", [128, 2048], BF16)
        pb = K.ps("p1b_pb", [128, 1024], BF16)
        psm = K.ps("p1b_psm", [128, 512], F32)
        pabc = Ring([K.ps("p1b_pabc%d" % i, [128, 512], F32) for i in range(2)])
        py = K.ps("p1b_py", [128, 512], F32)
        pyv = [K.view("p1b_py%d" % i, py.ap[:, i * 128:(i + 1) * 128], "ps") for i in range(4)]
        pupd = K.ps("p1b_pupd", [128, 512], F32)
        for t_ in (xdt_pad, st_pad, state):
            K.op(K.pool, lambda: nc.gpsimd.memset(t_.ap, 0.0), writes=[t_])
        szd = K.dram("sz_scr", sz_scr)
        xbd = K.dram("xbc_scr", xbc_scr)
        dtd = K.dram("dt_scr", dt_scr)
        ynd = K.dram("yn_scr", yn_scr)
        xbv = xbc_scr.rearrange("(q p) t -> p q t", p=128)
        szv = sz_scr.rearrange("(q p) t -> p q t", p=128)
        ynv = yn_scr.rearrange("(q p) t -> p q t", p=128)
        xdt4 = xdt_pad.ap.rearrange("p (q r) c -> p q r c", r=2)
        stp4 = st_pad.ap.rearrange("p (q r) c -> p q r c", r=2)
        for ch in range(nchunks):
            cols = slice(ch * 128, (ch + 1) * 128)
            xb = xb_ring.next()
            sz = sz_ring.next()
            dt = dt_ring.next()
            K.dma(K.sp, xb.ap, xbv[:, :, cols], reads=[xbd], writes=[xb])
            K.dma(K.sp, sz.ap, szv[:, :, cols], reads=[szd], writes=[sz])
            K.dma(K.sp, dt.ap, dt_scr[ch * 128:(ch + 1) * 128, :], reads=[dtd], writes=[dt])
            if _DBG_STOP <= 1:
                continue
            K.op(K.dve, lambda: nc.vector.tensor_tensor(a_ch.ap, dt.ap, C.A.ap, op=ALU.mult), reads=[dt, C.A], writes=[a_ch])
            K.op(K.pe, lambda: nc.tensor.matmul(psm.ap[:, 0:32], C.tri.ap, a_ch.ap, start=True, stop=True),
                 reads=[C.tri, a_ch], writes=[psm])
            K.op(K.pe, lambda: nc.tensor.matmul(psm.ap[:, 32:64], C.onesf.ap, a_ch.ap, start=True, stop=True),
                 reads=[C.onesf, a_ch], writes=[psm])
            K.op(K.act, lambda: nc.scalar.copy(acum.ap, psm.ap[:, 0:32]), reads=[psm], writes=[acum])
            K.op(K.dve, lambda: nc.vector.tensor_tensor(tmpw.ap, psm.ap[:, 32:64], acum.ap, op=ALU.subtract),
                 reads=[psm, acum], writes=[tmpw])
            K.op(K.act, lambda: nc.scalar.activation(wend.ap, tmpw.ap, AF.Exp), reads=[tmpw], writes=[wend])
            K.op(K.act, lambda: nc.scalar.activation(dA.ap, psm.ap[:, 32:64], AF.Exp), reads=[psm], writes=[dA])
            K.op(K.dve, lambda: nc.vector.tensor_tensor(dtw.ap, dt.ap, wend.ap, op=ALU.mult), reads=[dt, wend], writes=[dtw])
            if _DBG_STOP <= 2:
                continue
            for ci in range(16):
                K.op(K.pe, lambda: nc.tensor.transpose(pxs.ap[:, ci * 128:(ci + 1) * 128], xb.ap[:, ci, :], C.identb.ap),
                     reads=[xb, C.identb], writes=[pxs])
            for g in range(4):
                K.op(K.pe, lambda: nc.tensor.transpose(pb.ap[:, g * 128:(g + 1) * 128], xb.ap[:, 16 + g, :], C.identb.ap),
                     reads=[xb, C.identb], writes=[pb])
            pxs4 = pxs.ap.rearrange("p (q r c) -> p q r c", r=2, c=64)
            dt3 = dt.ap.rearrange("p (q r) -> p q r", r=2)
            for r in range(2):
                K.op(K.dve, lambda: nc.vector.tensor_tensor(xdt4[:, :, r, r * 64:(r + 1) * 64], pxs4[:, :, r, :],
                                                            dt3[:, :, r:r + 1].to_broadcast([128, 16, 64]), op=ALU.mult),
                     reads=[pxs, dt], writes=[xdt_pad])
            K.op(K.dve, lambda: nc.vector.tensor_tensor(xdtw.ap, pxs.ap.rearrange("p (h c) -> p h c", c=64),
                                                        dtw.ap.unsqueeze(2).to_broadcast([128, 32, 64]), op=ALU.mult),
                 reads=[pxs, dtw], writes=[xdtw])
            K.op(K.act, lambda: nc.scalar.copy(btok.ap.rearrange("p g n -> p (g n)"), pb.ap[:, 0:512]), reads=[pb], writes=[btok])
            if _DBG_STOP <= 3:
                continue
            for g in range(4):
                K.op(K.pe, lambda: nc.tensor.matmul(pupd.ap[:, g * 128:(g + 1) * 128],
                                                    xb.ap[:, 16 + g, :], xb.ap[:, 20 + g, :], start=True, stop=True),
                     reads=[xb], writes=[pupd])
            for g in range(4):
                K.op(K.dve, lambda: nc.vector.tensor_tensor(cbm.ap[:, g, :], pupd.ap[:, g * 128:(g + 1) * 128], C.tri.ap, op=ALU.mult),
                     reads=[pupd, C.tri], writes=[cbm])
            if _DBG_STOP <= 4:
                continue
            mts = {}
            css = {}
            for hq in range(8):
                g = hq // 2
                pa = pabc.next()
                for i in range(4):
                    h = hq * 4 + i
                    K.op(K.pe, lambda: nc.tensor.matmul(pa.ap[:, i * 128:(i + 1) * 128], a_ch.ap[:, h:h + 1].to_broadcast([128, 128]),
                                                        C.tri.ap, start=True, stop=True), reads=[a_ch, C.tri], writes=[pa])
                seg = seg_ring.next()
                dec = dec_ring.next()
                ea = ea_ring.next()
                mt = mt_ring.next()
                cs = cs_ring.next()
                for i in range(4):
                    h = hq * 4 + i
                    K.op(K.dve, lambda: nc.vector.tensor_scalar(seg.ap[:, i, :], pa.ap[:, i * 128:(i + 1) * 128], acum.ap[:, h:h + 1], 0.0,
                                                                op0=ALU.subtract, op1=ALU.min), reads=[pa, acum], writes=[seg])
                K.op(K.act, lambda: nc.scalar.activation(dec.ap, seg.ap, AF.Exp), reads=[seg], writes=[dec])
                K.op(K.act, lambda: nc.scalar.activation(ea.ap.rearrange("p i l -> p (i l)"), pa.ap, AF.Exp), reads=[pa], writes=[ea])
                for i in range(4):
                    K.op(K.pool, lambda: nc.gpsimd.tensor_tensor(mt.ap[:, i, :], dec.ap[:, i, :], cbm.ap[:, g, :], op=ALU.mult),
                         reads=[dec, cbm], writes=[mt])
                    K.op(K.pool, lambda: nc.gpsimd.tensor_tensor(cs.ap[:, i, :], ea.ap[:, i, :], xb.ap[:, 20 + g, :], op=ALU.mult),
                         reads=[ea, xb], writes=[cs])
                for pi in range(2):
                    q = hq * 2 + pi
                    pyq = py
                    yo = py.ap[:, (q % 4) * 128:(q % 4 + 1) * 128]
                    ops = [(xdt_pad, xdt_pad.ap[:, 2 * q, :], mt, mt.ap[:, 2 * pi, :]),
                           (xdt_pad, xdt_pad.ap[:, 2 * q + 1, :], mt, mt.ap[:, 2 * pi + 1, :]),
                           (st_pad, st_pad.ap[:, 2 * q, :], cs, cs.ap[:, 2 * pi, :]),
                           (st_pad, st_pad.ap[:, 2 * q + 1, :], cs, cs.ap[:, 2 * pi + 1, :])]
                    for k_, (lt, la, rt, ra) in enumerate(ops):
                        K.op(K.pe, lambda: nc.tensor.matmul(yo, la, ra, start=(k_ == 0), stop=(k_ == 3)), reads=[lt, rt], writes=[pyq])
                    yt = ytmp_ring.next()
                    K.op(K.dve, lambda: nc.vector.scalar_tensor_tensor(yt.ap, xb.ap[:, q, :], C.Dcol.ap[:, q:q + 1], yo,
                                                                       op0=ALU.mult, op1=ALU.add), reads=[xb, C.Dcol, pyq], writes=[yt])
                    K.op(K.pool, lambda: nc.gpsimd.tensor_tensor(ygate.ap[:, q, :], yt.ap, sz.ap[:, q, :], op=ALU.mult),
                         reads=[yt, sz], writes=[ygate])
            if _DBG_STOP <= 5:
                continue
            for g in range(4):
                K.op(K.pe, lambda: nc.tensor.matmul(pupd.ap, btok.ap[:, g, :], xdtw.ap[:, g * 8:(g + 1) * 8, :].rearrange("p h c -> p (h c)"),
                                                    start=True, stop=True), reads=[btok, xdtw], writes=[pupd])
                sv = state.ap[:, g * 8:(g + 1) * 8, :]
                K.op(K.dve, lambda: nc.vector.tensor_tensor(sv, sv, dA.ap[:, g * 8:(g + 1) * 8].unsqueeze(2).to_broadcast([128, 8, 64]),
                                                            op=ALU.mult), reads=[state, dA], writes=[state])
                K.op(K.dve, lambda: nc.vector.tensor_tensor(sv, sv, pupd.ap.rearrange("p (h c) -> p h c", c=64), op=ALU.add),
                     reads=[state, pupd], writes=[state])
            st4 = state.ap.rearrange("p (q r) c -> p q r c", r=2)
            for r in range(2):
                K.op(K.act, lambda: nc.scalar.copy(stp4[:, :, r, r * 64:(r + 1) * 64], st4[:, :, r, :]), reads=[state], writes=[st_pad])
            if _DBG_STOP <= 6:
                continue
            K.op(K.act, lambda: nc.scalar.activation(sq.ap, ygate.ap, AF.Square), reads=[ygate], writes=[sq])
            for G in range(4):
                for k_ in range(4):
                    K.op(K.pe, lambda: nc.tensor.matmul(psm.ap[:, G * 128:(G + 1) * 128], C.onesb.ap, sq.ap[:, G * 4 + k_, :],
                                                        start=(k_ == 0), stop=(k_ == 3)), reads=[C.onesb, sq], writes=[psm])
            K.op(K.act, lambda: nc.scalar.activation(grs.ap.rearrange("p g l -> p (g l)"), psm.ap, AF.Ln, bias=C.eps.ap, scale=1.0 / 512),
                 reads=[psm, C.eps], writes=[grs])
            K.op(K.act, lambda: nc.scalar.activation(grs.ap, grs.ap, AF.Exp, scale=-0.5), reads=[grs], writes=[grs])
            yn = yn_ring.next()
            for q in range(16):
                K.op(K.dve, lambda: nc.vector.scalar_tensor_tensor(yn.ap[:, q, :], ygate.ap[:, q, :], C.g_y[:, q:q + 1], grs.ap[:, q // 4, :],
                                                                   op0=ALU.mult, op1=ALU.mult), reads=[ygate, C.veca, grs], writes=[yn])
            K.dma(K.sp, ynv[:, :, cols], yn.ap, reads=[yn], writes=[ynd])
        K.end_phase()
        K.stack = K.es


def phase_out_ple(K, nc, C, name, a_scr, FC, w_o, h_in, p_in, g_ple, w_gate, w_proj, h_out):
    with ExitStack() as es:
        K.stack = es
        K.begin_phase()
        stage = Ring([K.sb(name + "_st%d" % i, [128, 1024], F32) for i in range(3)])
        Wo = load_weight(K, nc, es, name + "_wo", w_o, FC, 1024, stage, [K.pool, K.dve, K.act])
        Wg = load_weight(K, nc, es, name + "_wg", w_gate, 8, 1024, stage, [K.pool, K.dve, K.act])
        Wp = load_weight(K, nc, es, name + "_wp", w_proj, 2, 1024, stage, [K.pool, K.dve, K.act])
        at_ring = Ring([K.sb(name + "_at%d" % i, [128, FC, 512], BF16) for i in range(1)])
        h_ring = Ring([K.sb(name + "_h%d" % i, [128, 4, 1024], F32) for i in range(1)])
        p_ring = Ring([K.sb(name + "_p%d" % i, [128, 4, 256], F32) for i in range(2)])
        pbf = K.sb(name + "_pbf", [128, 4, 256], BF16)
        pT = K.sb(name + "_pT", [128, 2, 512], BF16)
        ss = K.sb(name + "_ss", [128, 4], F32)
        lnv = K.sb(name + "_ln", [128, 4], F32)
        rstd = K.sb(name + "_rstd", [128, 4], F32)
        junk = K.sb(name + "_junk", [128, 1024], BF16)
        xh = K.sb(name + "_xh", [128, 4, 1024], BF16)
        xT = K.sb(name + "_xT", [128, 8, 512], BF16)
        sig_ring = Ring([K.sb(name + "_sig%d" % i, [128, 512], F32) for i in range(2)])
        tmp_ring = Ring([K.sb(name + "_tmp%d" % i, [128, 512], F32) for i in range(2)])
        ho_ring = Ring([K.sb(name + "_ho%d" % i, [128, 4, 1024], F32) for i in range(1)])
        ptr = Ring([K.ps(name + "_pt%d" % i, [128, 1024], BF16) for i in range(2)])
        pmm = Ring([K.ps(name + "_pm%d" % i, [128, 512], F32) for i in range(2)])
        pg_ring = Ring([K.ps(name + "_pg%d" % i, [128, 512], F32) for i in range(2)])
        pp_ring = Ring([K.ps(name + "_pp%d" % i, [128, 512], F32) for i in range(2)])
        ad = K.dram(name + "_a", a_scr)
        hd = K.dram(name + "_hin", h_in)
        od = K.dram(name + "_hout", h_out)
        av = a_scr.rearrange("(c p) t -> p c t", p=128)
        hv = h_in.rearrange("(t a p) d -> t p a d", a=4, p=128)
        pv = p_in.rearrange("(t a p) d -> t p a d", a=4, p=128)
        ov = h_out.rearrange("(t a p) d -> t p a d", a=4, p=128)
        for ts in range(8):
            at = at_ring.next()
            ht = h_ring.next()
            pt_ = p_ring.next()
            K.dma(K.sp, at.ap, av[:, :, ts * 512:(ts + 1) * 512], reads=[ad], writes=[at])
            K.dma(K.sp, ht.ap, hv[ts], reads=[hd], writes=[ht])
            K.dma(K.sp, pt_.ap, pv[ts], writes=[pt_])
            for a in range(4):
                for half in range(2):
                    pm = pmm.next()
                    for c in range(FC):
                        K.op(K.pe, lambda: nc.tensor.matmul(pm.ap, at.ap[:, c, a * 128:(a + 1) * 128], Wo[c].ap[:, half * 512:(half + 1) * 512],
                                                            start=(c == 0), stop=(c == FC - 1)), reads=[at, Wo[c]], writes=[pm])
                    hs = ht.ap[:, a, half * 512:(half + 1) * 512]
                    K.op(K.dve, lambda: nc.vector.tensor_tensor(hs, hs, pm.ap, op=ALU.add), reads=[ht, pm], writes=[ht])
            rms_to_xT(K, nc, C, ht, 4, ss, lnv, rstd, junk, xh, ptr, [(xT, g_ple)])
            K.op(K.pool, lambda: nc.gpsimd.tensor_copy(pbf.ap, pt_.ap), reads=[pt_], writes=[pbf])
            for c2 in range(2):
                pt = ptr.next()
                for a in range(4):
                    K.op(K.pe, lambda: nc.tensor.transpose(pt.ap[:, a * 128:(a + 1) * 128], pbf.ap[:, a, c2 * 128:(c2 + 1) * 128], C.identb.ap),
                         reads=[pbf, C.identb], writes=[pt])
                K.op(K.act, lambda: nc.scalar.copy(pT.ap[:, c2, :], pt.ap[:, 0:512]), reads=[pt], writes=[pT])
            ho = ho_ring.next()
            for a in range(4):
                for half in range(2):
                    pg = pg_ring.next()
                    pp = pp_ring.next()
                    cs_ = slice(half * 512, (half + 1) * 512)
                    for c in range(8):
                        K.op(K.pe, lambda: nc.tensor.matmul(pg.ap, xT.ap[:, c, a * 128:(a + 1) * 128], Wg[c].ap[:, cs_],
                                                            start=(c == 0), stop=(c == 7)), reads=[xT, Wg[c]], writes=[pg])
                    for c in range(2):
                        K.op(K.pe, lambda: nc.tensor.matmul(pp.ap, pT.ap[:, c, a * 128:(a + 1) * 128], Wp[c].ap[:, cs_],
                                                            start=(c == 0), stop=(c == 1)), reads=[pT, Wp[c]], writes=[pp])
                    sg = sig_ring.next()
                    tm = tmp_ring.next()
                    K.op(K.act, lambda: nc.scalar.activation(sg.ap, pg.ap, AF.Sigmoid), reads=[pg], writes=[sg])
                    K.op(K.dve, lambda: nc.vector.tensor_tensor(tm.ap, pp.ap, sg.ap, op=ALU.mult), reads=[pp, sg], writes=[tm])
                    K.op(K.pool, lambda: nc.gpsimd.tensor_tensor(ho.ap[:, a, cs_], tm.ap, ht.ap[:, a, cs_], op=ALU.add),
                         reads=[tm, ht], writes=[ho])
            K.dma(K.sp, ov[ts], ho.ap, reads=[ho], writes=[od])
        K.end_phase()
        K.stack = K.es


def phase_p3(K, nc, C, h1, w_kv, s_in, kt_scr, v_scr, q_scr, sg_scr):
    with ExitStack() as es:
        K.stack = es
        K.begin_phase()
        stage = Ring([K.sb("p3_st%d" % i, [128, 2048], F32) for i in range(2)])
        Wkv = load_weight(K, nc, es, "p3_wkv", w_kv, 8, 2048, stage, [K.pool, K.dve, K.act])
        Wqg = load_weight(K, nc, es, "p3_wqg", s_in, 8, 2048, stage, [K.pool, K.dve, K.act])
        h_ring = Ring([K.sb("p3_h%d" % i, [128, 4, 1024], F32) for i in range(2)])
        ss = K.sb("p3_ss", [128, 4], F32)
        lnv = K.sb("p3_ln", [128, 4], F32)
        rstd = K.sb("p3_rstd", [128, 4], F32)
        junk = K.sb("p3_junk", [128, 1024], BF16)
        xh = K.sb("p3_xh", [128, 4, 1024], BF16)
        xTk = K.sb("p3_xTk", [128, 8, 512], BF16)
        xTq = K.sb("p3_xTq", [128, 8, 512], BF16)
        sq_ring = Ring([K.sb("p3_sq%d" % i, [128, 512], BF16) for i in range(2)])
        rs_ring = Ring([K.sb("p3_rs%d" % i, [128, 512], F32) for i in range(2)])
        kst = K.sb("p3_kst", [128, 8, 512], BF16)
        vst = K.sb("p3_vst", [128, 4, 1024], BF16)
        qst = K.sb("p3_qst", [128, 8, 512], BF16)
        gst = K.sb("p3_gst", [128, 8, 512], BF16)
        ptr = Ring([K.ps("p3_pt%d" % i, [128, 1024], BF16) for i in range(2)])
        pmm = Ring([K.ps("p3_pm%d" % i, [128, 512], F32) for i in range(3)])
        pn_ring = Ring([K.ps("p3_pn%d" % i, [128, 512], F32) for i in range(2)])
        hd = K.dram("p3_h1", h1)
        kd = K.dram("kt_scr", kt_scr)
        vd = K.dram("v_scr", v_scr)
        qd = K.dram("q_scr", q_scr)
        gd = K.dram("sg_scr", sg_scr)
        hv = h1.rearrange("(t a p) d -> t p a d", a=4, p=128)
        vv = v_scr.rearrange("(t a p) d -> t p a d", a=4, p=128)

        def headnorm(pm, gcol, out_ap, out_t):
            sq = sq_ring.next()
            rs = rs_ring.next()
            pn = pn_ring.next()
            K.op(K.act, lambda: nc.scalar.activation(sq.ap, pm.ap, AF.Square), reads=[pm], writes=[sq])
            K.op(K.pe, lambda: nc.tensor.matmul(pn.ap, C.blk.ap, sq.ap, start=True, stop=True), reads=[C.blk, sq], writes=[pn])
            K.op(K.act, lambda: nc.scalar.activation(rs.ap, pn.ap, AF.Ln, bias=C.eps.ap, scale=1.0 / 64), reads=[pn, C.eps], writes=[rs])
            K.op(K.act, lambda: nc.scalar.activation(rs.ap, rs.ap, AF.Exp, scale=-0.5), reads=[rs], writes=[rs])
            K.op(K.dve, lambda: nc.vector.scalar_tensor_tensor(out_ap, pm.ap, gcol, rs.ap, op0=ALU.mult, op1=ALU.mult),
                 reads=[pm, C.kq, rs], writes=[out_t])

        for ts in range(8):
            ht = h_ring.next()
            K.dma(K.sp, ht.ap, hv[ts], reads=[hd], writes=[ht])
            rms_to_xT(K, nc, C, ht, 4, ss, lnv, rstd, junk, xh, ptr, [(xTk, C.g_kv), (xTq, C.g_s)])
            tok = slice(ts * 512, (ts + 1) * 512)
            for fo in range(8):
                pm = pmm.next()
                for c in range(8):
                    K.op(K.pe, lambda: nc.tensor.matmul(pm.ap, Wkv[c].ap[:, fo * 128:(fo + 1) * 128], xTk.ap[:, c, :],
                                                        start=(c == 0), stop=(c == 7)), reads=[Wkv[c], xTk], writes=[pm])
                headnorm(pm, C.kq.ap[:, 0:1], kst.ap[:, fo, :], kst)
            for a in range(4):
                for half in range(2):
                    pm = pmm.next()
                    for c in range(8):
                        K.op(K.pe, lambda: nc.tensor.matmul(pm.ap, xTk.ap[:, c, a * 128:(a + 1) * 128],
                                                            Wkv[c].ap[:, 1024 + half * 512:1024 + (half + 1) * 512],
                                                            start=(c == 0), stop=(c == 7)), reads=[Wkv[c], xTk], writes=[pm])
                    K.op(K.act, lambda: nc.scalar.copy(vst.ap[:, a, half * 512:(half + 1) * 512], pm.ap), reads=[pm], writes=[vst])
            for fo in range(8):
                pm = pmm.next()
                for c in range(8):
                    K.op(K.pe, lambda: nc.tensor.matmul(pm.ap, Wqg[c].ap[:, fo * 128:(fo + 1) * 128], xTq.ap[:, c, :],
                                                        start=(c == 0), stop=(c == 7)), reads=[Wqg[c], xTq], writes=[pm])
                headnorm(pm, C.kq.ap[:, 1:2], qst.ap[:, fo, :], qst)
            for fo in range(8):
                pm = pmm.next()
                for c in range(8):
                    K.op(K.pe, lambda: nc.tensor.matmul(pm.ap, Wqg[c].ap[:, 1024 + fo * 128:1024 + (fo + 1) * 128], xTq.ap[:, c, :],
                                                        start=(c == 0), stop=(c == 7)), reads=[Wqg[c], xTq], writes=[pm])
                K.op(K.act, lambda: nc.scalar.activation(gst.ap[:, fo, :], pm.ap, AF.Silu), reads=[pm], writes=[gst])
            for hf in range(2):
                qs = slice(hf * 4, (hf + 1) * 4)
                K.dma(K.sp, kt_scr.rearrange("(q p) t -> p q t", p=128)[:, qs, tok], kst.ap[:, qs, :], reads=[kst], writes=[kd])
                K.dma(K.sp, q_scr.rearrange("(q p) t -> p q t", p=128)[:, qs, tok], qst.ap[:, qs, :], reads=[qst], writes=[qd])
                K.dma(K.sp, sg_scr.rearrange("(q p) t -> p q t", p=128)[:, qs, tok], gst.ap[:, qs, :], reads=[gst], writes=[gd])
            K.dma(K.sp, vv[ts], vst.ap, reads=[vst], writes=[vd])
        K.end_phase()
        K.stack = K.es


def phase_p4(K, nc, C, kt_scr, v_scr, q_scr, sg_scr, og_scr, nTB=8, nQ=8):
    with ExitStack() as es:
        K.stack = es
        K.begin_phase()
        ntri = K.sb("p4_ntri", [128, 128], BF16)
        ebig = K.sb("p4_ebig", [128, 255], BF16)
        nsel = K.sb("p4_nsel", [128, 32, 128], BF16)
        masks = K.sb("p4_masks", [128, 4, 512], BF16)
        with ExitStack() as est:
            K.stack = est
            negb = K.sb("p4_negb", [128, 128], BF16)
            negb3 = K.sb("p4_negb3", [128, 32, 128], BF16)
            oneb3 = K.sb("p4_oneb3", [128, 4, 512], BF16)
            K.op(K.pool, lambda: nc.gpsimd.memset(negb.ap, -1.0), writes=[negb])
            K.op(K.pool, lambda: nc.gpsimd.affine_select(ntri.ap, negb.ap, pattern=[[-1, 128]], compare_op=ALU.is_ge, fill=0.0,
                                                         base=0, channel_multiplier=1), reads=[negb], writes=[ntri])
            K.op(K.pool, lambda: nc.gpsimd.memset(ebig.ap, 0.0), writes=[ebig])
            K.op(K.pool, lambda: nc.gpsimd.memset(ebig.ap[:, 127:128], 1.0), writes=[ebig])
            K.op(K.pool, lambda: nc.gpsimd.memset(negb3.ap, -1.0), writes=[negb3])
            K.op(K.pool, lambda: nc.gpsimd.affine_select(nsel.ap, negb3.ap, pattern=[[-1, 32], [0, 128]], compare_op=ALU.is_ge, fill=0.0,
                                                         base=-1, channel_multiplier=1), reads=[negb3], writes=[nsel])
            K.op(K.pool, lambda: nc.gpsimd.memset(oneb3.ap, 1.0), writes=[oneb3])
            K.op(K.pool, lambda: nc.gpsimd.affine_select(masks.ap, oneb3.ap, pattern=[[-128, 4], [1, 512]], compare_op=ALU.is_gt, fill=0.0,
                                                         base=0, channel_multiplier=-1), reads=[oneb3], writes=[masks])
            K.barrier()
            K.stack = es
        qpad = [Ring([K.sb("p4_qp%d_%d" % (r, i), [128, 512], BF16) for i in range(2)]) for r in range(2)]
        for r in range(2):
            for t_ in qpad[r].t:
                K.op(K.pool, lambda: nc.gpsimd.memset(t_.ap, 0.0), writes=[t_])
        kt_ring = Ring([K.sb("p4_kt%d" % i, [128, S], BF16) for i in range(2)])
        v_ring = Ring([K.sb("p4_v%d" % i, [128, 32, 128], BF16) for i in range(2)])
        sgt_ring = Ring([K.sb("p4_sg%d" % i, [128, 512], BF16) for i in range(2)])
        ogst_ring = Ring([K.sb("p4_og%d" % i, [128, 512], BF16) for i in range(2)])
        SPs = [K.sb("p4_sp%d" % i, [128, 32, 512], BF16) for i in range(2)]
        SPvs = [[K.view("p4_sp%d_%d" % (b, i), SPs[b].ap[:, i, :]) for i in range(32)] for b in range(2)]
        e_ring = Ring([K.sb("p4_e%d" % i, [128, 512], F32) for i in range(3)])
        spt_ring = Ring([K.sb("p4_spt%d" % i, [128, 512], F32) for i in range(2)])
        csb_ring = Ring([K.sb("p4_csb%d" % i, [128, 512], BF16) for i in range(2)])
        wt_ring = Ring([K.sb("p4_wt%d" % i, [128, 512], BF16) for i in range(4)])
        wm_ring = Ring([K.sb("p4_wm%d" % i, [128, 512], BF16) for i in range(2)])
        pz = Ring([K.ps("p4_pz%d" % i, [128, 512], F32) for i in range(2)])
        pcs = K.ps("p4_pcs", [128, 512], F32)
        pgr = Ring([K.ps("p4_pg%d" % i, [128, 512], F32) for i in range(3)])
        po_ring = Ring([K.ps("p4_po%d" % i, [128, 512], F32) for i in range(2)])
        kd = K.dram("kt_scr", kt_scr)
        vd = K.dram("v_scr", v_scr)
        qd = K.dram("q_scr", q_scr)
        gd = K.dram("sg_scr", sg_scr)
        od = K.dram("og_scr", og_scr)
        vv = v_scr.rearrange("(jb p) (q c) -> q p jb c", p=128, c=128)
        hcount = 0
        pending = None
        for q in range(nQ):
            KT = kt_ring.next()
            V = v_ring.next()
            K.dma(K.sp, KT.ap, kt_scr[q * 128:(q + 1) * 128, :], reads=[kd], writes=[KT])
            K.dma(K.sp, V.ap, vv[q], reads=[vd], writes=[V])
            for TB in range(nTB):
                tok = slice(TB * 512, (TB + 1) * 512)
                nkb = 4 * (TB + 1)
                sgt = sgt_ring.next()
                K.dma(K.sp, sgt.ap, sg_scr[q * 128:(q + 1) * 128, tok], reads=[gd], writes=[sgt])
                ogst = ogst_ring.next()
                qps = []
                for r in range(2):
                    qp = qpad[r].next()
                    K.dma(K.sp, qp.ap[r * 64:(r + 1) * 64, :], q_scr[q * 128 + r * 64:q * 128 + (r + 1) * 64, tok], reads=[qd], writes=[qp])
                    qps.append(qp)
                if pending is not None:
                    pending()
                    pending = None
                for r in range(2):
                    rows = slice(r * 64, (r + 1) * 64)
                    qp = qps[r]
                    SPv = SPvs[hcount % 2]
                    hcount += 1
                    for jb in range(nkb):
                        z = pz.next()
                        K.op(K.pe, lambda: nc.tensor.matmul(z.ap, KT.ap[:, jb * 128:(jb + 1) * 128], qp.ap, start=True, stop=True),
                             reads=[KT, qp], writes=[z])
                        e = e_ring.next()
                        K.op(K.act, lambda: nc.scalar.activation(e.ap, z.ap, AF.Exp), reads=[z], writes=[e])
                        rr = jb - 4 * TB
                        if rr < 0:
                            K.op(K.act, lambda: nc.scalar.activation(SPv[jb].ap, e.ap, AF.Ln, bias=C.one.ap, scale=1.0),
                                 reads=[e, C.one], writes=[SPv[jb]])
                        else:
                            spt = spt_ring.next()
                            K.op(K.act, lambda: nc.scalar.activation(spt.ap, e.ap, AF.Ln, bias=C.one.ap, scale=1.0),
                                 reads=[e, C.one], writes=[spt])
                            K.op(K.dve, lambda: nc.vector.tensor_tensor(SPv[jb].ap, spt.ap, masks.ap[:, rr, :], op=ALU.mult),
                                 reads=[spt, masks], writes=[SPv[jb]])
                        K.op(K.pe, lambda: nc.tensor.matmul(pcs.ap, ebig.ap[:, 127 - jb:255 - jb], SPv[jb].ap,
                                                            start=(jb == 0), stop=(jb == nkb - 1)), reads=[ebig, SPv[jb]], writes=[pcs])
                    csb = csb_ring.next()
                    K.op(K.dve, lambda: nc.vector.tensor_copy(csb.ap, pcs.ap), reads=[pcs], writes=[csb])
                    po = po_ring.next()
                    for jb in range(nkb):
                        gq = pgr.next()
                        K.op(K.pe, lambda: nc.tensor.matmul(gq.ap, ntri.ap, SPv[jb].ap, start=True, stop=False),
                             reads=[ntri, SPv[jb]], writes=[gq])
                        K.op(K.pe, lambda: nc.tensor.matmul(gq.ap, KT.ap[:, jb * 128:(jb + 1) * 128], qp.ap, start=False, stop=False),
                             reads=[KT, qp], writes=[gq])
                        K.op(K.pe, lambda: nc.tensor.matmul(gq.ap, nsel.ap[:, jb, :], csb.ap, start=False, stop=True),
                             reads=[nsel, csb], writes=[gq])
                        wt = wt_ring.next()
                        K.op(K.act, lambda: nc.scalar.activation(wt.ap, gq.ap, AF.Exp), reads=[gq], writes=[wt])
                        rr = jb - 4 * TB
                        if rr >= 0:
                            wm = wm_ring.next()
                            K.op(K.dve, lambda: nc.vector.tensor_tensor(wm.ap, wt.ap, masks.ap[:, rr, :], op=ALU.mult),
                                 reads=[wt, masks], writes=[wm])
                            wt = wm
                        K.op(K.pe, lambda: nc.tensor.matmul(po.ap, V.ap[:, jb, :], wt.ap, start=(jb == 0), stop=(jb == nkb - 1)),
                             reads=[V, wt], writes=[po])
                    K.op(K.dve, lambda: nc.vector.tensor_tensor(ogst.ap[rows, :], po.ap[rows, :], sgt.ap[rows, :], op=ALU.mult),
                         reads=[po, sgt], writes=[ogst])
                pending = (lambda q_=q, tok_=tok, og_=ogst: K.dma(K.sp, og_scr[q_ * 128:(q_ + 1) * 128, tok_], og_.ap, reads=[og_], writes=[od]))
        if pending is not None:
            pending()
        K.end_phase()
        K.stack = K.es


PARAM_SHAPES = {
    "m_norm": [1, 1024], "m_in": [1, 1024, 5152], "m_conv_w": [1, 4, 3072], "m_conv_b": [1, 3072],
    "m_dt_bias": [1, 32], "m_A_log": [1, 32], "m_D": [1, 32], "m_ynorm": [1, 2048], "m_out": [1, 2048, 1024],
    "kv_norm": [1024], "w_kv": [1024, 2048], "k_norm": [64], "s_norm": [1, 1024], "s_in": [1, 1024, 2048],
    "q_norm": [1, 64], "s_out": [1, 1024, 1024], "ple_norm": [2, 1024], "ple_gate": [2, 1024, 1024],
    "ple_proj": [2, 256, 1024],
}

SCRATCH = {
    "sz_scr": ([2048, S], BF16), "xbc_scr": ([3072, S], BF16), "dt_scr": ([S, 32], F32),
    "yn_scr": ([2048, S], BF16), "h1_scr": ([S, D], F32), "q_scr": ([1024, S], BF16),
    "sg_scr": ([1024, S], BF16), "og_scr": ([1024, S], BF16),
    "kt_scr": ([1024, S], BF16), "v_scr": ([S, 1024], BF16),
}


def build(phases=("p1a", "p1b", "p2", "p3", "p4", "p5"), ext_in=(), ext_out=(), p1b_chunks=32, p4_tb=8, p4_q=8):
    nc = bass.Bass("TRN2", target_bir_lowering=False)
    x = nc.dram_tensor("x", [S, D], F32, kind="ExternalInput").ap()
    p0 = nc.dram_tensor("p0", [S, 256], F32, kind="ExternalInput").ap()
    p1 = nc.dram_tensor("p1", [S, 256], F32, kind="ExternalInput").ap()
    prm = {k: nc.dram_tensor(k, shp, F32, kind="ExternalInput").ap() for k, shp in PARAM_SHAPES.items()}
    out = nc.dram_tensor("out", [S, D], F32, kind="ExternalOutput").ap()
    scr = {}
    for k, (shp, dt_) in SCRATCH.items():
        kind = "ExternalInput" if k in ext_in else ("ExternalOutput" if k in ext_out else "Internal")
        scr[k] = nc.dram_tensor(k, shp, dt_, kind=kind).ap()
    with ExitStack() as es:
        K = Ctx(nc, es)
        C = make_consts(K, nc, prm)
        if "p1a" in phases:
            phase_p1a(K, nc, C, x, prm["m_in"][0], scr["sz_scr"], scr["xbc_scr"], scr["dt_scr"])
        if "p1b" in phases:
            phase_p1b(K, nc, C, scr["sz_scr"], scr["xbc_scr"], scr["dt_scr"], scr["yn_scr"], nchunks=p1b_chunks)
        if "p2" in phases:
            phase_out_ple(K, nc, C, "p2", scr["yn_scr"], 16, prm["m_out"][0], x, p0, C.g_p0,
                          prm["ple_gate"][0], prm["ple_proj"][0], scr["h1_scr"])
        if "p3" in phases:
            phase_p3(K, nc, C, scr["h1_scr"], prm["w_kv"], prm["s_in"][0], scr["kt_scr"], scr["v_scr"], scr["q_scr"], scr["sg_scr"])
        if "p4" in phases:
            phase_p4(K, nc, C, scr["kt_scr"], scr["v_scr"], scr["q_scr"], scr["sg_scr"], scr["og_scr"], nTB=p4_tb, nQ=p4_q)
        if "p5" in phases:
            phase_out_ple(K, nc, C, "p5", scr["og_scr"], 8, prm["s_out"][0], scr["h1_scr"], p1, C.g_p1,
                          prm["ple_gate"][1], prm["ple_proj"][1], out)
        K.finish()
    return nc


_NC_CACHE = {}


def kernel(**inputs):
    if "full" not in _NC_CACHE:
        _NC_CACHE["full"] = build()
    nc = _NC_CACHE["full"]
    x = np.asarray(inputs["x"], dtype=np.float32)
    p = np.asarray(inputs["p"], dtype=np.float32)
    in_maps = []
    for b in range(NCORES):
        m = {"x": np.ascontiguousarray(x[b]), "p0": np.ascontiguousarray(p[0, b]), "p1": np.ascontiguousarray(p[1, b])}
        for k in PARAM_SHAPES:
            m[k] = np.ascontiguousarray(np.asarray(inputs[k], dtype=np.float32))
        in_maps.append(m)
    res = run_bass_kernel_spmd(nc, in_maps, core_ids=list(range(NCORES)))
    return np.stack([np.asarray(res.results[b]["out"], dtype=np.float32) for b in range(NCORES)], axis=0)
```

```python
from bisect import bisect_left
from contextlib import ExitStack
import os
import numpy as np
import concourse.bass as bass
import concourse.mybir as mybir
from concourse.alu_op_type import AluOpType as ALU
from concourse.bass_utils import run_bass_kernel_spmd

F32 = mybir.dt.float32
BF16 = mybir.dt.bfloat16
AF = mybir.ActivationFunctionType

S = 4096
D = 1024
_DBG_STOP = int(os.environ.get('P1B_STOP', '99'))
NCORES = 8
EPS = 1e-6


class Eng:
    def __init__(self, name, eng, sem, eager):
        self.name, self.eng, self.sem, self.eager = name, eng, sem, eager
        self.n = 0
        self.count = 0
        self.sig_idx = []
        self.sig_cnt = []
        self.last = None
        self.last_signaled = True
        self.known = {}

    def signal_last(self):
        if not self.last_signaled:
            self.last.then_inc(self.sem, 1)
            self.count += 1
            self.sig_idx.append(self.n)
            self.sig_cnt.append(self.count)
            self.last_signaled = True

    def count_for(self, idx):
        i = bisect_left(self.sig_idx, idx)
        if i == len(self.sig_idx):
            self.signal_last()
            i = len(self.sig_idx) - 1
        assert self.sig_idx[i] >= idx
        return self.sig_cnt[i]


class DSem:
    def __init__(self, sem):
        self.sem = sem
        self.issued = 0


class T:
    def __init__(self, name, ap, space):
        self.name, self.ap, self.space = name, ap, space
        self.w = None
        self.r = {}
        self.dsem = None

    def __getitem__(self, k):
        return self.ap[k]


class Ring:
    def __init__(self, tiles):
        self.t = tiles
        self.i = 0

    def next(self):
        t = self.t[self.i % len(self.t)]
        self.i += 1
        return t


class Ctx:
    def __init__(self, nc, es):
        self.nc, self.es = nc, es
        mk = lambda n: es.enter_context(nc.semaphore(n))
        self.pe = Eng("pe", nc.tensor, mk("s_pe"), False)
        self.act = Eng("act", nc.scalar, mk("s_act"), True)
        self.dve = Eng("dve", nc.vector, mk("s_dve"), True)
        self.pool = Eng("pool", nc.gpsimd, mk("s_pool"), True)
        self.sp = Eng("sp", nc.sync, mk("s_sp"), True)
        self.engs = [self.pe, self.act, self.dve, self.pool, self.sp]
        self.dsems = []
        self.free_dsems = []
        self.nsem = 0
        self.stack = es

    def sb(self, name, shape, dtype, es=None):
        t = (es or self.stack).enter_context(self.nc.sbuf_tensor(name, shape, dtype))
        return T(name, t.ap(), "sb")

    def ps(self, name, shape, dtype, es=None):
        t = (es or self.stack).enter_context(self.nc.psum_tensor(name, shape, dtype))
        return T(name, t.ap(), "ps")

    def dram(self, name, ap):
        return T(name, ap, "dram")

    def view(self, name, ap, space="sb"):
        return T(name, ap, space)

    def begin_phase(self):
        self.phase_dsems = []

    def end_phase(self):
        self.barrier()
        self.free_dsems.extend(self.phase_dsems)
        self.phase_dsems = None

    def new_dsem(self):
        if self.free_dsems:
            d = self.free_dsems.pop()
        else:
            self.nsem += 1
            d = DSem(self.es.enter_context(self.nc.semaphore("s_d%d" % self.nsem)))
            self.dsems.append(d)
        if getattr(self, "phase_dsems", None) is not None:
            self.phase_dsems.append(d)
        return d

    def _waits(self, E, reads, writes):
        need = {}

        def add(ev, same_ok):
            if ev is None:
                return
            if ev[0] == "e":
                Dn, idx = ev[1], ev[2]
                if Dn is E and same_ok:
                    return
                c = Dn.count_for(idx)
                sem = Dn.sem
            else:
                c = ev[1].issued * 16
                sem = ev[1].sem
            key = id(sem)
            if need.get(key, (None, 0))[1] < c:
                need[key] = (sem, c)

        for t in reads:
            add(t.w, E is self.pe)
        for t in writes:
            add(t.w, True)
            for ev in t.r.values():
                add(ev, True)
        for key, (sem, c) in need.items():
            if E.known.get(key, 0) >= c:
                continue
            E.eng.wait_ge(sem, c)
            E.known[key] = c

    def op(self, E, make, reads=(), writes=()):
        writes = list(writes) + [t for t in reads if t.space == "ps"]
        reads = [t for t in reads if t.space != "ps"]
        self._waits(E, reads, writes)
        inst = make()
        E.n += 1
        E.last = inst
        E.last_signaled = False
        if E.eager:
            E.signal_last()
        ev = ("e", E, E.n)
        for t in reads:
            t.r[id(E)] = ev
        for t in writes:
            t.w = ev
            t.r = {}
        return inst

    def dma(self, Q, out_ap, in_ap, reads=(), writes=(), dsem=None, **kw):
        self._waits(Q, reads, writes)
        if dsem is None:
            cand = ([t for t in writes if t.space != "dram"] or [t for t in reads if t.space != "dram"]
                    or list(writes) or list(reads))
            t0 = cand[0]
            if t0.dsem is None:
                t0.dsem = self.new_dsem()
            dsem = t0.dsem
        inst = Q.eng.dma_start(out=out_ap, in_=in_ap, **kw)
        inst.then_inc(dsem.sem, 16)
        dsem.issued += 1
        ev = ("d", dsem)
        for t in reads:
            t.r[id(dsem)] = ev
        for t in writes:
            t.w = ev
            t.r = {}
        return inst

    def barrier(self):
        for E in self.engs:
            E.signal_last()
        for E in self.engs:
            for Dn in self.engs:
                if Dn is E or Dn.count == 0:
                    continue
                key = id(Dn.sem)
                if E.known.get(key, 0) < Dn.count:
                    E.eng.wait_ge(Dn.sem, Dn.count)
                    E.known[key] = Dn.count
            for d in self.dsems:
                if d.issued:
                    key = id(d.sem)
                    if E.known.get(key, 0) < d.issued * 16:
                        E.eng.wait_ge(d.sem, d.issued * 16)
                        E.known[key] = d.issued * 16

    def finish(self):
        for d in self.dsems:
            if d.issued:
                self.sp.eng.wait_ge(d.sem, d.issued * 16)


class Consts:
    pass


def make_consts(K, nc, prm):
    C = Consts()
    onesf = K.sb("c_onesf", [128, 128], F32)
    K.op(K.pool, lambda: nc.gpsimd.memset(onesf.ap, 1.0), writes=[onesf])
    C.onesf = onesf
    identf = K.sb("c_identf", [128, 128], F32)
    K.op(K.pool, lambda: nc.gpsimd.affine_select(identf.ap, onesf.ap, pattern=[[1, 128]], compare_op=ALU.is_equal,
                                                 fill=0.0, base=0, channel_multiplier=-1), reads=[onesf], writes=[identf])
    C.identf = identf
    identb = K.sb("c_identb", [128, 128], BF16)
    K.op(K.pool, lambda: nc.gpsimd.tensor_copy(identb.ap, identf.ap), reads=[identf], writes=[identb])
    C.identb = identb
    onesb = K.sb("c_onesb", [128, 128], BF16)
    K.op(K.pool, lambda: nc.gpsimd.memset(onesb.ap, 1.0), writes=[onesb])
    C.onesb = onesb
    tri = K.sb("c_tri", [128, 128], F32)
    K.op(K.pool, lambda: nc.gpsimd.affine_select(tri.ap, onesf.ap, pattern=[[1, 128]], compare_op=ALU.is_ge,
                                                 fill=0.0, base=0, channel_multiplier=-1), reads=[onesf], writes=[tri])
    C.tri = tri
    epsc = K.sb("c_eps", [128, 1], F32)
    K.op(K.pool, lambda: nc.gpsimd.memset(epsc.ap, EPS), writes=[epsc])
    C.eps = epsc
    onec = K.sb("c_one", [128, 1], F32)
    K.op(K.pool, lambda: nc.gpsimd.memset(onec.ap, 1.0), writes=[onec])
    C.one = onec
    blk = K.sb("c_blk", [128, 128], BF16)
    K.op(K.pool, lambda: nc.gpsimd.memset(blk.ap, 0.0), writes=[blk])
    K.op(K.pool, lambda: nc.gpsimd.memset(blk.ap[0:64, 0:64], 1.0), writes=[blk])
    K.op(K.pool, lambda: nc.gpsimd.memset(blk.ap[64:128, 64:128], 1.0), writes=[blk])
    C.blk = blk

    def vec_cols(name, items):
        R = sum(n for _, n in items)
        st = K.sb("vst_" + name, [R, 128], F32)
        r0 = 0
        for ap1, n in items:
            K.dma(K.sp, st.ap[r0:r0 + n, :], ap1.rearrange("(c p) -> c p", p=128), writes=[st])
            r0 += n
        pt = K.ps("vps_" + name, [128, 512], F32)
        K.op(K.pe, lambda: nc.tensor.transpose(pt.ap[:, 0:R], st.ap, identf.ap[0:R, 0:R]), reads=[st, identf], writes=[pt])
        out = K.sb("vec_" + name, [128, R], F32)
        K.op(K.dve, lambda: nc.vector.tensor_copy(out.ap, pt.ap[:, 0:R]), reads=[pt], writes=[out])
        return out

    veca = K.sb("c_veca", [128, 80], F32)
    vecb = K.sb("c_vecb", [128, 96], F32)
    with ExitStack() as es2:
        K.stack = es2
        va = vec_cols("a", [(prm["m_norm"][0], 8), (prm["kv_norm"], 8), (prm["s_norm"][0], 8),
                            (prm["ple_norm"][0], 8), (prm["ple_norm"][1], 8), (prm["m_ynorm"][0], 16),
                            (prm["m_conv_b"][0], 24)])
        vb = vec_cols("b", [(prm["m_conv_w"][0, j], 24) for j in range(4)])
        K.op(K.dve, lambda: nc.vector.tensor_copy(veca.ap, va.ap), reads=[va], writes=[veca])
        K.op(K.dve, lambda: nc.vector.tensor_copy(vecb.ap, vb.ap), reads=[vb], writes=[vecb])
        K.barrier()
        K.stack = K.es
    C.g_m, C.g_kv, C.g_s = veca.ap[:, 0:8], veca.ap[:, 8:16], veca.ap[:, 16:24]
    C.g_p0, C.g_p1 = veca.ap[:, 24:32], veca.ap[:, 32:40]
    C.g_y, C.conv_b = veca.ap[:, 40:56], veca.ap[:, 56:80]
    C.conv_w = vecb.ap
    C.veca, C.vecb = veca, vecb

    kq = K.sb("c_kq", [128, 2], F32)
    for half in range(2):
        K.dma(K.sp, kq.ap[half * 64:(half + 1) * 64, 0:1], prm["k_norm"].rearrange("(p o) -> p o", o=1), writes=[kq])
        K.dma(K.sp, kq.ap[half * 64:(half + 1) * 64, 1:2], prm["q_norm"][0].rearrange("(p o) -> p o", o=1), writes=[kq])
    kq2 = K.sb("c_kq2", [128, 2], F32)
    K.op(K.dve, lambda: nc.vector.tensor_copy(kq2.ap[:, 0:1], kq.ap[:, 0:1]), reads=[kq], writes=[kq2])
    K.op(K.dve, lambda: nc.vector.tensor_scalar(kq2.ap[:, 1:2], kq.ap[:, 1:2], 0.125, None, op0=ALU.mult), reads=[kq], writes=[kq2])
    C.kq = kq2
    bc = K.sb("c_bc", [128, 3, 32], F32)
    K.dma(K.sp, bc.ap[:, 0, :], prm["m_dt_bias"][0:1, :].to_broadcast([128, 32]), writes=[bc])
    K.dma(K.sp, bc.ap[:, 1, :], prm["m_A_log"][0:1, :].to_broadcast([128, 32]), writes=[bc])
    K.dma(K.sp, bc.ap[:, 2, :], prm["m_D"][0:1, :].to_broadcast([128, 32]), writes=[bc])
    C.bc = bc
    Abc = K.sb("c_A", [128, 32], F32)
    K.op(K.act, lambda: nc.scalar.activation(Abc.ap, bc.ap[:, 1, :], AF.Exp), reads=[bc], writes=[Abc])
    K.op(K.dve, lambda: nc.vector.tensor_scalar(Abc.ap, Abc.ap, -1.0, None, op0=ALU.mult), reads=[Abc], writes=[Abc])
    C.A = Abc
    Dcol = K.sb("c_Dcol", [128, 16], F32)
    dv = bc.ap[:, 2, :].rearrange("p (q r) -> p q r", r=2)
    K.op(K.dve, lambda: nc.vector.tensor_copy(Dcol.ap[0:64, :], dv[0:64, :, 0]), reads=[bc], writes=[Dcol])
    K.op(K.dve, lambda: nc.vector.tensor_copy(Dcol.ap[64:128, :], dv[64:128, :, 1]), reads=[bc], writes=[Dcol])
    C.Dcol = Dcol
    return C


def load_weight(K, nc, es, name, src, C_, N, stage_ring, cast_engs):
    w = K.sb(name, [128, C_, N], BF16, es)
    views = [K.view("%s_%d" % (name, c), w.ap[:, c, :]) for c in range(C_)]
    SW = stage_ring.t[0].ap.shape[1]
    i = 0
    for c in range(C_):
        for n0 in range(0, N, SW):
            n1 = min(N, n0 + SW)
            st = stage_ring.next()
            K.dma(K.sp, st.ap[:, 0:n1 - n0], src[c * 128:(c + 1) * 128, n0:n1], writes=[st])
            E = cast_engs[i % len(cast_engs)]
            i += 1
            if E is K.act:
                K.op(E, lambda: nc.scalar.copy(views[c].ap[:, n0:n1], st.ap[:, 0:n1 - n0]), reads=[st], writes=[views[c]])
            else:
                K.op(E, lambda: E.eng.tensor_copy(views[c].ap[:, n0:n1], st.ap[:, 0:n1 - n0]), reads=[st], writes=[views[c]])
    return views


def rms_to_xT(K, nc, C, ht, na, ss, lnv, rstd, junk, xh, ptr_ring, outs):
    for a in range(na):
        K.op(K.act, lambda: nc.scalar.activation(junk.ap, ht.ap[:, a, :], AF.Square, accum_out=ss.ap[:, a:a + 1]),
             reads=[ht], writes=[junk, ss])
    K.op(K.act, lambda: nc.scalar.activation(lnv.ap[:, 0:na], ss.ap[:, 0:na], AF.Ln, bias=C.eps.ap, scale=1.0 / D),
         reads=[ss, C.eps], writes=[lnv])
    K.op(K.act, lambda: nc.scalar.activation(rstd.ap[:, 0:na], lnv.ap[:, 0:na], AF.Exp, scale=-0.5), reads=[lnv], writes=[rstd])
    for a in range(na):
        K.op(K.dve, lambda: nc.vector.tensor_scalar(xh.ap[:, a, :], ht.ap[:, a, :], rstd.ap[:, a:a + 1], None, op0=ALU.mult),
             reads=[ht, rstd], writes=[xh])
    k = 0
    for c in range(8):
        pt = ptr_ring.next()
        for a in range(na):
            K.op(K.pe, lambda: nc.tensor.transpose(pt.ap[:, a * 128:(a + 1) * 128], xh.ap[:, a, c * 128:(c + 1) * 128], C.identb.ap),
                 reads=[xh, C.identb], writes=[pt])
        for (xT, g) in outs:
            if k % 2 == 0:
                K.op(K.act, lambda: nc.scalar.activation(xT.ap[:, c, :], pt.ap[:, 0:na * 128], AF.Identity, scale=g[:, c:c + 1]),
                     reads=[pt, C.veca], writes=[xT])
            else:
                K.op(K.dve, lambda: nc.vector.tensor_scalar(xT.ap[:, c, :], pt.ap[:, 0:na * 128], g[:, c:c + 1], None, op0=ALU.mult),
                     reads=[pt, C.veca], writes=[xT])
            k += 1


def phase_p1a(K, nc, C, x_d, w_in, sz_scr, xbc_scr, dt_scr):
    with ExitStack() as es:
        K.stack = es
        K.begin_phase()
        stage = Ring([K.sb("p1a_st%d" % i, [128, 1288], F32) for i in range(2)])
        W = load_weight(K, nc, es, "p1a_w", w_in, 8, 5152, stage, [K.pool, K.dve, K.act])
        xt_ring = Ring([K.sb("p1a_x%d" % i, [128, 4, 1024], F32) for i in range(1)])
        ss = K.sb("p1a_ss", [128, 4], F32)
        lnv = K.sb("p1a_ln", [128, 4], F32)
        rstd = K.sb("p1a_rstd", [128, 4], F32)
        junk = K.sb("p1a_junk", [128, 1024], BF16)
        xh = K.sb("p1a_xh", [128, 4, 1024], BF16)
        xT = K.sb("p1a_xT", [128, 8, 512], BF16)
        ptr = Ring([K.ps("p1a_pt%d" % i, [128, 1024], BF16) for i in range(2)])
        pmm = Ring([K.ps("p1a_pm%d" % i, [128, 512], F32) for i in range(4)])
        pdt = K.ps("p1a_pdt", [128, 512], F32)
        ost_ring = Ring([K.sb("p1a_ost%d" % i, [128, 512], BF16) for i in range(6)])
        raw_ring = Ring([K.sb("p1a_raw%d" % i, [128, 515], F32) for i in range(3)])
        acc_ring = Ring([K.sb("p1a_acc%d" % i, [128, 512], F32) for i in range(2)])
        halo = K.sb("p1a_halo", [128, 24, 3], F32)
        halos = [K.view("p1a_halo%d" % i, halo.ap[:, i, :]) for i in range(24)]
        K.op(K.pool, lambda: nc.gpsimd.memset(halo.ap, 0.0), writes=halos)
        dtx = K.sb("p1a_dtx", [128, 4, 32], F32)
        dta = K.sb("p1a_dta", [128, 4, 32], F32)
        dte = K.sb("p1a_dte", [128, 4, 32], F32)
        dtm = K.sb("p1a_dtm", [128, 4, 32], F32)
        dto = K.sb("p1a_dto", [128, 4, 32], F32)
        szd = K.dram("sz_scr", sz_scr)
        xbd = K.dram("xbc_scr", xbc_scr)
        dtd = K.dram("dt_scr", dt_scr)
        xv = x_d.rearrange("(t a p) d -> t p a d", a=4, p=128)
        for ts in range(8):
            xt = xt_ring.next()
            K.dma(K.sp, xt.ap, xv[ts], writes=[xt])
            rms_to_xT(K, nc, C, xt, 4, ss, lnv, rstd, junk, xh, ptr, [(xT, C.g_m)])
            tok = slice(ts * 512, (ts + 1) * 512)
            for ft in range(40):
                pm = pmm.next()
                for c in range(8):
                    K.op(K.pe, lambda: nc.tensor.matmul(pm.ap, W[c].ap[:, ft * 128:(ft + 1) * 128], xT.ap[:, c, :],
                                                        start=(c == 0), stop=(c == 7)), reads=[W[c], xT], writes=[pm])
                ost = ost_ring.next()
                if ft < 16:
                    K.op(K.act, lambda: nc.scalar.activation(ost.ap, pm.ap, AF.Silu), reads=[pm], writes=[ost])
                    K.dma(K.sp, sz_scr[ft * 128:(ft + 1) * 128, tok], ost.ap, reads=[ost], writes=[szd])
                else:
                    ci = ft - 16
                    raw = raw_ring.next()
                    acc = acc_ring.next()
                    K.op(K.pool, lambda: nc.gpsimd.tensor_copy(raw.ap[:, 0:3], halos[ci].ap), reads=[halos[ci]], writes=[raw])
                    K.op(K.act, lambda: nc.scalar.copy(raw.ap[:, 3:515], pm.ap), reads=[pm], writes=[raw])
                    K.op(K.pool, lambda: nc.gpsimd.tensor_copy(halos[ci].ap, raw.ap[:, 512:515]), reads=[raw], writes=[halos[ci]])
                    cw = lambda j: C.conv_w[:, j * 24 + ci:j * 24 + ci + 1]
                    K.op(K.dve, lambda: nc.vector.tensor_scalar(acc.ap, raw.ap[:, 0:512], cw(0), None, op0=ALU.mult),
                         reads=[raw, C.vecb], writes=[acc])
                    for j in range(1, 4):
                        K.op(K.dve, lambda: nc.vector.scalar_tensor_tensor(acc.ap, raw.ap[:, j:j + 512], cw(j), acc.ap,
                                                                           op0=ALU.mult, op1=ALU.add),
                             reads=[raw, acc, C.vecb], writes=[acc])
                    K.op(K.act, lambda: nc.scalar.activation(ost.ap, acc.ap, AF.Silu, bias=C.conv_b[:, ci:ci + 1]),
                         reads=[acc, C.veca], writes=[ost])
                    K.dma(K.sp, xbc_scr[ci * 128:(ci + 1) * 128, tok], ost.ap, reads=[ost], writes=[xbd])
            for a in range(4):
                for c in range(8):
                    K.op(K.pe, lambda: nc.tensor.matmul(pdt.ap[:, a * 32:(a + 1) * 32], xT.ap[:, c, a * 128:(a + 1) * 128],
                                                        W[c].ap[:, 5120:5152], start=(c == 0), stop=(c == 7)),
                         reads=[W[c], xT], writes=[pdt])
            pv = pdt.ap[:, 0:128].rearrange("p (a h) -> p a h", a=4)
            K.op(K.dve, lambda: nc.vector.tensor_tensor(dtx.ap, pv, C.bc.ap[:, 0:1, :].to_broadcast([128, 4, 32]), op=ALU.add),
                 reads=[pdt, C.bc], writes=[dtx])
            K.op(K.act, lambda: nc.scalar.activation(dta.ap, dtx.ap, AF.Abs), reads=[dtx], writes=[dta])
            K.op(K.act, lambda: nc.scalar.activation(dte.ap, dta.ap, AF.Exp, scale=-1.0), reads=[dta], writes=[dte])
            K.op(K.act, lambda: nc.scalar.activation(dte.ap, dte.ap, AF.Ln, bias=C.one.ap, scale=1.0), reads=[dte, C.one], writes=[dte])
            K.op(K.dve, lambda: nc.vector.tensor_scalar(dtm.ap, dtx.ap, 0.0, None, op0=ALU.max), reads=[dtx], writes=[dtm])
            K.op(K.dve, lambda: nc.vector.tensor_tensor(dto.ap, dtm.ap, dte.ap, op=ALU.add), reads=[dtm, dte], writes=[dto])
            K.dma(K.sp, dt_scr.rearrange("(t a p) h -> t p a h", a=4, p=128)[ts], dto.ap, reads=[dto], writes=[dtd])
        K.end_phase()
        K.stack = K.es


def phase_p1b(K, nc, C, sz_scr, xbc_scr, dt_scr, yn_scr, nchunks=32):
    with ExitStack() as es:
        K.stack = es
        K.begin_phase()
        xb_ring = Ring([K.sb("p1b_xb%d" % i, [128, 24, 128], BF16) for i in range(2)])
        sz_ring = Ring([K.sb("p1b_sz%d" % i, [128, 16, 128], BF16) for i in range(2)])
        dt_ring = Ring([K.sb("p1b_dt%d" % i, [128, 32], F32) for i in range(2)])
        a_ch = K.sb("p1b_a", [128, 32], F32)
        acum = K.sb("p1b_acum", [128, 32], F32)
        tmpw = K.sb("p1b_tmpw", [128, 32], F32)
        wend = K.sb("p1b_wend", [128, 32], F32)
        dtw = K.sb("p1b_dtw", [128, 32], F32)
        dA = K.sb("p1b_dA", [128, 32], F32)
        xdt_pad = K.sb("p1b_xdtp", [128, 32, 128], BF16)
        xdtw = K.sb("p1b_xdtw", [128, 32, 64], BF16)
        btok = K.sb("p1b_btok", [128, 4, 128], BF16)
        state = K.sb("p1b_state", [128, 32, 64], F32)
        st_pad = K.sb("p1b_stp", [128, 32, 128], BF16)
        cbm = K.sb("p1b_cbm", [128, 4, 128], F32)
        seg_ring = Ring([K.sb("p1b_seg%d" % i, [128, 4, 128], F32) for i in range(2)])
        dec_ring = Ring([K.sb("p1b_dec%d" % i, [128, 4, 128], F32) for i in range(2)])
        ea_ring = Ring([K.sb("p1b_ea%d" % i, [128, 4, 128], F32) for i in range(2)])
        mt_ring = Ring([K.sb("p1b_mt%d" % i, [128, 4, 128], BF16) for i in range(3)])
        cs_ring = Ring([K.sb("p1b_cs%d" % i, [128, 4, 128], BF16) for i in range(3)])
        ytmp_ring = Ring([K.sb("p1b_yt%d" % i, [128, 128], F32) for i in range(2)])
        ygate = K.sb("p1b_yg", [128, 16, 128], F32)
        sq = K.sb("p1b_sq", [128, 16, 128], BF16)
        grs = K.sb("p1b_grs", [128, 4, 128], F32)
        yn_ring = Ring([K.sb("p1b_yn%d" % i, [128, 16, 128], BF16) for i in range(2)])
        pxs = K.ps("p1b_pxs", [128, 2048], BF16)
        pb = K.ps("p1b_pb", [128, 1024], BF16)
        psm = K.ps("p1b_psm", [128, 512], F32)
        pabc = Ring([K.ps("p1b_pabc%d" % i, [128, 512], F32) for i in range(2)])
        py = K.ps("p1b_py", [128, 512], F32)
        pyv = [K.view("p1b_py%d" % i, py.ap[:, i * 128:(i + 1) * 128], "ps") for i in range(4)]
        pupd = K.ps("p1b_pupd", [128, 512], F32)
        for t_ in (xdt_pad, st_pad, state):
            K.op(K.pool, lambda: nc.gpsimd.memset(t_.ap, 0.0), writes=[t_])
        szd = K.dram("sz_scr", sz_scr)
        xbd = K.dram("xbc_scr", xbc_scr)
        dtd = K.dram("dt_scr", dt_scr)
        ynd = K.dram("yn_scr", yn_scr)
        xbv = xbc_scr.rearrange("(q p) t -> p q t", p=128)
        szv = sz_scr.rearrange("(q p) t -> p q t", p=128)
        ynv = yn_scr.rearrange("(q p) t -> p q t", p=128)
        xdt4 = xdt_pad.ap.rearrange("p (q r) c -> p q r c", r=2)
        stp4 = st_pad.ap.rearrange("p (q r) c -> p q r c", r=2)
        for ch in range(nchunks):
            cols = slice(ch * 128, (ch + 1) * 128)
            xb = xb_ring.next()
            sz = sz_ring.next()
            dt = dt_ring.next()
            K.dma(K.sp, xb.ap, xbv[:, :, cols], reads=[xbd], writes=[xb])
            K.dma(K.sp, sz.ap, szv[:, :, cols], reads=[szd], writes=[sz])
            K.dma(K.sp, dt.ap, dt_scr[ch * 128:(ch + 1) * 128, :], reads=[dtd], writes=[dt])
            if _DBG_STOP <= 1:
                continue
            K.op(K.dve, lambda: nc.vector.tensor_tensor(a_ch.ap, dt.ap, C.A.ap, op=ALU.mult), reads=[dt, C.A], writes=[a_ch])
            K.op(K.pe, lambda: nc.tensor.matmul(psm.ap[:, 0:32], C.tri.ap, a_ch.ap, start=True, stop=True),
                 reads=[C.tri, a_ch], writes=[psm])
            K.op(K.pe, lambda: nc.tensor.matmul(psm.ap[:, 32:64], C.onesf.ap, a_ch.ap, start=True, stop=True),
                 reads=[C.onesf, a_ch], writes=[psm])
            K.op(K.act, lambda: nc.scalar.copy(acum.ap, psm.ap[:, 0:32]), reads=[psm], writes=[acum])
            K.op(K.dve, lambda: nc.vector.tensor_tensor(tmpw.ap, psm.ap[:, 32:64], acum.ap, op=ALU.subtract),
                 reads=[psm, acum], writes=[tmpw])
            K.op(K.act, lambda: nc.scalar.activation(wend.ap, tmpw.ap, AF.Exp), reads=[tmpw], writes=[wend])
            K.op(K.act, lambda: nc.scalar.activation(dA.ap, psm.ap[:, 32:64], AF.Exp), reads=[psm], writes=[dA])
            K.op(K.dve, lambda: nc.vector.tensor_tensor(dtw.ap, dt.ap, wend.ap, op=ALU.mult), reads=[dt, wend], writes=[dtw])
            if _DBG_STOP <= 2:
                continue
            for ci in range(16):
                K.op(K.pe, lambda: nc.tensor.transpose(pxs.ap[:, ci * 128:(ci + 1) * 128], xb.ap[:, ci, :], C.identb.ap),
                     reads=[xb, C.identb], writes=[pxs])
            for g in range(4):
                K.op(K.pe, lambda: nc.tensor.transpose(pb.ap[:, g * 128:(g + 1) * 128], xb.ap[:, 16 + g, :], C.identb.ap),
                     reads=[xb, C.identb], writes=[pb])
            pxs4 = pxs.ap.rearrange("p (q r c) -> p q r c", r=2, c=64)
            dt3 = dt.ap.rearrange("p (q r) -> p q r", r=2)
            for r in range(2):
                K.op(K.dve, lambda: nc.vector.tensor_tensor(xdt4[:, :, r, r * 64:(r + 1) * 64], pxs4[:, :, r, :],
                                                            dt3[:, :, r:r + 1].to_broadcast([128, 16, 64]), op=ALU.mult),
                     reads=[pxs, dt], writes=[xdt_pad])
            K.op(K.dve, lambda: nc.vector.tensor_tensor(xdtw.ap, pxs.ap.rearrange("p (h c) -> p h c", c=64),
                                                        dtw.ap.unsqueeze(2).to_broadcast([128, 32, 64]), op=ALU.mult),
                 reads=[pxs, dtw], writes=[xdtw])
            K.op(K.act, lambda: nc.scalar.copy(btok.ap.rearrange("p g n -> p (g n)"), pb.ap[:, 0:512]), reads=[pb], writes=[btok])
            if _DBG_STOP <= 3:
                continue
            for g in range(4):
                K.op(K.pe, lambda: nc.tensor.matmul(pupd.ap[:, g * 128:(g + 1) * 128],
                                                    xb.ap[:, 16 + g, :], xb.ap[:, 20 + g, :], start=True, stop=True),
                     reads=[xb], writes=[pupd])
            for g in range(4):
                K.op(K.dve, lambda: nc.vector.tensor_tensor(cbm.ap[:, g, :], pupd.ap[:, g * 128:(g + 1) * 128], C.tri.ap, op=ALU.mult),
                     reads=[pupd, C.tri], writes=[cbm])
            if _DBG_STOP <= 4:
                continue
            mts = {}
            css = {}
            for hq in range(8):
                g = hq // 2
                pa = pabc.next()
                for i in range(4):
                    h = hq * 4 + i
                    K.op(K.pe, lambda: nc.tensor.matmul(pa.ap[:, i * 128:(i + 1) * 128], a_ch.ap[:, h:h + 1].to_broadcast([128, 128]),
                                                        C.tri.ap, start=True, stop=True), reads=[a_ch, C.tri], writes=[pa])
                seg = seg_ring.next()
                dec = dec_ring.next()
                ea = ea_ring.next()
                mt = mt_ring.next()
                cs = cs_ring.next()
                for i in range(4):
                    h = hq * 4 + i
                    K.op(K.dve, lambda: nc.vector.tensor_scalar(seg.ap[:, i, :], pa.ap[:, i * 128:(i + 1) * 128], acum.ap[:, h:h + 1], 0.0,
                                                                op0=ALU.subtract, op1=ALU.min), reads=[pa, acum], writes=[seg])
                K.op(K.act, lambda: nc.scalar.activation(dec.ap, seg.ap, AF.Exp), reads=[seg], writes=[dec])
                K.op(K.act, lambda: nc.scalar.activation(ea.ap.rearrange("p i l -> p (i l)"), pa.ap, AF.Exp), reads=[pa], writes=[ea])
                for i in range(4):
                    K.op(K.pool, lambda: nc.gpsimd.tensor_tensor(mt.ap[:, i, :], dec.ap[:, i, :], cbm.ap[:, g, :], op=ALU.mult),
                         reads=[dec, cbm], writes=[mt])
                    K.op(K.pool, lambda: nc.gpsimd.tensor_tensor(cs.ap[:, i, :], ea.ap[:, i, :], xb.ap[:, 20 + g, :], op=ALU.mult),
                         reads=[ea, xb], writes=[cs])
                for pi in range(2):
                    q = hq * 2 + pi
                    pyq = py
                    yo = py.ap[:, (q % 4) * 128:(q % 4 + 1) * 128]
                    ops = [(xdt_pad, xdt_pad.ap[:, 2 * q, :], mt, mt.ap[:, 2 * pi, :]),
                           (xdt_pad, xdt_pad.ap[:, 2 * q + 1, :], mt, mt.ap[:, 2 * pi + 1, :]),
                           (st_pad, st_pad.ap[:, 2 * q, :], cs, cs.ap[:, 2 * pi, :]),
                           (st_pad, st_pad.ap[:, 2 * q + 1, :], cs, cs.ap[:, 2 * pi + 1, :])]
                    for k_, (lt, la, rt, ra) in enumerate(ops):
                        K.op(K.pe, lambda: nc.tensor.matmul(yo, la, ra, start=(k_ == 0), stop=(k_ == 3)), reads=[lt, rt], writes=[pyq])
                    yt = ytmp_ring.next()
                    K.op(K.dve, lambda: nc.vector.scalar_tensor_tensor(yt.ap, xb.ap[:, q, :], C.Dcol.ap[:, q:q + 1], yo,
                                                                       op0=ALU.mult, op1=ALU.add), reads=[xb, C.Dcol, pyq], writes=[yt])
                    K.op(K.pool, lambda: nc.gpsimd.tensor_tensor(ygate.ap[:, q, :], yt.ap, sz.ap[:, q, :], op=ALU.mult),
                         reads=[yt, sz], writes=[ygate])
            if _DBG_STOP <= 5:
                continue
            for g in range(4):
                K.op(K.pe, lambda: nc.tensor.matmul(pupd.ap, btok.ap[:, g, :], xdtw.ap[:, g * 8:(g + 1) * 8, :].rearrange("p h c -> p (h c)"),
                                                    start=True, stop=True), reads=[btok, xdtw], writes=[pupd])
                sv = state.ap[:, g * 8:(g + 1) * 8, :]
                K.op(K.dve, lambda: nc.vector.tensor_tensor(sv, sv, dA.ap[:, g * 8:(g + 1) * 8].unsqueeze(2).to_broadcast([128, 8, 64]),
                                                            op=ALU.mult), reads=[state, dA], writes=[state])
                K.op(K.dve, lambda: nc.vector.tensor_tensor(sv, sv, pupd.ap.rearrange("p (h c) -> p h c", c=64), op=ALU.add),
                     reads=[state, pupd], writes=[state])
            st4 = state.ap.rearrange("p (q r) c -> p q r c", r=2)
            for r in range(2):
                K.op(K.act, lambda: nc.scalar.copy(stp4[:, :, r, r * 64:(r + 1) * 64], st4[:, :, r, :]), reads=[state], writes=[st_pad])
            if _DBG_STOP <= 6:
                continue
            K.op(K.act, lambda: nc.scalar.activation(sq.ap, ygate.ap, AF.Square), reads=[ygate], writes=[sq])
            for G in range(4):
                for k_ in range(4):
                    K.op(K.pe, lambda: nc.tensor.matmul(psm.ap[:, G * 128:(G + 1) * 128], C.onesb.ap, sq.ap[:, G * 4 + k_, :],
                                                        start=(k_ == 0), stop=(k_ == 3)), reads=[C.onesb, sq], writes=[psm])
            K.op(K.act, lambda: nc.scalar.activation(grs.ap.rearrange("p g l -> p (g l)"), psm.ap, AF.Ln, bias=C.eps.ap, scale=1.0 / 512),
                 reads=[psm, C.eps], writes=[grs])
            K.op(K.act, lambda: nc.scalar.activation(grs.ap, grs.ap, AF.Exp, scale=-0.5), reads=[grs], writes=[grs])
            yn = yn_ring.next()
            for q in range(16):
                K.op(K.dve, lambda: nc.vector.scalar_tensor_tensor(yn.ap[:, q, :], ygate.ap[:, q, :], C.g_y[:, q:q + 1], grs.ap[:, q // 4, :],
                                                                   op0=ALU.mult, op1=ALU.mult), reads=[ygate, C.veca, grs], writes=[yn])
            K.dma(K.sp, ynv[:, :, cols], yn.ap, reads=[yn], writes=[ynd])
        K.end_phase()
        K.stack = K.es


def phase_out_ple(K, nc, C, name, a_scr, FC, w_o, h_in, p_in, g_ple, w_gate, w_proj, h_out):
    with ExitStack() as es:
        K.stack = es
        K.begin_phase()
        stage = Ring([K.sb(name + "_st%d" % i, [128, 1024], F32) for i in range(3)])
        Wo = load_weight(K, nc, es, name + "_wo", w_o, FC, 1024, stage, [K.pool, K.dve, K.act])
        Wg = load_weight(K, nc, es, name + "_wg", w_gate, 8, 1024, stage, [K.pool, K.dve, K.act])
        Wp = load_weight(K, nc, es, name + "_wp", w_proj, 2, 1024, stage, [K.pool, K.dve, K.act])
        at_ring = Ring([K.sb(name + "_at%d" % i, [128, FC, 512], BF16) for i in range(1)])
        h_ring = Ring([K.sb(name + "_h%d" % i, [128, 4, 1024], F32) for i in range(1)])
        p_ring = Ring([K.sb(name + "_p%d" % i, [128, 4, 256], F32) for i in range(2)])
        pbf = K.sb(name + "_pbf", [128, 4, 256], BF16)
        pT = K.sb(name + "_pT", [128, 2, 512], BF16)
        ss = K.sb(name + "_ss", [128, 4], F32)
        lnv = K.sb(name + "_ln", [128, 4], F32)
        rstd = K.sb(name + "_rstd", [128, 4], F32)
        junk = K.sb(name + "_junk", [128, 1024], BF16)
        xh = K.sb(name + "_xh", [128, 4, 1024], BF16)
        xT = K.sb(name + "_xT", [128, 8, 512], BF16)
        sig_ring = Ring([K.sb(name + "_sig%d" % i, [128, 512], F32) for i in range(2)])
        tmp_ring = Ring([K.sb(name + "_tmp%d" % i, [128, 512], F32) for i in range(2)])
        ho_ring = Ring([K.sb(name + "_ho%d" % i, [128, 4, 1024], F32) for i in range(1)])
        ptr = Ring([K.ps(name + "_pt%d" % i, [128, 1024], BF16) for i in range(2)])
        pmm = Ring([K.ps(name + "_pm%d" % i, [128, 512], F32) for i in range(2)])
        pg_ring = Ring([K.ps(name + "_pg%d" % i, [128, 512], F32) for i in range(2)])
        pp_ring = Ring([K.ps(name + "_pp%d" % i, [128, 512], F32) for i in range(2)])
        ad = K.dram(name + "_a", a_scr)
        hd = K.dram(name + "_hin", h_in)
        od = K.dram(name + "_hout", h_out)
        av = a_scr.rearrange("(c p) t -> p c t", p=128)
        hv = h_in.rearrange("(t a p) d -> t p a d", a=4, p=128)
        pv = p_in.rearrange("(t a p) d -> t p a d", a=4, p=128)
        ov = h_out.rearrange("(t a p) d -> t p a d", a=4, p=128)
        for ts in range(8):
            at = at_ring.next()
            ht = h_ring.next()
            pt_ = p_ring.next()
            K.dma(K.sp, at.ap, av[:, :, ts * 512:(ts + 1) * 512], reads=[ad], writes=[at])
            K.dma(K.sp, ht.ap, hv[ts], reads=[hd], writes=[ht])
            K.dma(K.sp, pt_.ap, pv[ts], writes=[pt_])
            for a in range(4):
                for half in range(2):
                    pm = pmm.next()
                    for c in range(FC):
                        K.op(K.pe, lambda: nc.tensor.matmul(pm.ap, at.ap[:, c, a * 128:(a + 1) * 128], Wo[c].ap[:, half * 512:(half + 1) * 512],
                                                            start=(c == 0), stop=(c == FC - 1)), reads=[at, Wo[c]], writes=[pm])
                    hs = ht.ap[:, a, half * 512:(half + 1) * 512]
                    K.op(K.dve, lambda: nc.vector.tensor_tensor(hs, hs, pm.ap, op=ALU.add), reads=[ht, pm], writes=[ht])
            rms_to_xT(K, nc, C, ht, 4, ss, lnv, rstd, junk, xh, ptr, [(xT, g_ple)])
            K.op(K.pool, lambda: nc.gpsimd.tensor_copy(pbf.ap, pt_.ap), reads=[pt_], writes=[pbf])
            for c2 in range(2):
                pt = ptr.next()
                for a in range(4):
                    K.op(K.pe, lambda: nc.tensor.transpose(pt.ap[:, a * 128:(a + 1) * 128], pbf.ap[:, a, c2 * 128:(c2 + 1) * 128], C.identb.ap),
                         reads=[pbf, C.identb], writes=[pt])
                K.op(K.act, lambda: nc.scalar.copy(pT.ap[:, c2, :], pt.ap[:, 0:512]), reads=[pt], writes=[pT])
            ho = ho_ring.next()
            for a in range(4):
                for half in range(2):
                    pg = pg_ring.next()
                    pp = pp_ring.next()
                    cs_ = slice(half * 512, (half + 1) * 512)
                    for c in range(8):
                        K.op(K.pe, lambda: nc.tensor.matmul(pg.ap, xT.ap[:, c, a * 128:(a + 1) * 128], Wg[c].ap[:, cs_],
                                                            start=(c == 0), stop=(c == 7)), reads=[xT, Wg[c]], writes=[pg])
                    for c in range(2):
                        K.op(K.pe, lambda: nc.tensor.matmul(pp.ap, pT.ap[:, c, a * 128:(a + 1) * 128], Wp[c].ap[:, cs_],
                                                            start=(c == 0), stop=(c == 1)), reads=[pT, Wp[c]], writes=[pp])
                    sg = sig_ring.next()
                    tm = tmp_ring.next()
                    K.op(K.act, lambda: nc.scalar.activation(sg.ap, pg.ap, AF.Sigmoid), reads=[pg], writes=[sg])
                    K.op(K.dve, lambda: nc.vector.tensor_tensor(tm.ap, pp.ap, sg.ap, op=ALU.mult), reads=[pp, sg], writes=[tm])
                    K.op(K.pool, lambda: nc.gpsimd.tensor_tensor(ho.ap[:, a, cs_], tm.ap, ht.ap[:, a, cs_], op=ALU.add),
                         reads=[tm, ht], writes=[ho])
            K.dma(K.sp, ov[ts], ho.ap, reads=[ho], writes=[od])
        K.end_phase()
        K.stack = K.es


def phase_p3(K, nc, C, h1, w_kv, s_in, kt_scr, v_scr, q_scr, sg_scr):
    with ExitStack() as es:
        K.stack = es
        K.begin_phase()
        stage = Ring([K.sb("p3_st%d" % i, [128, 2048], F32) for i in range(2)])
        Wkv = load_weight(K, nc, es, "p3_wkv", w_kv, 8, 2048, stage, [K.pool, K.dve, K.act])
        Wqg = load_weight(K, nc, es, "p3_wqg", s_in, 8, 2048, stage, [K.pool, K.dve, K.act])
        h_ring = Ring([K.sb("p3_h%d" % i, [128, 4, 1024], F32) for i in range(2)])
        ss = K.sb("p3_ss", [128, 4], F32)
        lnv = K.sb("p3_ln", [128, 4], F32)
        rstd = K.sb("p3_rstd", [128, 4], F32)
        junk = K.sb("p3_junk", [128, 1024], BF16)
        xh = K.sb("p3_xh", [128, 4, 1024], BF16)
        xTk = K.sb("p3_xTk", [128, 8, 512], BF16)
        xTq = K.sb("p3_xTq", [128, 8, 512], BF16)
        sq_ring = Ring([K.sb("p3_sq%d" % i, [128, 512], BF16) for i in range(2)])
        rs_ring = Ring([K.sb("p3_rs%d" % i, [128, 512], F32) for i in range(2)])
        kst = K.sb("p3_kst", [128, 8, 512], BF16)
        vst = K.sb("p3_vst", [128, 4, 1024], BF16)
        qst = K.sb("p3_qst", [128, 8, 512], BF16)
        gst = K.sb("p3_gst", [128, 8, 512], BF16)
        ptr = Ring([K.ps("p3_pt%d" % i, [128, 1024], BF16) for i in range(2)])
        pmm = Ring([K.ps("p3_pm%d" % i, [128, 512], F32) for i in range(3)])
        pn_ring = Ring([K.ps("p3_pn%d" % i, [128, 512], F32) for i in range(2)])
        hd = K.dram("p3_h1", h1)
        kd = K.dram("kt_scr", kt_scr)
        vd = K.dram("v_scr", v_scr)
        qd = K.dram("q_scr", q_scr)
        gd = K.dram("sg_scr", sg_scr)
        hv = h1.rearrange("(t a p) d -> t p a d", a=4, p=128)
        vv = v_scr.rearrange("(t a p) d -> t p a d", a=4, p=128)

        def headnorm(pm, gcol, out_ap, out_t):
            sq = sq_ring.next()
            rs = rs_ring.next()
            pn = pn_ring.next()
            K.op(K.act, lambda: nc.scalar.activation(sq.ap, pm.ap, AF.Square), reads=[pm], writes=[sq])
            K.op(K.pe, lambda: nc.tensor.matmul(pn.ap, C.blk.ap, sq.ap, start=True, stop=True), reads=[C.blk, sq], writes=[pn])
            K.op(K.act, lambda: nc.scalar.activation(rs.ap, pn.ap, AF.Ln, bias=C.eps.ap, scale=1.0 / 64), reads=[pn, C.eps], writes=[rs])
            K.op(K.act, lambda: nc.scalar.activation(rs.ap, rs.ap, AF.Exp, scale=-0.5), reads=[rs], writes=[rs])
            K.op(K.dve, lambda: nc.vector.scalar_tensor_tensor(out_ap, pm.ap, gcol, rs.ap, op0=ALU.mult, op1=ALU.mult),
                 reads=[pm, C.kq, rs], writes=[out_t])

        for ts in range(8):
            ht = h_ring.next()
            K.dma(K.sp, ht.ap, hv[ts], reads=[hd], writes=[ht])
            rms_to_xT(K, nc, C, ht, 4, ss, lnv, rstd, junk, xh, ptr, [(xTk, C.g_kv), (xTq, C.g_s)])
            tok = slice(ts * 512, (ts + 1) * 512)
            for fo in range(8):
                pm = pmm.next()
                for c in range(8):
                    K.op(K.pe, lambda: nc.tensor.matmul(pm.ap, Wkv[c].ap[:, fo * 128:(fo + 1) * 128], xTk.ap[:, c, :],
                                                        start=(c == 0), stop=(c == 7)), reads=[Wkv[c], xTk], writes=[pm])
                headnorm(pm, C.kq.ap[:, 0:1], kst.ap[:, fo, :], kst)
            for a in range(4):
                for half in range(2):
                    pm = pmm.next()
                    for c in range(8):
                        K.op(K.pe, lambda: nc.tensor.matmul(pm.ap, xTk.ap[:, c, a * 128:(a + 1) * 128],
                                                            Wkv[c].ap[:, 1024 + half * 512:1024 + (half + 1) * 512],
                                                            start=(c == 0), stop=(c == 7)), reads=[Wkv[c], xTk], writes=[pm])
                    K.op(K.act, lambda: nc.scalar.copy(vst.ap[:, a, half * 512:(half + 1) * 512], pm.ap), reads=[pm], writes=[vst])
            for fo in range(8):
                pm = pmm.next()
                for c in range(8):
                    K.op(K.pe, lambda: nc.tensor.matmul(pm.ap, Wqg[c].ap[:, fo * 128:(fo + 1) * 128], xTq.ap[:, c, :],
                                                        start=(c == 0), stop=(c == 7)), reads=[Wqg[c], xTq], writes=[pm])
                headnorm(pm, C.kq.ap[:, 1:2], qst.ap[:, fo, :], qst)
            for fo in range(8):
                pm = pmm.next()
                for c in range(8):
                    K.op(K.pe, lambda: nc.tensor.matmul(pm.ap, Wqg[c].ap[:, 1024 + fo * 128:1024 + (fo + 1) * 128], xTq.ap[:, c, :],
                                                        start=(c == 0), stop=(c == 7)), reads=[Wqg[c], xTq], writes=[pm])
                K.op(K.act, lambda: nc.scalar.activation(gst.ap[:, fo, :], pm.ap, AF.Silu), reads=[pm], writes=[gst])
            for hf in range(2):
                qs = slice(hf * 4, (hf + 1) * 4)
                K.dma(K.sp, kt_scr.rearrange("(q p) t -> p q t", p=128)[:, qs, tok], kst.ap[:, qs, :], reads=[kst], writes=[kd])
                K.dma(K.sp, q_scr.rearrange("(q p) t -> p q t", p=128)[:, qs, tok], qst.ap[:, qs, :], reads=[qst], writes=[qd])
                K.dma(K.sp, sg_scr.rearrange("(q p) t -> p q t", p=128)[:, qs, tok], gst.ap[:, qs, :], reads=[gst], writes=[gd])
            K.dma(K.sp, vv[ts], vst.ap, reads=[vst], writes=[vd])
        K.end_phase()
        K.stack = K.es


def phase_p4(K, nc, C, kt_scr, v_scr, q_scr, sg_scr, og_scr, nTB=8, nQ=8):
    with ExitStack() as es:
        K.stack = es
        K.begin_phase()
        ntri = K.sb("p4_ntri", [128, 128], BF16)
        ebig = K.sb("p4_ebig", [128, 255], BF16)
        nsel = K.sb("p4_nsel", [128, 32, 128], BF16)
        masks = K.sb("p4_masks", [128, 4, 512], BF16)
        with ExitStack() as est:
            K.stack = est
            negb = K.sb("p4_negb", [128, 128], BF16)
            negb3 = K.sb("p4_negb3", [128, 32, 128], BF16)
            oneb3 = K.sb("p4_oneb3", [128, 4, 512], BF16)
            K.op(K.pool, lambda: nc.gpsimd.memset(negb.ap, -1.0), writes=[negb])
            K.op(K.pool, lambda: nc.gpsimd.affine_select(ntri.ap, negb.ap, pattern=[[-1, 128]], compare_op=ALU.is_ge, fill=0.0,
                                                         base=0, channel_multiplier=1), reads=[negb], writes=[ntri])
            K.op(K.pool, lambda: nc.gpsimd.memset(ebig.ap, 0.0), writes=[ebig])
            K.op(K.pool, lambda: nc.gpsimd.memset(ebig.ap[:, 127:128], 1.0), writes=[ebig])
            K.op(K.pool, lambda: nc.gpsimd.memset(negb3.ap, -1.0), writes=[negb3])
            K.op(K.pool, lambda: nc.gpsimd.affine_select(nsel.ap, negb3.ap, pattern=[[-1, 32], [0, 128]], compare_op=ALU.is_ge, fill=0.0,
                                                         base=-1, channel_multiplier=1), reads=[negb3], writes=[nsel])
            K.op(K.pool, lambda: nc.gpsimd.memset(oneb3.ap, 1.0), writes=[oneb3])
            K.op(K.pool, lambda: nc.gpsimd.affine_select(masks.ap, oneb3.ap, pattern=[[-128, 4], [1, 512]], compare_op=ALU.is_gt, fill=0.0,
                                                         base=0, channel_multiplier=-1), reads=[oneb3], writes=[masks])
            K.barrier()
            K.stack = es
        qpad = [Ring([K.sb("p4_qp%d_%d" % (r, i), [128, 512], BF16) for i in range(2)]) for r in range(2)]
        for r in range(2):
            for t_ in qpad[r].t:
                K.op(K.pool, lambda: nc.gpsimd.memset(t_.ap, 0.0), writes=[t_])
        kt_ring = Ring([K.sb("p4_kt%d" % i, [128, S], BF16) for i in range(2)])
        v_ring = Ring([K.sb("p4_v%d" % i, [128, 32, 128], BF16) for i in range(2)])
        sgt_ring = Ring([K.sb("p4_sg%d" % i, [128, 512], BF16) for i in range(2)])
        ogst_ring = Ring([K.sb("p4_og%d" % i, [128, 512], BF16) for i in range(2)])
        SPs = [K.sb("p4_sp%d" % i, [128, 32, 512], BF16) for i in range(2)]
        SPvs = [[K.view("p4_sp%d_%d" % (b, i), SPs[b].ap[:, i, :]) for i in range(32)] for b in range(2)]
        e_ring = Ring([K.sb("p4_e%d" % i, [128, 512], F32) for i in range(3)])
        spt_ring = Ring([K.sb("p4_spt%d" % i, [128, 512], F32) for i in range(2)])
        csb_ring = Ring([K.sb("p4_csb%d" % i, [128, 512], BF16) for i in range(2)])
        wt_ring = Ring([K.sb("p4_wt%d" % i, [128, 512], BF16) for i in range(4)])
        wm_ring = Ring([K.sb("p4_wm%d" % i, [128, 512], BF16) for i in range(2)])
        pz = Ring([K.ps("p4_pz%d" % i, [128, 512], F32) for i in range(3)])
        pcs = K.ps("p4_pcs", [128, 512], F32)
        pgr = Ring([K.ps("p4_pg%d" % i, [128, 512], F32) for i in range(3)])
        po_ring = Ring([K.ps("p4_po%d" % i, [128, 512], F32) for i in range(1)])
        kd = K.dram("kt_scr", kt_scr)
        vd = K.dram("v_scr", v_scr)
        qd = K.dram("q_scr", q_scr)
        gd = K.dram("sg_scr", sg_scr)
        od = K.dram("og_scr", og_scr)
        vv = v_scr.rearrange("(jb p) (q c) -> q p jb c", p=128, c=128)
        hcount = 0
        pending = None
        for q in range(nQ):
            KT = kt_ring.next()
            V = v_ring.next()
            K.dma(K.sp, KT.ap, kt_scr[q * 128:(q + 1) * 128, :], reads=[kd], writes=[KT])
            K.dma(K.sp, V.ap, vv[q], reads=[vd], writes=[V])
            for TB in range(nTB):
                tok = slice(TB * 512, (TB + 1) * 512)
                nkb = 4 * (TB + 1)
                sgt = sgt_ring.next()
                K.dma(K.sp, sgt.ap, sg_scr[q * 128:(q + 1) * 128, tok], reads=[gd], writes=[sgt])
                ogst = ogst_ring.next()
                qps = []
                for r in range(2):
                    qp = qpad[r].next()
                    K.dma(K.sp, qp.ap[r * 64:(r + 1) * 64, :], q_scr[q * 128 + r * 64:q * 128 + (r + 1) * 64, tok], reads=[qd], writes=[qp])
                    qps.append(qp)
                if pending is not None:
                    pending()
                    pending = None
                for r in range(2):
                    rows = slice(r * 64, (r + 1) * 64)
                    qp = qps[r]
                    SPv = SPvs[hcount % 2]
                    hcount += 1
                    zs, es_ = {}, {}

                    def emit_z(jb):
                        z = pz.next()
                        K.op(K.pe, lambda: nc.tensor.matmul(z.ap, KT.ap[:, jb * 128:(jb + 1) * 128], qp.ap, start=True, stop=True),
                             reads=[KT, qp], writes=[z])
                        zs[jb] = z

                    def emit_exp(jb):
                        e = e_ring.next()
                        K.op(K.act, lambda: nc.scalar.activation(e.ap, zs[jb].ap, AF.Exp), reads=[zs[jb]], writes=[e])
                        es_[jb] = e

                    def emit_ln(jb):
                        e = es_[jb]
                        rr = jb - 4 * TB
                        if rr < 0:
                            K.op(K.act, lambda: nc.scalar.activation(SPv[jb].ap, e.ap, AF.Ln, bias=C.one.ap, scale=1.0),
                                 reads=[e, C.one], writes=[SPv[jb]])
                        else:
                            spt = spt_ring.next()
                            K.op(K.act, lambda: nc.scalar.activation(spt.ap, e.ap, AF.Ln, bias=C.one.ap, scale=1.0),
                                 reads=[e, C.one], writes=[spt])
                            K.op(K.dve, lambda: nc.vector.tensor_tensor(SPv[jb].ap, spt.ap, masks.ap[:, rr, :], op=ALU.mult),
                                 reads=[spt, masks], writes=[SPv[jb]])

                    def emit_cs(jb):
                        K.op(K.pe, lambda: nc.tensor.matmul(pcs.ap, ebig.ap[:, 127 - jb:255 - jb], SPv[jb].ap,
                                                            start=(jb == 0), stop=(jb == nkb - 1)), reads=[ebig, SPv[jb]], writes=[pcs])

                    emit_z(0)
                    if nkb > 1:
                        emit_z(1)
                    emit_exp(0)
                    for jb in range(nkb):
                        if jb + 1 < nkb:
                            emit_exp(jb + 1)
                        emit_ln(jb)
                        if jb + 2 < nkb:
                            emit_z(jb + 2)
                        emit_cs(jb)
                    csb = csb_ring.next()
                    K.op(K.dve, lambda: nc.vector.tensor_copy(csb.ap, pcs.ap), reads=[pcs], writes=[csb])
                    po = po_ring.next()
                    gs = {}

                    def emit_G(jb):
                        gq = pgr.next()
                        K.op(K.pe, lambda: nc.tensor.matmul(gq.ap, ntri.ap, SPv[jb].ap, start=True, stop=False),
                             reads=[ntri, SPv[jb]], writes=[gq])
                        K.op(K.pe, lambda: nc.tensor.matmul(gq.ap, KT.ap[:, jb * 128:(jb + 1) * 128], qp.ap, start=False, stop=False),
                             reads=[KT, qp], writes=[gq])
                        K.op(K.pe, lambda: nc.tensor.matmul(gq.ap, nsel.ap[:, jb, :], csb.ap, start=False, stop=True),
                             reads=[nsel, csb], writes=[gq])
                        gs[jb] = gq

                    emit_G(0)
                    if nkb > 1:
                        emit_G(1)
                    for jb in range(nkb):
                        gq = gs[jb]
                        wt = wt_ring.next()
                        K.op(K.act, lambda: nc.scalar.activation(wt.ap, gq.ap, AF.Exp), reads=[gq], writes=[wt])
                        rr = jb - 4 * TB
                        if rr >= 0:
                            wm = wm_ring.next()
                            K.op(K.dve, lambda: nc.vector.tensor_tensor(wm.ap, wt.ap, masks.ap[:, rr, :], op=ALU.mult),
                                 reads=[wt, masks], writes=[wm])
                            wt = wm
                        if jb + 2 < nkb:
                            emit_G(jb + 2)
                        K.op(K.pe, lambda: nc.tensor.matmul(po.ap, V.ap[:, jb, :], wt.ap, start=(jb == 0), stop=(jb == nkb - 1)),
                             reads=[V, wt], writes=[po])
                    K.op(K.dve, lambda: nc.vector.tensor_tensor(ogst.ap[rows, :], po.ap[rows, :], sgt.ap[rows, :], op=ALU.mult),
                         reads=[po, sgt], writes=[ogst])
                pending = (lambda q_=q, tok_=tok, og_=ogst: K.dma(K.sp, og_scr[q_ * 128:(q_ + 1) * 128, tok_], og_.ap, reads=[og_], writes=[od]))
        if pending is not None:
            pending()
        K.end_phase()
        K.stack = K.es


PARAM_SHAPES = {
    "m_norm": [1, 1024], "m_in": [1, 1024, 5152], "m_conv_w": [1, 4, 3072], "m_conv_b": [1, 3072],
    "m_dt_bias": [1, 32], "m_A_log": [1, 32], "m_D": [1, 32], "m_ynorm": [1, 2048], "m_out": [1, 2048, 1024],
    "kv_norm": [1024], "w_kv": [1024, 2048], "k_norm": [64], "s_norm": [1, 1024], "s_in": [1, 1024, 2048],
    "q_norm": [1, 64], "s_out": [1, 1024, 1024], "ple_norm": [2, 1024], "ple_gate": [2, 1024, 1024],
    "ple_proj": [2, 256, 1024],
}

SCRATCH = {
    "sz_scr": ([2048, S], BF16), "xbc_scr": ([3072, S], BF16), "dt_scr": ([S, 32], F32),
    "yn_scr": ([2048, S], BF16), "h1_scr": ([S, D], F32), "q_scr": ([1024, S], BF16),
    "sg_scr": ([1024, S], BF16), "og_scr": ([1024, S], BF16),
    "kt_scr": ([1024, S], BF16), "v_scr": ([S, 1024], BF16),
}


def build(phases=("p1a", "p1b", "p2", "p3", "p4", "p5"), ext_in=(), ext_out=(), p1b_chunks=32, p4_tb=8, p4_q=8):
    nc = bass.Bass("TRN2", target_bir_lowering=False)
    x = nc.dram_tensor("x", [S, D], F32, kind="ExternalInput").ap()
    p0 = nc.dram_tensor("p0", [S, 256], F32, kind="ExternalInput").ap()
    p1 = nc.dram_tensor("p1", [S, 256], F32, kind="ExternalInput").ap()
    prm = {k: nc.dram_tensor(k, shp, F32, kind="ExternalInput").ap() for k, shp in PARAM_SHAPES.items()}
    out = nc.dram_tensor("out", [S, D], F32, kind="ExternalOutput").ap()
    scr = {}
    for k, (shp, dt_) in SCRATCH.items():
        kind = "ExternalInput" if k in ext_in else ("ExternalOutput" if k in ext_out else "Internal")
        scr[k] = nc.dram_tensor(k, shp, dt_, kind=kind).ap()
    with ExitStack() as es:
        K = Ctx(nc, es)
        C = make_consts(K, nc, prm)
        if "p1a" in phases:
            phase_p1a(K, nc, C, x, prm["m_in"][0], scr["sz_scr"], scr["xbc_scr"], scr["dt_scr"])
        if "p1b" in phases:
            phase_p1b(K, nc, C, scr["sz_scr"], scr["xbc_scr"], scr["dt_scr"], scr["yn_scr"], nchunks=p1b_chunks)
        if "p2" in phases:
            phase_out_ple(K, nc, C, "p2", scr["yn_scr"], 16, prm["m_out"][0], x, p0, C.g_p0,
                          prm["ple_gate"][0], prm["ple_proj"][0], scr["h1_scr"])
        if "p3" in phases:
            phase_p3(K, nc, C, scr["h1_scr"], prm["w_kv"], prm["s_in"][0], scr["kt_scr"], scr["v_scr"], scr["q_scr"], scr["sg_scr"])
        if "p4" in phases:
            phase_p4(K, nc, C, scr["kt_scr"], scr["v_scr"], scr["q_scr"], scr["sg_scr"], scr["og_scr"], nTB=p4_tb, nQ=p4_q)
        if "p5" in phases:
            phase_out_ple(K, nc, C, "p5", scr["og_scr"], 8, prm["s_out"][0], scr["h1_scr"], p1, C.g_p1,
                          prm["ple_gate"][1], prm["ple_proj"][1], out)
        K.finish()
    return nc


_NC_CACHE = {}


def kernel(**inputs):
    if "full" not in _NC_CACHE:
        _NC_CACHE["full"] = build()
    nc = _NC_CACHE["full"]
    x = np.asarray(inputs["x"], dtype=np.float32)
    p = np.asarray(inputs["p"], dtype=np.float32)
    in_maps = []
    for b in range(NCORES):
        m = {"x": np.ascontiguousarray(x[b]), "p0": np.ascontiguousarray(p[0, b]), "p1": np.ascontiguousarray(p[1, b])}
        for k in PARAM_SHAPES:
            m[k] = np.ascontiguousarray(np.asarray(inputs[k], dtype=np.float32))
        in_maps.append(m)
    res = run_bass_kernel_spmd(nc, in_maps, core_ids=list(range(NCORES)))
    return np.stack([np.asarray(res.results[b]["out"], dtype=np.float32) for b in range(NCORES)], axis=0)
```

```python
from bisect import bisect_left
from contextlib import ExitStack
import os
import numpy as np
import concourse.bass as bass
import concourse.mybir as mybir
from concourse.alu_op_type import AluOpType as ALU
from concourse.bass_utils import run_bass_kernel_spmd

F32 = mybir.dt.float32
BF16 = mybir.dt.bfloat16
AF = mybir.ActivationFunctionType

S = 4096
D = 1024
_DBG_STOP = int(os.environ.get('P1B_STOP', '99'))
NCORES = 8
EPS = 1e-6


class Eng:
    def __init__(self, name, eng, sem, eager):
        self.name, self.eng, self.sem, self.eager = name, eng, sem, eager
        self.n = 0
        self.count = 0
        self.sig_idx = []
        self.sig_cnt = []
        self.last = None
        self.last_signaled = True
        self.known = {}

    def signal_last(self):
        if not self.last_signaled:
            self.last.then_inc(self.sem, 1)
            self.count += 1
            self.sig_idx.append(self.n)
            self.sig_cnt.append(self.count)
            self.last_signaled = True

    def count_for(self, idx):
        i = bisect_left(self.sig_idx, idx)
        if i == len(self.sig_idx):
            self.signal_last()
            i = len(self.sig_idx) - 1
        assert self.sig_idx[i] >= idx
        return self.sig_cnt[i]


class DSem:
    def __init__(self, sem):
        self.sem = sem
        self.issued = 0


class T:
    def __init__(self, name, ap, space):
        self.name, self.ap, self.space = name, ap, space
        self.w = None
        self.r = {}
        self.dsem = None

    def __getitem__(self, k):
        return self.ap[k]


class Ring:
    def __init__(self, tiles):
        self.t = tiles
        self.i = 0

    def next(self):
        t = self.t[self.i % len(self.t)]
        self.i += 1
        return t


class Ctx:
    def __init__(self, nc, es):
        self.nc, self.es = nc, es
        mk = lambda n: es.enter_context(nc.semaphore(n))
        self.pe = Eng("pe", nc.tensor, mk("s_pe"), False)
        self.act = Eng("act", nc.scalar, mk("s_act"), True)
        self.dve = Eng("dve", nc.vector, mk("s_dve"), True)
        self.pool = Eng("pool", nc.gpsimd, mk("s_pool"), True)
        self.sp = Eng("sp", nc.sync, mk("s_sp"), True)
        self.engs = [self.pe, self.act, self.dve, self.pool, self.sp]
        self.dsems = []
        self.free_dsems = []
        self.nsem = 0
        self.stack = es

    def sb(self, name, shape, dtype, es=None):
        t = (es or self.stack).enter_context(self.nc.sbuf_tensor(name, shape, dtype))
        return T(name, t.ap(), "sb")

    def ps(self, name, shape, dtype, es=None):
        t = (es or self.stack).enter_context(self.nc.psum_tensor(name, shape, dtype))
        return T(name, t.ap(), "ps")

    def dram(self, name, ap):
        return T(name, ap, "dram")

    def view(self, name, ap, space="sb"):
        return T(name, ap, space)

    def begin_phase(self):
        self.phase_dsems = []

    def end_phase(self):
        self.barrier()
        self.free_dsems.extend(self.phase_dsems)
        self.phase_dsems = None

    def new_dsem(self):
        if self.free_dsems:
            d = self.free_dsems.pop()
        else:
            self.nsem += 1
            d = DSem(self.es.enter_context(self.nc.semaphore("s_d%d" % self.nsem)))
            self.dsems.append(d)
        if getattr(self, "phase_dsems", None) is not None:
            self.phase_dsems.append(d)
        return d

    def _waits(self, E, reads, writes):
        need = {}

        def add(ev, same_ok):
            if ev is None:
                return
            if ev[0] == "e":
                Dn, idx = ev[1], ev[2]
                if Dn is E and same_ok:
                    return
                c = Dn.count_for(idx)
                sem = Dn.sem
            else:
                c = ev[1].issued * 16
                sem = ev[1].sem
            key = id(sem)
            if need.get(key, (None, 0))[1] < c:
                need[key] = (sem, c)

        for t in reads:
            add(t.w, E is self.pe)
        for t in writes:
            add(t.w, True)
            for ev in t.r.values():
                add(ev, True)
        for key, (sem, c) in need.items():
            if E.known.get(key, 0) >= c:
                continue
            E.eng.wait_ge(sem, c)
            E.known[key] = c

    def op(self, E, make, reads=(), writes=()):
        writes = list(writes) + [t for t in reads if t.space == "ps"]
        reads = [t for t in reads if t.space != "ps"]
        self._waits(E, reads, writes)
        inst = make()
        E.n += 1
        E.last = inst
        E.last_signaled = False
        if E.eager:
            E.signal_last()
        ev = ("e", E, E.n)
        for t in reads:
            t.r[id(E)] = ev
        for t in writes:
            t.w = ev
            t.r = {}
        return inst

    def dma(self, Q, out_ap, in_ap, reads=(), writes=(), dsem=None, **kw):
        self._waits(Q, reads, writes)
        if dsem is None:
            cand = ([t for t in writes if t.space != "dram"] or [t for t in reads if t.space != "dram"]
                    or list(writes) or list(reads))
            t0 = cand[0]
            if t0.dsem is None:
                t0.dsem = self.new_dsem()
            dsem = t0.dsem
        inst = Q.eng.dma_start(out=out_ap, in_=in_ap, **kw)
        inst.then_inc(dsem.sem, 16)
        dsem.issued += 1
        ev = ("d", dsem)
        for t in reads:
            t.r[id(dsem)] = ev
        for t in writes:
            t.w = ev
            t.r = {}
        return inst

    def barrier(self):
        for E in self.engs:
            E.signal_last()
        for E in self.engs:
            for Dn in self.engs:
                if Dn is E or Dn.count == 0:
                    continue
                key = id(Dn.sem)
                if E.known.get(key, 0) < Dn.count:
                    E.eng.wait_ge(Dn.sem, Dn.count)
                    E.known[key] = Dn.count
            for d in self.dsems:
                if d.issued:
                    key = id(d.sem)
                    if E.known.get(key, 0) < d.issued * 16:
                        E.eng.wait_ge(d.sem, d.issued * 16)
                        E.known[key] = d.issued * 16

    def finish(self):
        for d in self.dsems:
            if d.issued:
                self.sp.eng.wait_ge(d.sem, d.issued * 16)


class Consts:
    pass


def make_consts(K, nc, prm):
    C = Consts()
    onesf = K.sb("c_onesf", [128, 128], F32)
    K.op(K.pool, lambda: nc.gpsimd.memset(onesf.ap, 1.0), writes=[onesf])
    C.onesf = onesf
    identf = K.sb("c_identf", [128, 128], F32)
    K.op(K.pool, lambda: nc.gpsimd.affine_select(identf.ap, onesf.ap, pattern=[[1, 128]], compare_op=ALU.is_equal,
                                                 fill=0.0, base=0, channel_multiplier=-1), reads=[onesf], writes=[identf])
    C.identf = identf
    identb = K.sb("c_identb", [128, 128], BF16)
    K.op(K.pool, lambda: nc.gpsimd.tensor_copy(identb.ap, identf.ap), reads=[identf], writes=[identb])
    C.identb = identb
    onesb = K.sb("c_onesb", [128, 128], BF16)
    K.op(K.pool, lambda: nc.gpsimd.memset(onesb.ap, 1.0), writes=[onesb])
    C.onesb = onesb
    tri = K.sb("c_tri", [128, 128], F32)
    K.op(K.pool, lambda: nc.gpsimd.affine_select(tri.ap, onesf.ap, pattern=[[1, 128]], compare_op=ALU.is_ge,
                                                 fill=0.0, base=0, channel_multiplier=-1), reads=[onesf], writes=[tri])
    C.tri = tri
    epsc = K.sb("c_eps", [128, 1], F32)
    K.op(K.pool, lambda: nc.gpsimd.memset(epsc.ap, EPS), writes=[epsc])
    C.eps = epsc
    onec = K.sb("c_one", [128, 1], F32)
    K.op(K.pool, lambda: nc.gpsimd.memset(onec.ap, 1.0), writes=[onec])
    C.one = onec
    blk = K.sb("c_blk", [128, 128], BF16)
    K.op(K.pool, lambda: nc.gpsimd.memset(blk.ap, 0.0), writes=[blk])
    K.op(K.pool, lambda: nc.gpsimd.memset(blk.ap[0:64, 0:64], 1.0), writes=[blk])
    K.op(K.pool, lambda: nc.gpsimd.memset(blk.ap[64:128, 64:128], 1.0), writes=[blk])
    C.blk = blk

    def vec_cols(name, items):
        R = sum(n for _, n in items)
        st = K.sb("vst_" + name, [R, 128], F32)
        r0 = 0
        for ap1, n in items:
            K.dma(K.sp, st.ap[r0:r0 + n, :], ap1.rearrange("(c p) -> c p", p=128), writes=[st])
            r0 += n
        pt = K.ps("vps_" + name, [128, 512], F32)
        K.op(K.pe, lambda: nc.tensor.transpose(pt.ap[:, 0:R], st.ap, identf.ap[0:R, 0:R]), reads=[st, identf], writes=[pt])
        out = K.sb("vec_" + name, [128, R], F32)
        K.op(K.dve, lambda: nc.vector.tensor_copy(out.ap, pt.ap[:, 0:R]), reads=[pt], writes=[out])
        return out

    veca = K.sb("c_veca", [128, 80], F32)
    vecb = K.sb("c_vecb", [128, 96], F32)
    with ExitStack() as es2:
        K.stack = es2
        va = vec_cols("a", [(prm["m_norm"][0], 8), (prm["kv_norm"], 8), (prm["s_norm"][0], 8),
                            (prm["ple_norm"][0], 8), (prm["ple_norm"][1], 8), (prm["m_ynorm"][0], 16),
                            (prm["m_conv_b"][0], 24)])
        vb = vec_cols("b", [(prm["m_conv_w"][0, j], 24) for j in range(4)])
        K.op(K.dve, lambda: nc.vector.tensor_copy(veca.ap, va.ap), reads=[va], writes=[veca])
        K.op(K.dve, lambda: nc.vector.tensor_copy(vecb.ap, vb.ap), reads=[vb], writes=[vecb])
        K.barrier()
        K.stack = K.es
    C.g_m, C.g_kv, C.g_s = veca.ap[:, 0:8], veca.ap[:, 8:16], veca.ap[:, 16:24]
    C.g_p0, C.g_p1 = veca.ap[:, 24:32], veca.ap[:, 32:40]
    C.g_y, C.conv_b = veca.ap[:, 40:56], veca.ap[:, 56:80]
    C.conv_w = vecb.ap
    C.veca, C.vecb = veca, vecb

    kq = K.sb("c_kq", [128, 2], F32)
    for half in range(2):
        K.dma(K.sp, kq.ap[half * 64:(half + 1) * 64, 0:1], prm["k_norm"].rearrange("(p o) -> p o", o=1), writes=[kq])
        K.dma(K.sp, kq.ap[half * 64:(half + 1) * 64, 1:2], prm["q_norm"][0].rearrange("(p o) -> p o", o=1), writes=[kq])
    kq2 = K.sb("c_kq2", [128, 2], F32)
    K.op(K.dve, lambda: nc.vector.tensor_copy(kq2.ap[:, 0:1], kq.ap[:, 0:1]), reads=[kq], writes=[kq2])
    K.op(K.dve, lambda: nc.vector.tensor_scalar(kq2.ap[:, 1:2], kq.ap[:, 1:2], 0.125, None, op0=ALU.mult), reads=[kq], writes=[kq2])
    C.kq = kq2
    bc = K.sb("c_bc", [128, 3, 32], F32)
    K.dma(K.sp, bc.ap[:, 0, :], prm["m_dt_bias"][0:1, :].to_broadcast([128, 32]), writes=[bc])
    K.dma(K.sp, bc.ap[:, 1, :], prm["m_A_log"][0:1, :].to_broadcast([128, 32]), writes=[bc])
    K.dma(K.sp, bc.ap[:, 2, :], prm["m_D"][0:1, :].to_broadcast([128, 32]), writes=[bc])
    C.bc = bc
    Abc = K.sb("c_A", [128, 32], F32)
    K.op(K.act, lambda: nc.scalar.activation(Abc.ap, bc.ap[:, 1, :], AF.Exp), reads=[bc], writes=[Abc])
    K.op(K.dve, lambda: nc.vector.tensor_scalar(Abc.ap, Abc.ap, -1.0, None, op0=ALU.mult), reads=[Abc], writes=[Abc])
    C.A = Abc
    Dcol = K.sb("c_Dcol", [128, 16], F32)
    dv = bc.ap[:, 2, :].rearrange("p (q r) -> p q r", r=2)
    K.op(K.dve, lambda: nc.vector.tensor_copy(Dcol.ap[0:64, :], dv[0:64, :, 0]), reads=[bc], writes=[Dcol])
    K.op(K.dve, lambda: nc.vector.tensor_copy(Dcol.ap[64:128, :], dv[64:128, :, 1]), reads=[bc], writes=[Dcol])
    C.Dcol = Dcol
    return C


def load_weight(K, nc, es, name, src, C_, N, stage_ring, cast_engs):
    w = K.sb(name, [128, C_, N], BF16, es)
    views = [K.view("%s_%d" % (name, c), w.ap[:, c, :]) for c in range(C_)]
    SW = stage_ring.t[0].ap.shape[1]
    i = 0
    for c in range(C_):
        for n0 in range(0, N, SW):
            n1 = min(N, n0 + SW)
            st = stage_ring.next()
            K.dma(K.sp, st.ap[:, 0:n1 - n0], src[c * 128:(c + 1) * 128, n0:n1], writes=[st])
            E = cast_engs[i % len(cast_engs)]
            i += 1
            if E is K.act:
                K.op(E, lambda: nc.scalar.copy(views[c].ap[:, n0:n1], st.ap[:, 0:n1 - n0]), reads=[st], writes=[views[c]])
            else:
                K.op(E, lambda: E.eng.tensor_copy(views[c].ap[:, n0:n1], st.ap[:, 0:n1 - n0]), reads=[st], writes=[views[c]])
    return views


def rms_to_xT(K, nc, C, ht, na, ss, lnv, rstd, junk, xh, ptr_ring, outs):
    for a in range(na):
        K.op(K.act, lambda: nc.scalar.activation(junk.ap, ht.ap[:, a, :], AF.Square, accum_out=ss.ap[:, a:a + 1]),
             reads=[ht], writes=[junk, ss])
    K.op(K.act, lambda: nc.scalar.copy(junk.ap[:, 0:8], junk.ap[:, 8:16]), reads=[junk], writes=[junk])
    K.op(K.act, lambda: nc.scalar.activation(lnv.ap[:, 0:na], ss.ap[:, 0:na], AF.Ln, bias=C.eps.ap, scale=1.0 / D),
         reads=[ss, C.eps], writes=[lnv])
    K.op(K.act, lambda: nc.scalar.activation(rstd.ap[:, 0:na], lnv.ap[:, 0:na], AF.Exp, scale=-0.5), reads=[lnv], writes=[rstd])
    for a in range(na):
        K.op(K.dve, lambda: nc.vector.tensor_scalar(xh.ap[:, a, :], ht.ap[:, a, :], rstd.ap[:, a:a + 1], None, op0=ALU.mult),
             reads=[ht, rstd], writes=[xh])
    k = 0
    for c in range(8):
        pt = ptr_ring.next()
        for a in range(na):
            K.op(K.pe, lambda: nc.tensor.transpose(pt.ap[:, a * 128:(a + 1) * 128], xh.ap[:, a, c * 128:(c + 1) * 128], C.identb.ap),
                 reads=[xh, C.identb], writes=[pt])
        for (xT, g) in outs:
            if k % 2 == 0:
                K.op(K.act, lambda: nc.scalar.activation(xT.ap[:, c, :], pt.ap[:, 0:na * 128], AF.Identity, scale=g[:, c:c + 1]),
                     reads=[pt, C.veca], writes=[xT])
            else:
                K.op(K.dve, lambda: nc.vector.tensor_scalar(xT.ap[:, c, :], pt.ap[:, 0:na * 128], g[:, c:c + 1], None, op0=ALU.mult),
                     reads=[pt, C.veca], writes=[xT])
            k += 1


def phase_p1a(K, nc, C, x_d, w_in, sz_scr, xbc_scr, dt_scr):
    with ExitStack() as es:
        K.stack = es
        K.begin_phase()
        stage = Ring([K.sb("p1a_st%d" % i, [128, 1288], F32) for i in range(2)])
        W = load_weight(K, nc, es, "p1a_w", w_in, 8, 5152, stage, [K.pool, K.dve, K.act])
        xt_ring = Ring([K.sb("p1a_x%d" % i, [128, 4, 1024], F32) for i in range(1)])
        ss = K.sb("p1a_ss", [128, 4], F32)
        lnv = K.sb("p1a_ln", [128, 4], F32)
        rstd = K.sb("p1a_rstd", [128, 4], F32)
        junk = K.sb("p1a_junk", [128, 1024], BF16)
        xh = K.sb("p1a_xh", [128, 4, 1024], BF16)
        xT = K.sb("p1a_xT", [128, 8, 512], BF16)
        ptr = Ring([K.ps("p1a_pt%d" % i, [128, 1024], BF16) for i in range(2)])
        pmm = Ring([K.ps("p1a_pm%d" % i, [128, 512], F32) for i in range(4)])
        pdt = K.ps("p1a_pdt", [128, 512], F32)
        ost_ring = Ring([K.sb("p1a_ost%d" % i, [128, 512], BF16) for i in range(6)])
        raw_ring = Ring([K.sb("p1a_raw%d" % i, [128, 515], F32) for i in range(3)])
        acc_ring = Ring([K.sb("p1a_acc%d" % i, [128, 512], F32) for i in range(2)])
        halo = K.sb("p1a_halo", [128, 24, 3], F32)
        halos = [K.view("p1a_halo%d" % i, halo.ap[:, i, :]) for i in range(24)]
        K.op(K.pool, lambda: nc.gpsimd.memset(halo.ap, 0.0), writes=halos)
        dtx = K.sb("p1a_dtx", [128, 4, 32], F32)
        dta = K.sb("p1a_dta", [128, 4, 32], F32)
        dte = K.sb("p1a_dte", [128, 4, 32], F32)
        dtm = K.sb("p1a_dtm", [128, 4, 32], F32)
        dto = K.sb("p1a_dto", [128, 4, 32], F32)
        szd = K.dram("sz_scr", sz_scr)
        xbd = K.dram("xbc_scr", xbc_scr)
        dtd = K.dram("dt_scr", dt_scr)
        xv = x_d.rearrange("(t a p) d -> t p a d", a=4, p=128)
        for ts in range(8):
            xt = xt_ring.next()
            K.dma(K.sp, xt.ap, xv[ts], writes=[xt])
            rms_to_xT(K, nc, C, xt, 4, ss, lnv, rstd, junk, xh, ptr, [(xT, C.g_m)])
            tok = slice(ts * 512, (ts + 1) * 512)
            for ft in range(40):
                pm = pmm.next()
                for c in range(8):
                    K.op(K.pe, lambda: nc.tensor.matmul(pm.ap, W[c].ap[:, ft * 128:(ft + 1) * 128], xT.ap[:, c, :],
                                                        start=(c == 0), stop=(c == 7)), reads=[W[c], xT], writes=[pm])
                ost = ost_ring.next()
                if ft < 16:
                    K.op(K.act, lambda: nc.scalar.activation(ost.ap, pm.ap, AF.Silu), reads=[pm], writes=[ost])
                    K.dma(K.sp, sz_scr[ft * 128:(ft + 1) * 128, tok], ost.ap, reads=[ost], writes=[szd])
                else:
                    ci = ft - 16
                    raw = raw_ring.next()
                    acc = acc_ring.next()
                    K.op(K.pool, lambda: nc.gpsimd.tensor_copy(raw.ap[:, 0:3], halos[ci].ap), reads=[halos[ci]], writes=[raw])
                    K.op(K.act, lambda: nc.scalar.copy(raw.ap[:, 3:515], pm.ap), reads=[pm], writes=[raw])
                    K.op(K.pool, lambda: nc.gpsimd.tensor_copy(halos[ci].ap, raw.ap[:, 512:515]), reads=[raw], writes=[halos[ci]])
                    cw = lambda j: C.conv_w[:, j * 24 + ci:j * 24 + ci + 1]
                    K.op(K.dve, lambda: nc.vector.tensor_scalar(acc.ap, raw.ap[:, 0:512], cw(0), None, op0=ALU.mult),
                         reads=[raw, C.vecb], writes=[acc])
                    for j in range(1, 4):
                        K.op(K.dve, lambda: nc.vector.scalar_tensor_tensor(acc.ap, raw.ap[:, j:j + 512], cw(j), acc.ap,
                                                                           op0=ALU.mult, op1=ALU.add),
                             reads=[raw, acc, C.vecb], writes=[acc])
                    K.op(K.act, lambda: nc.scalar.activation(ost.ap, acc.ap, AF.Silu, bias=C.conv_b[:, ci:ci + 1]),
                         reads=[acc, C.veca], writes=[ost])
                    K.dma(K.sp, xbc_scr[ci * 128:(ci + 1) * 128, tok], ost.ap, reads=[ost], writes=[xbd])
            for a in range(4):
                for c in range(8):
                    K.op(K.pe, lambda: nc.tensor.matmul(pdt.ap[:, a * 32:(a + 1) * 32], xT.ap[:, c, a * 128:(a + 1) * 128],
                                                        W[c].ap[:, 5120:5152], start=(c == 0), stop=(c == 7)),
                         reads=[W[c], xT], writes=[pdt])
            pv = pdt.ap[:, 0:128].rearrange("p (a h) -> p a h", a=4)
            K.op(K.dve, lambda: nc.vector.tensor_tensor(dtx.ap, pv, C.bc.ap[:, 0:1, :].to_broadcast([128, 4, 32]), op=ALU.add),
                 reads=[pdt, C.bc], writes=[dtx])
            K.op(K.act, lambda: nc.scalar.activation(dta.ap, dtx.ap, AF.Abs), reads=[dtx], writes=[dta])
            K.op(K.act, lambda: nc.scalar.activation(dte.ap, dta.ap, AF.Exp, scale=-1.0), reads=[dta], writes=[dte])
            K.op(K.act, lambda: nc.scalar.activation(dte.ap, dte.ap, AF.Ln, bias=C.one.ap, scale=1.0), reads=[dte, C.one], writes=[dte])
            K.op(K.dve, lambda: nc.vector.tensor_scalar(dtm.ap, dtx.ap, 0.0, None, op0=ALU.max), reads=[dtx], writes=[dtm])
            K.op(K.dve, lambda: nc.vector.tensor_tensor(dto.ap, dtm.ap, dte.ap, op=ALU.add), reads=[dtm, dte], writes=[dto])
            K.dma(K.sp, dt_scr.rearrange("(t a p) h -> t p a h", a=4, p=128)[ts], dto.ap, reads=[dto], writes=[dtd])
        K.end_phase()
        K.stack = K.es


def phase_p1b(K, nc, C, sz_scr, xbc_scr, dt_scr, yn_scr, nchunks=32):
    with ExitStack() as es:
        K.stack = es
        K.begin_phase()
        xb_ring = Ring([K.sb("p1b_xb%d" % i, [128, 24, 128], BF16) for i in range(2)])
        sz_ring = Ring([K.sb("p1b_sz%d" % i, [128, 16, 128], BF16) for i in range(2)])
        dt_ring = Ring([K.sb("p1b_dt%d" % i, [128, 32], F32) for i in range(2)])
        a_ch = K.sb("p1b_a", [128, 32], F32)
        acum = K.sb("p1b_acum", [128, 32], F32)
        tmpw = K.sb("p1b_tmpw", [128, 32], F32)
        wend = K.sb("p1b_wend", [128, 32], F32)
        dtw = K.sb("p1b_dtw", [128, 32], F32)
        dA = K.sb("p1b_dA", [128, 32], F32)
        xdt_pad = K.sb("p1b_xdtp", [128, 32, 128], BF16)
        xdtw = K.sb("p1b_xdtw", [128, 32, 64], BF16)
        btok = K.sb("p1b_btok", [128, 4, 128], BF16)
        state = K.sb("p1b_state", [128, 32, 64], F32)
        st_pad = K.sb("p1b_stp", [128, 32, 128], BF16)
        cbm = K.sb("p1b_cbm", [128, 4, 128], F32)
        seg_ring = Ring([K.sb("p1b_seg%d" % i, [128, 4, 128], F32) for i in range(2)])
        dec_ring = Ring([K.sb("p1b_dec%d" % i, [128, 4, 128], F32) for i in range(2)])
        ea_ring = Ring([K.sb("p1b_ea%d" % i, [128, 4, 128], F32) for i in range(2)])
        mt_ring = Ring([K.sb("p1b_mt%d" % i, [128, 4, 128], BF16) for i in range(3)])
        cs_ring = Ring([K.sb("p1b_cs%d" % i, [128, 4, 128], BF16) for i in range(3)])
        ytmp_ring = Ring([K.sb("p1b_yt%d" % i, [128, 128], F32) for i in range(2)])
        ygate = K.sb("p1b_yg", [128, 16, 128], F32)
        sq = K.sb("p1b_sq", [128, 16, 128], BF16)
        grs = K.sb("p1b_grs", [128, 4, 128], F32)
        yn_ring = Ring([K.sb("p1b_yn%d" % i, [128, 16, 128], BF16) for i in range(2)])
        pxs = K.ps("p1b_pxs", [128, 2048], BF16)
        pb = K.ps("p1b_pb", [128, 1024], BF16)
        psm = K.ps("p1b_psm", [128, 512], F32)
        pabc = Ring([K.ps("p1b_pabc%d" % i, [128, 512], F32) for i in range(2)])
        py_ring = Ring([K.ps("p1b_py%d" % i, [128, 512], F32) for i in range(2)])
        pupd = psm
        for t_ in (xdt_pad, st_pad, state):
            K.op(K.pool, lambda: nc.gpsimd.memset(t_.ap, 0.0), writes=[t_])
        szd = K.dram("sz_scr", sz_scr)
        xbd = K.dram("xbc_scr", xbc_scr)
        dtd = K.dram("dt_scr", dt_scr)
        ynd = K.dram("yn_scr", yn_scr)
        xbv = xbc_scr.rearrange("(q p) t -> p q t", p=128)
        szv = sz_scr.rearrange("(q p) t -> p q t", p=128)
        ynv = yn_scr.rearrange("(q p) t -> p q t", p=128)
        xdt4 = xdt_pad.ap.rearrange("p (q r) c -> p q r c", r=2)
        stp4 = st_pad.ap.rearrange("p (q r) c -> p q r c", r=2)
        for ch in range(nchunks):
            cols = slice(ch * 128, (ch + 1) * 128)
            xb = xb_ring.next()
            sz = sz_ring.next()
            dt = dt_ring.next()
            K.dma(K.sp, xb.ap, xbv[:, :, cols], reads=[xbd], writes=[xb])
            K.dma(K.sp, sz.ap, szv[:, :, cols], reads=[szd], writes=[sz])
            K.dma(K.sp, dt.ap, dt_scr[ch * 128:(ch + 1) * 128, :], reads=[dtd], writes=[dt])
            if _DBG_STOP <= 1:
                continue
            K.op(K.dve, lambda: nc.vector.tensor_tensor(a_ch.ap, dt.ap, C.A.ap, op=ALU.mult), reads=[dt, C.A], writes=[a_ch])
            K.op(K.pe, lambda: nc.tensor.matmul(psm.ap[:, 0:32], C.tri.ap, a_ch.ap, start=True, stop=True),
                 reads=[C.tri, a_ch], writes=[psm])
            K.op(K.pe, lambda: nc.tensor.matmul(psm.ap[:, 32:64], C.onesf.ap, a_ch.ap, start=True, stop=True),
                 reads=[C.onesf, a_ch], writes=[psm])
            K.op(K.act, lambda: nc.scalar.copy(acum.ap, psm.ap[:, 0:32]), reads=[psm], writes=[acum])
            K.op(K.dve, lambda: nc.vector.tensor_tensor(tmpw.ap, psm.ap[:, 32:64], acum.ap, op=ALU.subtract),
                 reads=[psm, acum], writes=[tmpw])
            K.op(K.act, lambda: nc.scalar.activation(wend.ap, tmpw.ap, AF.Exp), reads=[tmpw], writes=[wend])
            K.op(K.act, lambda: nc.scalar.activation(dA.ap, psm.ap[:, 32:64], AF.Exp), reads=[psm], writes=[dA])
            K.op(K.dve, lambda: nc.vector.tensor_tensor(dtw.ap, dt.ap, wend.ap, op=ALU.mult), reads=[dt, wend], writes=[dtw])
            if _DBG_STOP <= 2:
                continue
            for ci in range(16):
                K.op(K.pe, lambda: nc.tensor.transpose(pxs.ap[:, ci * 128:(ci + 1) * 128], xb.ap[:, ci, :], C.identb.ap),
                     reads=[xb, C.identb], writes=[pxs])
            for g in range(4):
                K.op(K.pe, lambda: nc.tensor.transpose(pb.ap[:, g * 128:(g + 1) * 128], xb.ap[:, 16 + g, :], C.identb.ap),
                     reads=[xb, C.identb], writes=[pb])
            pxs4 = pxs.ap.rearrange("p (q r c) -> p q r c", r=2, c=64)
            dt3 = dt.ap.rearrange("p (q r) -> p q r", r=2)
            for r in range(2):
                K.op(K.dve, lambda: nc.vector.tensor_tensor(xdt4[:, :, r, r * 64:(r + 1) * 64], pxs4[:, :, r, :],
                                                            dt3[:, :, r:r + 1].to_broadcast([128, 16, 64]), op=ALU.mult),
                     reads=[pxs, dt], writes=[xdt_pad])
            K.op(K.dve, lambda: nc.vector.tensor_tensor(xdtw.ap, pxs.ap.rearrange("p (h c) -> p h c", c=64),
                                                        dtw.ap.unsqueeze(2).to_broadcast([128, 32, 64]), op=ALU.mult),
                 reads=[pxs, dtw], writes=[xdtw])
            K.op(K.act, lambda: nc.scalar.copy(btok.ap.rearrange("p g n -> p (g n)"), pb.ap[:, 0:512]), reads=[pb], writes=[btok])
            if _DBG_STOP <= 3:
                continue
            for g in range(4):
                K.op(K.pe, lambda: nc.tensor.matmul(pupd.ap[:, g * 128:(g + 1) * 128],
                                                    xb.ap[:, 16 + g, :], xb.ap[:, 20 + g, :], start=True, stop=True),
                     reads=[xb], writes=[pupd])
            for g in range(4):
                K.op(K.dve, lambda: nc.vector.tensor_tensor(cbm.ap[:, g, :], pupd.ap[:, g * 128:(g + 1) * 128], C.tri.ap, op=ALU.mult),
                     reads=[pupd, C.tri], writes=[cbm])
            if _DBG_STOP <= 4:
                continue
            mts = {}
            css = {}
            pas = {}

            def emit_abc(hq):
                pa = pabc.next()
                for i in range(4):
                    h = hq * 4 + i
                    K.op(K.pe, lambda: nc.tensor.matmul(pa.ap[:, i * 128:(i + 1) * 128], a_ch.ap[:, h:h + 1].to_broadcast([128, 128]),
                                                        C.tri.ap, start=True, stop=True), reads=[a_ch, C.tri], writes=[pa])
                pas[hq] = pa

            emit_abc(0)
            for hq in range(8):
                g = hq // 2
                pa = pas[hq]
                seg = seg_ring.next()
                dec = dec_ring.next()
                ea = ea_ring.next()
                mt = mt_ring.next()
                cs = cs_ring.next()
                for i in range(4):
                    h = hq * 4 + i
                    K.op(K.dve, lambda: nc.vector.tensor_scalar(seg.ap[:, i, :], pa.ap[:, i * 128:(i + 1) * 128], acum.ap[:, h:h + 1], 0.0,
                                                                op0=ALU.subtract, op1=ALU.min), reads=[pa, acum], writes=[seg])
                K.op(K.act, lambda: nc.scalar.activation(dec.ap, seg.ap, AF.Exp), reads=[seg], writes=[dec])
                K.op(K.act, lambda: nc.scalar.activation(ea.ap.rearrange("p i l -> p (i l)"), pa.ap, AF.Exp), reads=[pa], writes=[ea])
                for i in range(4):
                    K.op(K.pool, lambda: nc.gpsimd.tensor_tensor(mt.ap[:, i, :], dec.ap[:, i, :], cbm.ap[:, g, :], op=ALU.mult),
                         reads=[dec, cbm], writes=[mt])
                    K.op(K.pool, lambda: nc.gpsimd.tensor_tensor(cs.ap[:, i, :], ea.ap[:, i, :], xb.ap[:, 20 + g, :], op=ALU.mult),
                         reads=[ea, xb], writes=[cs])
                if hq + 1 < 8:
                    emit_abc(hq + 1)
                for pi in range(2):
                    q = hq * 2 + pi
                    pyq = py_ring.next()
                    yo = pyq.ap[:, 0:128]
                    ops = [(xdt_pad, xdt_pad.ap[:, 2 * q, :], mt, mt.ap[:, 2 * pi, :]),
                           (xdt_pad, xdt_pad.ap[:, 2 * q + 1, :], mt, mt.ap[:, 2 * pi + 1, :]),
                           (st_pad, st_pad.ap[:, 2 * q, :], cs, cs.ap[:, 2 * pi, :]),
                           (st_pad, st_pad.ap[:, 2 * q + 1, :], cs, cs.ap[:, 2 * pi + 1, :])]
                    for k_, (lt, la, rt, ra) in enumerate(ops):
                        K.op(K.pe, lambda: nc.tensor.matmul(yo, la, ra, start=(k_ == 0), stop=(k_ == 3)), reads=[lt, rt], writes=[pyq])
                    yt = ytmp_ring.next()
                    K.op(K.dve, lambda: nc.vector.scalar_tensor_tensor(yt.ap, xb.ap[:, q, :], C.Dcol.ap[:, q:q + 1], yo,
                                                                       op0=ALU.mult, op1=ALU.add), reads=[xb, C.Dcol, pyq], writes=[yt])
                    K.op(K.pool, lambda: nc.gpsimd.tensor_tensor(ygate.ap[:, q, :], yt.ap, sz.ap[:, q, :], op=ALU.mult),
                         reads=[yt, sz], writes=[ygate])
            if _DBG_STOP <= 5:
                continue
            for g in range(4):
                K.op(K.pe, lambda: nc.tensor.matmul(pupd.ap, btok.ap[:, g, :], xdtw.ap[:, g * 8:(g + 1) * 8, :].rearrange("p h c -> p (h c)"),
                                                    start=True, stop=True), reads=[btok, xdtw], writes=[pupd])
                sv = state.ap[:, g * 8:(g + 1) * 8, :]
                K.op(K.dve, lambda: nc.vector.tensor_tensor(sv, sv, dA.ap[:, g * 8:(g + 1) * 8].unsqueeze(2).to_broadcast([128, 8, 64]),
                                                            op=ALU.mult), reads=[state, dA], writes=[state])
                K.op(K.dve, lambda: nc.vector.tensor_tensor(sv, sv, pupd.ap.rearrange("p (h c) -> p h c", c=64), op=ALU.add),
                     reads=[state, pupd], writes=[state])
            st4 = state.ap.rearrange("p (q r) c -> p q r c", r=2)
            for r in range(2):
                K.op(K.act, lambda: nc.scalar.copy(stp4[:, :, r, r * 64:(r + 1) * 64], st4[:, :, r, :]), reads=[state], writes=[st_pad])
            if _DBG_STOP <= 6:
                continue
            K.op(K.act, lambda: nc.scalar.activation(sq.ap, ygate.ap, AF.Square), reads=[ygate], writes=[sq])
            for G in range(4):
                for k_ in range(4):
                    K.op(K.pe, lambda: nc.tensor.matmul(psm.ap[:, G * 128:(G + 1) * 128], C.onesb.ap, sq.ap[:, G * 4 + k_, :],
                                                        start=(k_ == 0), stop=(k_ == 3)), reads=[C.onesb, sq], writes=[psm])
            K.op(K.act, lambda: nc.scalar.activation(grs.ap.rearrange("p g l -> p (g l)"), psm.ap, AF.Ln, bias=C.eps.ap, scale=1.0 / 512),
                 reads=[psm, C.eps], writes=[grs])
            K.op(K.act, lambda: nc.scalar.activation(grs.ap, grs.ap, AF.Exp, scale=-0.5), reads=[grs], writes=[grs])
            yn = yn_ring.next()
            for q in range(16):
                K.op(K.dve, lambda: nc.vector.scalar_tensor_tensor(yn.ap[:, q, :], ygate.ap[:, q, :], C.g_y[:, q:q + 1], grs.ap[:, q // 4, :],
                                                                   op0=ALU.mult, op1=ALU.mult), reads=[ygate, C.veca, grs], writes=[yn])
            K.dma(K.sp, ynv[:, :, cols], yn.ap, reads=[yn], writes=[ynd])
        K.end_phase()
        K.stack = K.es


def phase_out_ple(K, nc, C, name, a_scr, FC, w_o, h_in, p_in, g_ple, w_gate, w_proj, h_out):
    with ExitStack() as es:
        K.stack = es
        K.begin_phase()
        stage = Ring([K.sb(name + "_st%d" % i, [128, 1024], F32) for i in range(3)])
        Wo = load_weight(K, nc, es, name + "_wo", w_o, FC, 1024, stage, [K.pool, K.dve, K.act])
        Wg = load_weight(K, nc, es, name + "_wg", w_gate, 8, 1024, stage, [K.pool, K.dve, K.act])
        Wp = load_weight(K, nc, es, name + "_wp", w_proj, 2, 1024, stage, [K.pool, K.dve, K.act])
        at_ring = Ring([K.sb(name + "_at%d" % i, [128, FC, 512], BF16) for i in range(1)])
        h_ring = Ring([K.sb(name + "_h%d" % i, [128, 4, 1024], F32) for i in range(1)])
        p_ring = Ring([K.sb(name + "_p%d" % i, [128, 4, 256], F32) for i in range(2)])
        pbf = K.sb(name + "_pbf", [128, 4, 256], BF16)
        pT = K.sb(name + "_pT", [128, 2, 512], BF16)
        ss = K.sb(name + "_ss", [128, 4], F32)
        lnv = K.sb(name + "_ln", [128, 4], F32)
        rstd = K.sb(name + "_rstd", [128, 4], F32)
        junk = K.sb(name + "_junk", [128, 1024], BF16)
        xh = K.sb(name + "_xh", [128, 4, 1024], BF16)
        xT = K.sb(name + "_xT", [128, 8, 512], BF16)
        sig_ring = Ring([K.sb(name + "_sig%d" % i, [128, 512], F32) for i in range(2)])
        tmp_ring = Ring([K.sb(name + "_tmp%d" % i, [128, 512], F32) for i in range(2)])
        ho_ring = Ring([K.sb(name + "_ho%d" % i, [128, 4, 1024], F32) for i in range(1)])
        ptr = Ring([K.ps(name + "_pt%d" % i, [128, 1024], BF16) for i in range(2)])
        pmm = Ring([K.ps(name + "_pm%d" % i, [128, 512], F32) for i in range(2)])
        pg_ring = Ring([K.ps(name + "_pg%d" % i, [128, 512], F32) for i in range(2)])
        pp_ring = Ring([K.ps(name + "_pp%d" % i, [128, 512], F32) for i in range(2)])
        ad = K.dram(name + "_a", a_scr)
        hd = K.dram(name + "_hin", h_in)
        od = K.dram(name + "_hout", h_out)
        av = a_scr.rearrange("(c p) t -> p c t", p=128)
        hv = h_in.rearrange("(t a p) d -> t p a d", a=4, p=128)
        pv = p_in.rearrange("(t a p) d -> t p a d", a=4, p=128)
        ov = h_out.rearrange("(t a p) d -> t p a d", a=4, p=128)
        for ts in range(8):
            at = at_ring.next()
            ht = h_ring.next()
            pt_ = p_ring.next()
            K.dma(K.sp, at.ap, av[:, :, ts * 512:(ts + 1) * 512], reads=[ad], writes=[at])
            K.dma(K.sp, ht.ap, hv[ts], reads=[hd], writes=[ht])
            K.dma(K.sp, pt_.ap, pv[ts], writes=[pt_])
            for a in range(4):
                for half in range(2):
                    pm = pmm.next()
                    for c in range(FC):
                        K.op(K.pe, lambda: nc.tensor.matmul(pm.ap, at.ap[:, c, a * 128:(a + 1) * 128], Wo[c].ap[:, half * 512:(half + 1) * 512],
                                                            start=(c == 0), stop=(c == FC - 1)), reads=[at, Wo[c]], writes=[pm])
                    hs = ht.ap[:, a, half * 512:(half + 1) * 512]
                    K.op(K.dve, lambda: nc.vector.tensor_tensor(hs, hs, pm.ap, op=ALU.add), reads=[ht, pm], writes=[ht])
            rms_to_xT(K, nc, C, ht, 4, ss, lnv, rstd, junk, xh, ptr, [(xT, g_ple)])
            K.op(K.pool, lambda: nc.gpsimd.tensor_copy(pbf.ap, pt_.ap), reads=[pt_], writes=[pbf])
            for c2 in range(2):
                pt = ptr.next()
                for a in range(4):
                    K.op(K.pe, lambda: nc.tensor.transpose(pt.ap[:, a * 128:(a + 1) * 128], pbf.ap[:, a, c2 * 128:(c2 + 1) * 128], C.identb.ap),
                         reads=[pbf, C.identb], writes=[pt])
                K.op(K.act, lambda: nc.scalar.copy(pT.ap[:, c2, :], pt.ap[:, 0:512]), reads=[pt], writes=[pT])
            ho = ho_ring.next()
            for a in range(4):
                for half in range(2):
                    pg = pg_ring.next()
                    pp = pp_ring.next()
                    cs_ = slice(half * 512, (half + 1) * 512)
                    for c in range(8):
                        K.op(K.pe, lambda: nc.tensor.matmul(pg.ap, xT.ap[:, c, a * 128:(a + 1) * 128], Wg[c].ap[:, cs_],
                                                            start=(c == 0), stop=(c == 7)), reads=[xT, Wg[c]], writes=[pg])
                    for c in range(2):
                        K.op(K.pe, lambda: nc.tensor.matmul(pp.ap, pT.ap[:, c, a * 128:(a + 1) * 128], Wp[c].ap[:, cs_],
                                                            start=(c == 0), stop=(c == 1)), reads=[pT, Wp[c]], writes=[pp])
                    sg = sig_ring.next()
                    tm = tmp_ring.next()
                    K.op(K.act, lambda: nc.scalar.activation(sg.ap, pg.ap, AF.Sigmoid), reads=[pg], writes=[sg])
                    K.op(K.dve, lambda: nc.vector.tensor_tensor(tm.ap, pp.ap, sg.ap, op=ALU.mult), reads=[pp, sg], writes=[tm])
                    K.op(K.pool, lambda: nc.gpsimd.tensor_tensor(ho.ap[:, a, cs_], tm.ap, ht.ap[:, a, cs_], op=ALU.add),
                         reads=[tm, ht], writes=[ho])
            K.dma(K.sp, ov[ts], ho.ap, reads=[ho], writes=[od])
        K.end_phase()
        K.stack = K.es


def phase_p3(K, nc, C, h1, w_kv, s_in, kt_scr, v_scr, q_scr, sg_scr):
    with ExitStack() as es:
        K.stack = es
        K.begin_phase()
        stage = Ring([K.sb("p3_st%d" % i, [128, 2048], F32) for i in range(2)])
        Wkv = load_weight(K, nc, es, "p3_wkv", w_kv, 8, 2048, stage, [K.pool, K.dve, K.act])
        Wqg = load_weight(K, nc, es, "p3_wqg", s_in, 8, 2048, stage, [K.pool, K.dve, K.act])
        h_ring = Ring([K.sb("p3_h%d" % i, [128, 4, 1024], F32) for i in range(2)])
        ss = K.sb("p3_ss", [128, 4], F32)
        lnv = K.sb("p3_ln", [128, 4], F32)
        rstd = K.sb("p3_rstd", [128, 4], F32)
        junk = K.sb("p3_junk", [128, 1024], BF16)
        xh = K.sb("p3_xh", [128, 4, 1024], BF16)
        xTk = K.sb("p3_xTk", [128, 8, 512], BF16)
        xTq = K.sb("p3_xTq", [128, 8, 512], BF16)
        sq_ring = Ring([K.sb("p3_sq%d" % i, [128, 512], BF16) for i in range(2)])
        rs_ring = Ring([K.sb("p3_rs%d" % i, [128, 512], F32) for i in range(2)])
        kst = K.sb("p3_kst", [128, 8, 512], BF16)
        vst = K.sb("p3_vst", [128, 4, 1024], BF16)
        qst = K.sb("p3_qst", [128, 8, 512], BF16)
        gst = K.sb("p3_gst", [128, 8, 512], BF16)
        ptr = Ring([K.ps("p3_pt%d" % i, [128, 1024], BF16) for i in range(2)])
        pmm = Ring([K.ps("p3_pm%d" % i, [128, 512], F32) for i in range(3)])
        pn_ring = Ring([K.ps("p3_pn%d" % i, [128, 512], F32) for i in range(2)])
        hd = K.dram("p3_h1", h1)
        kd = K.dram("kt_scr", kt_scr)
        vd = K.dram("v_scr", v_scr)
        qd = K.dram("q_scr", q_scr)
        gd = K.dram("sg_scr", sg_scr)
        hv = h1.rearrange("(t a p) d -> t p a d", a=4, p=128)
        vv = v_scr.rearrange("(t a p) d -> t p a d", a=4, p=128)

        def headnorm(pm, gcol, out_ap, out_t):
            sq = sq_ring.next()
            rs = rs_ring.next()
            pn = pn_ring.next()
            K.op(K.act, lambda: nc.scalar.activation(sq.ap, pm.ap, AF.Square), reads=[pm], writes=[sq])
            K.op(K.pe, lambda: nc.tensor.matmul(pn.ap, C.blk.ap, sq.ap, start=True, stop=True), reads=[C.blk, sq], writes=[pn])
            K.op(K.act, lambda: nc.scalar.activation(rs.ap, pn.ap, AF.Ln, bias=C.eps.ap, scale=1.0 / 64), reads=[pn, C.eps], writes=[rs])
            K.op(K.act, lambda: nc.scalar.activation(rs.ap, rs.ap, AF.Exp, scale=-0.5), reads=[rs], writes=[rs])
            K.op(K.dve, lambda: nc.vector.scalar_tensor_tensor(out_ap, pm.ap, gcol, rs.ap, op0=ALU.mult, op1=ALU.mult),
                 reads=[pm, C.kq, rs], writes=[out_t])

        for ts in range(8):
            ht = h_ring.next()
            K.dma(K.sp, ht.ap, hv[ts], reads=[hd], writes=[ht])
            rms_to_xT(K, nc, C, ht, 4, ss, lnv, rstd, junk, xh, ptr, [(xTk, C.g_kv), (xTq, C.g_s)])
            tok = slice(ts * 512, (ts + 1) * 512)
            for fo in range(8):
                pm = pmm.next()
                for c in range(8):
                    K.op(K.pe, lambda: nc.tensor.matmul(pm.ap, Wkv[c].ap[:, fo * 128:(fo + 1) * 128], xTk.ap[:, c, :],
                                                        start=(c == 0), stop=(c == 7)), reads=[Wkv[c], xTk], writes=[pm])
                headnorm(pm, C.kq.ap[:, 0:1], kst.ap[:, fo, :], kst)
            for a in range(4):
                for half in range(2):
                    pm = pmm.next()
                    for c in range(8):
                        K.op(K.pe, lambda: nc.tensor.matmul(pm.ap, xTk.ap[:, c, a * 128:(a + 1) * 128],
                                                            Wkv[c].ap[:, 1024 + half * 512:1024 + (half + 1) * 512],
                                                            start=(c == 0), stop=(c == 7)), reads=[Wkv[c], xTk], writes=[pm])
                    K.op(K.act, lambda: nc.scalar.copy(vst.ap[:, a, half * 512:(half + 1) * 512], pm.ap), reads=[pm], writes=[vst])
            for fo in range(8):
                pm = pmm.next()
                for c in range(8):
                    K.op(K.pe, lambda: nc.tensor.matmul(pm.ap, Wqg[c].ap[:, fo * 128:(fo + 1) * 128], xTq.ap[:, c, :],
                                                        start=(c == 0), stop=(c == 7)), reads=[Wqg[c], xTq], writes=[pm])
                headnorm(pm, C.kq.ap[:, 1:2], qst.ap[:, fo, :], qst)
            for fo in range(8):
                pm = pmm.next()
                for c in range(8):
                    K.op(K.pe, lambda: nc.tensor.matmul(pm.ap, Wqg[c].ap[:, 1024 + fo * 128:1024 + (fo + 1) * 128], xTq.ap[:, c, :],
                                                        start=(c == 0), stop=(c == 7)), reads=[Wqg[c], xTq], writes=[pm])
                K.op(K.act, lambda: nc.scalar.activation(gst.ap[:, fo, :], pm.ap, AF.Silu), reads=[pm], writes=[gst])
            for hf in range(2):
                qs = slice(hf * 4, (hf + 1) * 4)
                K.dma(K.sp, kt_scr.rearrange("(q p) t -> p q t", p=128)[:, qs, tok], kst.ap[:, qs, :], reads=[kst], writes=[kd])
                K.dma(K.sp, q_scr.rearrange("(q p) t -> p q t", p=128)[:, qs, tok], qst.ap[:, qs, :], reads=[qst], writes=[qd])
                K.dma(K.sp, sg_scr.rearrange("(q p) t -> p q t", p=128)[:, qs, tok], gst.ap[:, qs, :], reads=[gst], writes=[gd])
            K.dma(K.sp, vv[ts], vst.ap, reads=[vst], writes=[vd])
        K.end_phase()
        K.stack = K.es


def phase_p4(K, nc, C, kt_scr, v_scr, q_scr, sg_scr, og_scr, nTB=8, nQ=8):
    with ExitStack() as es:
        K.stack = es
        K.begin_phase()
        ntri = K.sb("p4_ntri", [128, 128], BF16)
        ebig = K.sb("p4_ebig", [128, 255], BF16)
        nsel = K.sb("p4_nsel", [128, 32, 128], BF16)
        masks = K.sb("p4_masks", [128, 4, 512], BF16)
        with ExitStack() as est:
            K.stack = est
            negb = K.sb("p4_negb", [128, 128], BF16)
            negb3 = K.sb("p4_negb3", [128, 32, 128], BF16)
            oneb3 = K.sb("p4_oneb3", [128, 4, 512], BF16)
            K.op(K.pool, lambda: nc.gpsimd.memset(negb.ap, -1.0), writes=[negb])
            K.op(K.pool, lambda: nc.gpsimd.affine_select(ntri.ap, negb.ap, pattern=[[-1, 128]], compare_op=ALU.is_ge, fill=0.0,
                                                         base=0, channel_multiplier=1), reads=[negb], writes=[ntri])
            K.op(K.pool, lambda: nc.gpsimd.memset(ebig.ap, 0.0), writes=[ebig])
            K.op(K.pool, lambda: nc.gpsimd.memset(ebig.ap[:, 127:128], 1.0), writes=[ebig])
            K.op(K.pool, lambda: nc.gpsimd.memset(negb3.ap, -1.0), writes=[negb3])
            K.op(K.pool, lambda: nc.gpsimd.affine_select(nsel.ap, negb3.ap, pattern=[[-1, 32], [0, 128]], compare_op=ALU.is_ge, fill=0.0,
                                                         base=-1, channel_multiplier=1), reads=[negb3], writes=[nsel])
            K.op(K.pool, lambda: nc.gpsimd.memset(oneb3.ap, 1.0), writes=[oneb3])
            K.op(K.pool, lambda: nc.gpsimd.affine_select(masks.ap, oneb3.ap, pattern=[[-128, 4], [1, 512]], compare_op=ALU.is_gt, fill=0.0,
                                                         base=0, channel_multiplier=-1), reads=[oneb3], writes=[masks])
            K.barrier()
            K.stack = es
        qpad = [Ring([K.sb("p4_qp%d_%d" % (r, i), [128, 512], BF16) for i in range(2)]) for r in range(2)]
        for r in range(2):
            for t_ in qpad[r].t:
                K.op(K.pool, lambda: nc.gpsimd.memset(t_.ap, 0.0), writes=[t_])
        kt_ring = Ring([K.sb("p4_kt%d" % i, [128, S], BF16) for i in range(2)])
        v_ring = Ring([K.sb("p4_v%d" % i, [128, 32, 128], BF16) for i in range(2)])
        sgt_ring = Ring([K.sb("p4_sg%d" % i, [128, 512], BF16) for i in range(2)])
        ogst_ring = Ring([K.sb("p4_og%d" % i, [128, 512], BF16) for i in range(2)])
        SPs = [K.sb("p4_sp%d" % i, [128, 32, 512], BF16) for i in range(2)]
        SPvs = [[K.view("p4_sp%d_%d" % (b, i), SPs[b].ap[:, i, :]) for i in range(32)] for b in range(2)]
        e_ring = Ring([K.sb("p4_e%d" % i, [128, 512], F32) for i in range(3)])
        spt_ring = Ring([K.sb("p4_spt%d" % i, [128, 512], F32) for i in range(2)])
        csb_ring = Ring([K.sb("p4_csb%d" % i, [128, 512], BF16) for i in range(2)])
        wt_ring = Ring([K.sb("p4_wt%d" % i, [128, 512], BF16) for i in range(4)])
        wm_ring = Ring([K.sb("p4_wm%d" % i, [128, 512], BF16) for i in range(2)])
        pz = Ring([K.ps("p4_pz%d" % i, [128, 512], F32) for i in range(3)])
        pcs = K.ps("p4_pcs", [128, 512], F32)
        pgr = Ring([K.ps("p4_pg%d" % i, [128, 512], F32) for i in range(3)])
        po_ring = Ring([K.ps("p4_po%d" % i, [128, 512], F32) for i in range(1)])
        kd = K.dram("kt_scr", kt_scr)
        vd = K.dram("v_scr", v_scr)
        qd = K.dram("q_scr", q_scr)
        gd = K.dram("sg_scr", sg_scr)
        od = K.dram("og_scr", og_scr)
        vv = v_scr.rearrange("(jb p) (q c) -> q p jb c", p=128, c=128)
        heads = [(q, TB, r) for q in range(nQ) for TB in range(nTB) for r in range(2)]
        groups = {}
        qstate = {}
        hstate = {}

        def prepare(i):
            q, TB, r = heads[i]
            if q not in qstate:
                KT = kt_ring.next()
                V = v_ring.next()
                K.dma(K.sp, KT.ap, kt_scr[q * 128:(q + 1) * 128, :], reads=[kd], writes=[KT])
                K.dma(K.sp, V.ap, vv[q], reads=[vd], writes=[V])
                qstate[q] = (KT, V)
            if (q, TB) not in groups:
                tok = slice(TB * 512, (TB + 1) * 512)
                sgt = sgt_ring.next()
                K.dma(K.sp, sgt.ap, sg_scr[q * 128:(q + 1) * 128, tok], reads=[gd], writes=[sgt])
                ogst = ogst_ring.next()
                qps = []
                for r_ in range(2):
                    qp = qpad[r_].next()
                    K.dma(K.sp, qp.ap[r_ * 64:(r_ + 1) * 64, :], q_scr[q * 128 + r_ * 64:q * 128 + (r_ + 1) * 64, tok],
                          reads=[qd], writes=[qp])
                    qps.append(qp)
                groups[(q, TB)] = (sgt, ogst, qps, tok)
            hstate[i] = {"SPv": SPvs[i % 2]}

        def sweep1(i):
            q, TB, r = heads[i]
            KT, V = qstate[q]
            sgt, ogst, qps, tok = groups[(q, TB)]
            qp = qps[r]
            SPv = hstate[i]["SPv"]
            nkb = 4 * (TB + 1)
            zs, es_ = {}, {}

            def emit_z(jb):
                z = pz.next()
                K.op(K.pe, lambda: nc.tensor.matmul(z.ap, KT.ap[:, jb * 128:(jb + 1) * 128], qp.ap, start=True, stop=True),
                     reads=[KT, qp], writes=[z])
                zs[jb] = z

            def emit_exp(jb):
                e = e_ring.next()
                K.op(K.act, lambda: nc.scalar.activation(e.ap, zs[jb].ap, AF.Exp), reads=[zs[jb]], writes=[e])
                es_[jb] = e

            def emit_ln(jb):
                e = es_[jb]
                rr = jb - 4 * TB
                if rr < 0:
                    K.op(K.act, lambda: nc.scalar.activation(SPv[jb].ap, e.ap, AF.Ln, bias=C.one.ap, scale=1.0),
                         reads=[e, C.one], writes=[SPv[jb]])
                else:
                    spt = spt_ring.next()
                    K.op(K.act, lambda: nc.scalar.activation(spt.ap, e.ap, AF.Ln, bias=C.one.ap, scale=1.0),
                         reads=[e, C.one], writes=[spt])
                    K.op(K.dve, lambda: nc.vector.tensor_tensor(SPv[jb].ap, spt.ap, masks.ap[:, rr, :], op=ALU.mult),
                         reads=[spt, masks], writes=[SPv[jb]])

            def emit_cs(jb):
                K.op(K.pe, lambda: nc.tensor.matmul(pcs.ap, ebig.ap[:, 127 - jb:255 - jb], SPv[jb].ap,
                                                    start=(jb == 0), stop=(jb == nkb - 1)), reads=[ebig, SPv[jb]], writes=[pcs])

            emit_z(0)
            if nkb > 1:
                emit_z(1)
            emit_exp(0)
            for jb in range(nkb):
                if jb + 1 < nkb:
                    emit_exp(jb + 1)
                emit_ln(jb)
                if jb + 2 < nkb:
                    emit_z(jb + 2)
                emit_cs(jb)
            csb = csb_ring.next()
            K.op(K.dve, lambda: nc.vector.tensor_copy(csb.ap, pcs.ap), reads=[pcs], writes=[csb])
            hstate[i]["csb"] = csb

        def sweep2(i):
            q, TB, r = heads[i]
            KT, V = qstate[q]
            sgt, ogst, qps, tok = groups[(q, TB)]
            qp = qps[r]
            SPv = hstate[i]["SPv"]
            csb = hstate[i]["csb"]
            nkb = 4 * (TB + 1)
            rows = slice(r * 64, (r + 1) * 64)
            po = po_ring.next()
            gs = {}

            def emit_G(jb):
                gq = pgr.next()
                K.op(K.pe, lambda: nc.tensor.matmul(gq.ap, ntri.ap, SPv[jb].ap, start=True, stop=False),
                     reads=[ntri, SPv[jb]], writes=[gq])
                K.op(K.pe, lambda: nc.tensor.matmul(gq.ap, KT.ap[:, jb * 128:(jb + 1) * 128], qp.ap, start=False, stop=False),
                     reads=[KT, qp], writes=[gq])
                K.op(K.pe, lambda: nc.tensor.matmul(gq.ap, nsel.ap[:, jb, :], csb.ap, start=False, stop=True),
                     reads=[nsel, csb], writes=[gq])
                gs[jb] = gq

            emit_G(0)
            if nkb > 1:
                emit_G(1)
            for jb in range(nkb):
                gq = gs[jb]
                wt = wt_ring.next()
                K.op(K.act, lambda: nc.scalar.activation(wt.ap, gq.ap, AF.Exp), reads=[gq], writes=[wt])
                rr = jb - 4 * TB
                if rr >= 0:
                    wm = wm_ring.next()
                    K.op(K.dve, lambda: nc.vector.tensor_tensor(wm.ap, wt.ap, masks.ap[:, rr, :], op=ALU.mult),
                         reads=[wt, masks], writes=[wm])
                    wt = wm
                if jb + 2 < nkb:
                    emit_G(jb + 2)
                K.op(K.pe, lambda: nc.tensor.matmul(po.ap, V.ap[:, jb, :], wt.ap, start=(jb == 0), stop=(jb == nkb - 1)),
                     reads=[V, wt], writes=[po])
            K.op(K.dve, lambda: nc.vector.tensor_tensor(ogst.ap[rows, :], po.ap[rows, :], sgt.ap[rows, :], op=ALU.mult),
                 reads=[po, sgt], writes=[ogst])
            if r == 1:
                K.dma(K.sp, og_scr[q * 128:(q + 1) * 128, tok], ogst.ap, reads=[ogst], writes=[od])
            del hstate[i]

        prepare(0)
        sweep1(0)
        for i in range(len(heads)):
            if i + 1 < len(heads):
                prepare(i + 1)
                sweep1(i + 1)
            sweep2(i)
        K.end_phase()
        K.stack = K.es


PARAM_SHAPES = {
    "m_norm": [1, 1024], "m_in": [1, 1024, 5152], "m_conv_w": [1, 4, 3072], "m_conv_b": [1, 3072],
    "m_dt_bias": [1, 32], "m_A_log": [1, 32], "m_D": [1, 32], "m_ynorm": [1, 2048], "m_out": [1, 2048, 1024],
    "kv_norm": [1024], "w_kv": [1024, 2048], "k_norm": [64], "s_norm": [1, 1024], "s_in": [1, 1024, 2048],
    "q_norm": [1, 64], "s_out": [1, 1024, 1024], "ple_norm": [2, 1024], "ple_gate": [2, 1024, 1024],
    "ple_proj": [2, 256, 1024],
}

SCRATCH = {
    "sz_scr": ([2048, S], BF16), "xbc_scr": ([3072, S], BF16), "dt_scr": ([S, 32], F32),
    "yn_scr": ([2048, S], BF16), "h1_scr": ([S, D], F32), "q_scr": ([1024, S], BF16),
    "sg_scr": ([1024, S], BF16), "og_scr": ([1024, S], BF16),
    "kt_scr": ([1024, S], BF16), "v_scr": ([S, 1024], BF16),
}


def build(phases=("p1a", "p1b", "p2", "p3", "p4", "p5"), ext_in=(), ext_out=(), p1b_chunks=32, p4_tb=8, p4_q=8):
    nc = bass.Bass("TRN2", target_bir_lowering=False)
    x = nc.dram_tensor("x", [S, D], F32, kind="ExternalInput").ap()
    p0 = nc.dram_tensor("p0", [S, 256], F32, kind="ExternalInput").ap()
    p1 = nc.dram_tensor("p1", [S, 256], F32, kind="ExternalInput").ap()
    prm = {k: nc.dram_tensor(k, shp, F32, kind="ExternalInput").ap() for k, shp in PARAM_SHAPES.items()}
    out = nc.dram_tensor("out", [S, D], F32, kind="ExternalOutput").ap()
    scr = {}
    for k, (shp, dt_) in SCRATCH.items():
        kind = "ExternalInput" if k in ext_in else ("ExternalOutput" if k in ext_out else "Internal")
        scr[k] = nc.dram_tensor(k, shp, dt_, kind=kind).ap()
    with ExitStack() as es:
        K = Ctx(nc, es)
        C = make_consts(K, nc, prm)
        if "p1a" in phases:
            phase_p1a(K, nc, C, x, prm["m_in"][0], scr["sz_scr"], scr["xbc_scr"], scr["dt_scr"])
        if "p1b" in phases:
            phase_p1b(K, nc, C, scr["sz_scr"], scr["xbc_scr"], scr["dt_scr"], scr["yn_scr"], nchunks=p1b_chunks)
        if "p2" in phases:
            phase_out_ple(K, nc, C, "p2", scr["yn_scr"], 16, prm["m_out"][0], x, p0, C.g_p0,
                          prm["ple_gate"][0], prm["ple_proj"][0], scr["h1_scr"])
        if "p3" in phases:
            phase_p3(K, nc, C, scr["h1_scr"], prm["w_kv"], prm["s_in"][0], scr["kt_scr"], scr["v_scr"], scr["q_scr"], scr["sg_scr"])
        if "p4" in phases:
            phase_p4(K, nc, C, scr["kt_scr"], scr["v_scr"], scr["q_scr"], scr["sg_scr"], scr["og_scr"], nTB=p4_tb, nQ=p4_q)
        if "p5" in phases:
            phase_out_ple(K, nc, C, "p5", scr["og_scr"], 8, prm["s_out"][0], scr["h1_scr"], p1, C.g_p1,
                          prm["ple_gate"][1], prm["ple_proj"][1], out)
        K.finish()
    return nc


_NC_CACHE = {}


def kernel(**inputs):
    if "full" not in _NC_CACHE:
        _NC_CACHE["full"] = build()
    nc = _NC_CACHE["full"]
    x = np.asarray(inputs["x"], dtype=np.float32)
    p = np.asarray(inputs["p"], dtype=np.float32)
    in_maps = []
    for b in range(NCORES):
        m = {"x": np.ascontiguousarray(x[b]), "p0": np.ascontiguousarray(p[0, b]), "p1": np.ascontiguousarray(p[1, b])}
        for k in PARAM_SHAPES:
            m[k] = np.ascontiguousarray(np.asarray(inputs[k], dtype=np.float32))
        in_maps.append(m)
    res = run_bass_kernel_spmd(nc, in_maps, core_ids=list(range(NCORES)))
    return np.stack([np.asarray(res.results[b]["out"], dtype=np.float32) for b in range(NCORES)], axis=0)
```

```python
from bisect import bisect_left
from contextlib import ExitStack
import os
import numpy as np
import concourse.bass as bass
import concourse.mybir as mybir
from concourse.alu_op_type import AluOpType as ALU
from concourse.bass_utils import run_bass_kernel_spmd

F32 = mybir.dt.float32
BF16 = mybir.dt.bfloat16
AF = mybir.ActivationFunctionType

S = 4096
D = 1024
_DBG_STOP = int(os.environ.get('P1B_STOP', '99'))
NCORES = 8
EPS = 1e-6


class Eng:
    def __init__(self, name, eng, sem, eager):
        self.name, self.eng, self.sem, self.eager = name, eng, sem, eager
        self.n = 0
        self.count = 0
        self.sig_idx = []
        self.sig_cnt = []
        self.last = None
        self.last_signaled = True
        self.known = {}

    def signal_last(self):
        if not self.last_signaled:
            self.last.then_inc(self.sem, 1)
            self.count += 1
            self.sig_idx.append(self.n)
            self.sig_cnt.append(self.count)
            self.last_signaled = True

    def count_for(self, idx):
        i = bisect_left(self.sig_idx, idx)
        if i == len(self.sig_idx):
            self.signal_last()
            i = len(self.sig_idx) - 1
        assert self.sig_idx[i] >= idx
        return self.sig_cnt[i]


class DSem:
    def __init__(self, sem):
        self.sem = sem
        self.issued = 0


class T:
    def __init__(self, name, ap, space):
        self.name, self.ap, self.space = name, ap, space
        self.w = None
        self.r = {}
        self.dsem = None

    def __getitem__(self, k):
        return self.ap[k]


class Ring:
    def __init__(self, tiles):
        self.t = tiles
        self.i = 0

    def next(self):
        t = self.t[self.i % len(self.t)]
        self.i += 1
        return t


class Ctx:
    def __init__(self, nc, es):
        self.nc, self.es = nc, es
        mk = lambda n: es.enter_context(nc.semaphore(n))
        self.pe = Eng("pe", nc.tensor, mk("s_pe"), False)
        self.act = Eng("act", nc.scalar, mk("s_act"), True)
        self.dve = Eng("dve", nc.vector, mk("s_dve"), True)
        self.pool = Eng("pool", nc.gpsimd, mk("s_pool"), True)
        self.sp = Eng("sp", nc.sync, mk("s_sp"), True)
        self.engs = [self.pe, self.act, self.dve, self.pool, self.sp]
        self.dsems = []
        self.free_dsems = []
        self.nsem = 0
        self.stack = es

    def sb(self, name, shape, dtype, es=None):
        t = (es or self.stack).enter_context(self.nc.sbuf_tensor(name, shape, dtype))
        return T(name, t.ap(), "sb")

    def ps(self, name, shape, dtype, es=None):
        t = (es or self.stack).enter_context(self.nc.psum_tensor(name, shape, dtype))
        return T(name, t.ap(), "ps")

    def dram(self, name, ap):
        return T(name, ap, "dram")

    def view(self, name, ap, space="sb"):
        return T(name, ap, space)

    def begin_phase(self):
        self.phase_dsems = []

    def end_phase(self):
        self.barrier()
        self.free_dsems.extend(self.phase_dsems)
        self.phase_dsems = None

    def new_dsem(self):
        if self.free_dsems:
            d = self.free_dsems.pop()
        else:
            self.nsem += 1
            d = DSem(self.es.enter_context(self.nc.semaphore("s_d%d" % self.nsem)))
            self.dsems.append(d)
        if getattr(self, "phase_dsems", None) is not None:
            self.phase_dsems.append(d)
        return d

    def _waits(self, E, reads, writes):
        need = {}

        def add(ev, same_ok):
            if ev is None:
                return
            if ev[0] == "e":
                Dn, idx = ev[1], ev[2]
                if Dn is E and same_ok:
                    return
                c = Dn.count_for(idx)
                sem = Dn.sem
            else:
                c = ev[1].issued * 16
                sem = ev[1].sem
            key = id(sem)
            if need.get(key, (None, 0))[1] < c:
                need[key] = (sem, c)

        for t in reads:
            add(t.w, E is self.pe)
        for t in writes:
            add(t.w, True)
            for ev in t.r.values():
                add(ev, True)
        for key, (sem, c) in need.items():
            if E.known.get(key, 0) >= c:
                continue
            E.eng.wait_ge(sem, c)
            E.known[key] = c

    def op(self, E, make, reads=(), writes=()):
        writes = list(writes) + [t for t in reads if t.space == "ps"]
        reads = [t for t in reads if t.space != "ps"]
        self._waits(E, reads, writes)
        inst = make()
        E.n += 1
        E.last = inst
        E.last_signaled = False
        if E.eager:
            E.signal_last()
        ev = ("e", E, E.n)
        for t in reads:
            t.r[id(E)] = ev
        for t in writes:
            t.w = ev
            t.r = {}
        return inst

    def dma(self, Q, out_ap, in_ap, reads=(), writes=(), dsem=None, **kw):
        self._waits(Q, reads, writes)
        if dsem is None:
            cand = ([t for t in writes if t.space != "dram"] or [t for t in reads if t.space != "dram"]
                    or list(writes) or list(reads))
            t0 = cand[0]
            if t0.dsem is None:
                t0.dsem = self.new_dsem()
            dsem = t0.dsem
        inst = Q.eng.dma_start(out=out_ap, in_=in_ap, **kw)
        inst.then_inc(dsem.sem, 16)
        dsem.issued += 1
        ev = ("d", dsem)
        for t in reads:
            t.r[id(dsem)] = ev
        for t in writes:
            t.w = ev
            t.r = {}
        return inst

    def barrier(self):
        for E in self.engs:
            E.signal_last()
        for E in self.engs:
            for Dn in self.engs:
                if Dn is E or Dn.count == 0:
                    continue
                key = id(Dn.sem)
                if E.known.get(key, 0) < Dn.count:
                    E.eng.wait_ge(Dn.sem, Dn.count)
                    E.known[key] = Dn.count
            for d in self.dsems:
                if d.issued:
                    key = id(d.sem)
                    if E.known.get(key, 0) < d.issued * 16:
                        E.eng.wait_ge(d.sem, d.issued * 16)
                        E.known[key] = d.issued * 16

    def finish(self):
        for d in self.dsems:
            if d.issued:
                self.sp.eng.wait_ge(d.sem, d.issued * 16)


class Consts:
    pass


def make_consts(K, nc, prm):
    C = Consts()
    onesf = K.sb("c_onesf", [128, 128], F32)
    K.op(K.pool, lambda: nc.gpsimd.memset(onesf.ap, 1.0), writes=[onesf])
    C.onesf = onesf
    identf = K.sb("c_identf", [128, 128], F32)
    K.op(K.pool, lambda: nc.gpsimd.affine_select(identf.ap, onesf.ap, pattern=[[1, 128]], compare_op=ALU.is_equal,
                                                 fill=0.0, base=0, channel_multiplier=-1), reads=[onesf], writes=[identf])
    C.identf = identf
    identb = K.sb("c_identb", [128, 128], BF16)
    K.op(K.pool, lambda: nc.gpsimd.tensor_copy(identb.ap, identf.ap), reads=[identf], writes=[identb])
    C.identb = identb
    onesb = K.sb("c_onesb", [128, 128], BF16)
    K.op(K.pool, lambda: nc.gpsimd.memset(onesb.ap, 1.0), writes=[onesb])
    C.onesb = onesb
    tri = K.sb("c_tri", [128, 128], F32)
    K.op(K.pool, lambda: nc.gpsimd.affine_select(tri.ap, onesf.ap, pattern=[[1, 128]], compare_op=ALU.is_ge,
                                                 fill=0.0, base=0, channel_multiplier=-1), reads=[onesf], writes=[tri])
    C.tri = tri
    epsc = K.sb("c_eps", [128, 1], F32)
    K.op(K.pool, lambda: nc.gpsimd.memset(epsc.ap, EPS), writes=[epsc])
    C.eps = epsc
    onec = K.sb("c_one", [128, 1], F32)
    K.op(K.pool, lambda: nc.gpsimd.memset(onec.ap, 1.0), writes=[onec])
    C.one = onec
    blk = K.sb("c_blk", [128, 128], BF16)
    K.op(K.pool, lambda: nc.gpsimd.memset(blk.ap, 0.0), writes=[blk])
    K.op(K.pool, lambda: nc.gpsimd.memset(blk.ap[0:64, 0:64], 1.0), writes=[blk])
    K.op(K.pool, lambda: nc.gpsimd.memset(blk.ap[64:128, 64:128], 1.0), writes=[blk])
    C.blk = blk

    def vec_cols(name, items):
        R = sum(n for _, n in items)
        st = K.sb("vst_" + name, [R, 128], F32)
        r0 = 0
        for ap1, n in items:
            K.dma(K.sp, st.ap[r0:r0 + n, :], ap1.rearrange("(c p) -> c p", p=128), writes=[st])
            r0 += n
        pt = K.ps("vps_" + name, [128, 512], F32)
        K.op(K.pe, lambda: nc.tensor.transpose(pt.ap[:, 0:R], st.ap, identf.ap[0:R, 0:R]), reads=[st, identf], writes=[pt])
        out = K.sb("vec_" + name, [128, R], F32)
        K.op(K.dve, lambda: nc.vector.tensor_copy(out.ap, pt.ap[:, 0:R]), reads=[pt], writes=[out])
        return out

    veca = K.sb("c_veca", [128, 80], F32)
    vecb = K.sb("c_vecb", [128, 96], F32)
    with ExitStack() as es2:
        K.stack = es2
        va = vec_cols("a", [(prm["m_norm"][0], 8), (prm["kv_norm"], 8), (prm["s_norm"][0], 8),
                            (prm["ple_norm"][0], 8), (prm["ple_norm"][1], 8), (prm["m_ynorm"][0], 16),
                            (prm["m_conv_b"][0], 24)])
        vb = vec_cols("b", [(prm["m_conv_w"][0, j], 24) for j in range(4)])
        K.op(K.dve, lambda: nc.vector.tensor_copy(veca.ap, va.ap), reads=[va], writes=[veca])
        K.op(K.dve, lambda: nc.vector.tensor_copy(vecb.ap, vb.ap), reads=[vb], writes=[vecb])
        K.barrier()
        K.stack = K.es
    C.g_m, C.g_kv, C.g_s = veca.ap[:, 0:8], veca.ap[:, 8:16], veca.ap[:, 16:24]
    C.g_p0, C.g_p1 = veca.ap[:, 24:32], veca.ap[:, 32:40]
    C.g_y, C.conv_b = veca.ap[:, 40:56], veca.ap[:, 56:80]
    C.conv_w = vecb.ap
    C.veca, C.vecb = veca, vecb

    kq = K.sb("c_kq", [128, 2], F32)
    for half in range(2):
        K.dma(K.sp, kq.ap[half * 64:(half + 1) * 64, 0:1], prm["k_norm"].rearrange("(p o) -> p o", o=1), writes=[kq])
        K.dma(K.sp, kq.ap[half * 64:(half + 1) * 64, 1:2], prm["q_norm"][0].rearrange("(p o) -> p o", o=1), writes=[kq])
    kq2 = K.sb("c_kq2", [128, 2], F32)
    K.op(K.dve, lambda: nc.vector.tensor_copy(kq2.ap[:, 0:1], kq.ap[:, 0:1]), reads=[kq], writes=[kq2])
    K.op(K.dve, lambda: nc.vector.tensor_scalar(kq2.ap[:, 1:2], kq.ap[:, 1:2], 0.125, None, op0=ALU.mult), reads=[kq], writes=[kq2])
    C.kq = kq2
    bc = K.sb("c_bc", [128, 3, 32], F32)
    K.dma(K.sp, bc.ap[:, 0, :], prm["m_dt_bias"][0:1, :].to_broadcast([128, 32]), writes=[bc])
    K.dma(K.sp, bc.ap[:, 1, :], prm["m_A_log"][0:1, :].to_broadcast([128, 32]), writes=[bc])
    K.dma(K.sp, bc.ap[:, 2, :], prm["m_D"][0:1, :].to_broadcast([128, 32]), writes=[bc])
    C.bc = bc
    Abc = K.sb("c_A", [128, 32], F32)
    K.op(K.act, lambda: nc.scalar.activation(Abc.ap, bc.ap[:, 1, :], AF.Exp), reads=[bc], writes=[Abc])
    K.op(K.dve, lambda: nc.vector.tensor_scalar(Abc.ap, Abc.ap, -1.0, None, op0=ALU.mult), reads=[Abc], writes=[Abc])
    C.A = Abc
    Dcol = K.sb("c_Dcol", [128, 16], F32)
    dv = bc.ap[:, 2, :].rearrange("p (q r) -> p q r", r=2)
    K.op(K.dve, lambda: nc.vector.tensor_copy(Dcol.ap[0:64, :], dv[0:64, :, 0]), reads=[bc], writes=[Dcol])
    K.op(K.dve, lambda: nc.vector.tensor_copy(Dcol.ap[64:128, :], dv[64:128, :, 1]), reads=[bc], writes=[Dcol])
    C.Dcol = Dcol
    return C


def load_weight(K, nc, es, name, src, C_, N, stage_ring, cast_engs):
    w = K.sb(name, [128, C_, N], BF16, es)
    views = [K.view("%s_%d" % (name, c), w.ap[:, c, :]) for c in range(C_)]
    SW = stage_ring.t[0].ap.shape[1]
    i = 0
    for c in range(C_):
        for n0 in range(0, N, SW):
            n1 = min(N, n0 + SW)
            st = stage_ring.next()
            K.dma(K.sp, st.ap[:, 0:n1 - n0], src[c * 128:(c + 1) * 128, n0:n1], writes=[st])
            E = cast_engs[i % len(cast_engs)]
            i += 1
            if E is K.act:
                K.op(E, lambda: nc.scalar.copy(views[c].ap[:, n0:n1], st.ap[:, 0:n1 - n0]), reads=[st], writes=[views[c]])
            else:
                K.op(E, lambda: E.eng.tensor_copy(views[c].ap[:, n0:n1], st.ap[:, 0:n1 - n0]), reads=[st], writes=[views[c]])
    return views


def rms_to_xT(K, nc, C, ht, na, ss, lnv, rstd, junk, xh, ptr_ring, outs):
    for a in range(na):
        K.op(K.act, lambda: nc.scalar.activation(junk.ap, ht.ap[:, a, :], AF.Square, accum_out=ss.ap[:, a:a + 1]),
             reads=[ht], writes=[junk, ss])
    K.op(K.act, lambda: nc.scalar.copy(junk.ap[:, 0:8], junk.ap[:, 8:16]), reads=[junk], writes=[junk])
    K.op(K.act, lambda: nc.scalar.activation(lnv.ap[:, 0:na], ss.ap[:, 0:na], AF.Ln, bias=C.eps.ap, scale=1.0 / D),
         reads=[ss, C.eps], writes=[lnv])
    K.op(K.act, lambda: nc.scalar.activation(rstd.ap[:, 0:na], lnv.ap[:, 0:na], AF.Exp, scale=-0.5), reads=[lnv], writes=[rstd])
    for a in range(na):
        K.op(K.dve, lambda: nc.vector.tensor_scalar(xh.ap[:, a, :], ht.ap[:, a, :], rstd.ap[:, a:a + 1], None, op0=ALU.mult),
             reads=[ht, rstd], writes=[xh])
    k = 0
    for c in range(8):
        pt = ptr_ring.next()
        for a in range(na):
            K.op(K.pe, lambda: nc.tensor.transpose(pt.ap[:, a * 128:(a + 1) * 128], xh.ap[:, a, c * 128:(c + 1) * 128], C.identb.ap),
                 reads=[xh, C.identb], writes=[pt])
        for (xT, g) in outs:
            if k % 2 == 0:
                K.op(K.act, lambda: nc.scalar.activation(xT.ap[:, c, :], pt.ap[:, 0:na * 128], AF.Identity, scale=g[:, c:c + 1]),
                     reads=[pt, C.veca], writes=[xT])
            else:
                K.op(K.dve, lambda: nc.vector.tensor_scalar(xT.ap[:, c, :], pt.ap[:, 0:na * 128], g[:, c:c + 1], None, op0=ALU.mult),
                     reads=[pt, C.veca], writes=[xT])
            k += 1


def phase_p1a(K, nc, C, x_d, w_in, sz_scr, xbc_scr, dt_scr):
    with ExitStack() as es:
        K.stack = es
        K.begin_phase()
        stage = Ring([K.sb("p1a_st%d" % i, [128, 1288], F32) for i in range(2)])
        W = load_weight(K, nc, es, "p1a_w", w_in, 8, 5152, stage, [K.pool, K.dve, K.act])
        xt_ring = Ring([K.sb("p1a_x%d" % i, [128, 4, 1024], F32) for i in range(1)])
        ss = K.sb("p1a_ss", [128, 4], F32)
        lnv = K.sb("p1a_ln", [128, 4], F32)
        rstd = K.sb("p1a_rstd", [128, 4], F32)
        junk = K.sb("p1a_junk", [128, 1024], BF16)
        xh = K.sb("p1a_xh", [128, 4, 1024], BF16)
        xT = K.sb("p1a_xT", [128, 8, 512], BF16)
        ptr = Ring([K.ps("p1a_pt%d" % i, [128, 1024], BF16) for i in range(2)])
        pmm = Ring([K.ps("p1a_pm%d" % i, [128, 512], F32) for i in range(4)])
        pdt = K.ps("p1a_pdt", [128, 512], F32)
        ost_ring = Ring([K.sb("p1a_ost%d" % i, [128, 512], BF16) for i in range(6)])
        raw_ring = Ring([K.sb("p1a_raw%d" % i, [128, 515], F32) for i in range(3)])
        acc_ring = Ring([K.sb("p1a_acc%d" % i, [128, 512], F32) for i in range(2)])
        halo = K.sb("p1a_halo", [128, 24, 3], F32)
        halos = [K.view("p1a_halo%d" % i, halo.ap[:, i, :]) for i in range(24)]
        K.op(K.pool, lambda: nc.gpsimd.memset(halo.ap, 0.0), writes=halos)
        dtx = K.sb("p1a_dtx", [128, 4, 32], F32)
        dta = K.sb("p1a_dta", [128, 4, 32], F32)
        dte = K.sb("p1a_dte", [128, 4, 32], F32)
        dtm = K.sb("p1a_dtm", [128, 4, 32], F32)
        dto = K.sb("p1a_dto", [128, 4, 32], F32)
        szd = K.dram("sz_scr", sz_scr)
        xbd = K.dram("xbc_scr", xbc_scr)
        dtd = K.dram("dt_scr", dt_scr)
        xv = x_d.rearrange("(t a p) d -> t p a d", a=4, p=128)
        for ts in range(8):
            xt = xt_ring.next()
            K.dma(K.sp, xt.ap, xv[ts], writes=[xt])
            rms_to_xT(K, nc, C, xt, 4, ss, lnv, rstd, junk, xh, ptr, [(xT, C.g_m)])
            tok = slice(ts * 512, (ts + 1) * 512)
            for ft in range(40):
                pm = pmm.next()
                for c in range(8):
                    K.op(K.pe, lambda: nc.tensor.matmul(pm.ap, W[c].ap[:, ft * 128:(ft + 1) * 128], xT.ap[:, c, :],
                                                        start=(c == 0), stop=(c == 7)), reads=[W[c], xT], writes=[pm])
                ost = ost_ring.next()
                if ft < 16:
                    K.op(K.act, lambda: nc.scalar.activation(ost.ap, pm.ap, AF.Silu), reads=[pm], writes=[ost])
                    K.dma(K.sp, sz_scr[ft * 128:(ft + 1) * 128, tok], ost.ap, reads=[ost], writes=[szd])
                else:
                    ci = ft - 16
                    raw = raw_ring.next()
                    acc = acc_ring.next()
                    K.op(K.pool, lambda: nc.gpsimd.tensor_copy(raw.ap[:, 0:3], halos[ci].ap), reads=[halos[ci]], writes=[raw])
                    K.op(K.act, lambda: nc.scalar.copy(raw.ap[:, 3:515], pm.ap), reads=[pm], writes=[raw])
                    K.op(K.pool, lambda: nc.gpsimd.tensor_copy(halos[ci].ap, raw.ap[:, 512:515]), reads=[raw], writes=[halos[ci]])
                    cw = lambda j: C.conv_w[:, j * 24 + ci:j * 24 + ci + 1]
                    K.op(K.dve, lambda: nc.vector.tensor_scalar(acc.ap, raw.ap[:, 0:512], cw(0), None, op0=ALU.mult),
                         reads=[raw, C.vecb], writes=[acc])
                    for j in range(1, 4):
                        K.op(K.dve, lambda: nc.vector.scalar_tensor_tensor(acc.ap, raw.ap[:, j:j + 512], cw(j), acc.ap,
                                                                           op0=ALU.mult, op1=ALU.add),
                             reads=[raw, acc, C.vecb], writes=[acc])
                    K.op(K.act, lambda: nc.scalar.activation(ost.ap, acc.ap, AF.Silu, bias=C.conv_b[:, ci:ci + 1]),
                         reads=[acc, C.veca], writes=[ost])
                    K.dma(K.sp, xbc_scr[ci * 128:(ci + 1) * 128, tok], ost.ap, reads=[ost], writes=[xbd])
            for a in range(4):
                for c in range(8):
                    K.op(K.pe, lambda: nc.tensor.matmul(pdt.ap[:, a * 32:(a + 1) * 32], xT.ap[:, c, a * 128:(a + 1) * 128],
                                                        W[c].ap[:, 5120:5152], start=(c == 0), stop=(c == 7)),
                         reads=[W[c], xT], writes=[pdt])
            pv = pdt.ap[:, 0:128].rearrange("p (a h) -> p a h", a=4)
            K.op(K.dve, lambda: nc.vector.tensor_tensor(dtx.ap, pv, C.bc.ap[:, 0:1, :].to_broadcast([128, 4, 32]), op=ALU.add),
                 reads=[pdt, C.bc], writes=[dtx])
            K.op(K.act, lambda: nc.scalar.activation(dta.ap, dtx.ap, AF.Abs), reads=[dtx], writes=[dta])
            K.op(K.act, lambda: nc.scalar.activation(dte.ap, dta.ap, AF.Exp, scale=-1.0), reads=[dta], writes=[dte])
            K.op(K.act, lambda: nc.scalar.activation(dte.ap, dte.ap, AF.Ln, bias=C.one.ap, scale=1.0), reads=[dte, C.one], writes=[dte])
            K.op(K.dve, lambda: nc.vector.tensor_scalar(dtm.ap, dtx.ap, 0.0, None, op0=ALU.max), reads=[dtx], writes=[dtm])
            K.op(K.dve, lambda: nc.vector.tensor_tensor(dto.ap, dtm.ap, dte.ap, op=ALU.add), reads=[dtm, dte], writes=[dto])
            K.dma(K.sp, dt_scr.rearrange("(t a p) h -> t p a h", a=4, p=128)[ts], dto.ap, reads=[dto], writes=[dtd])
        K.end_phase()
        K.stack = K.es


def phase_p1b(K, nc, C, sz_scr, xbc_scr, dt_scr, yn_scr, nchunks=32):
    with ExitStack() as es:
        K.stack = es
        K.begin_phase()
        xb_ring = Ring([K.sb("p1b_xb%d" % i, [128, 24, 128], BF16) for i in range(2)])
        sz_ring = Ring([K.sb("p1b_sz%d" % i, [128, 16, 128], BF16) for i in range(2)])
        dt_ring = Ring([K.sb("p1b_dt%d" % i, [128, 32], F32) for i in range(2)])
        a_ch = K.sb("p1b_a", [128, 32], F32)
        acum = K.sb("p1b_acum", [128, 32], F32)
        tmpw = K.sb("p1b_tmpw", [128, 32], F32)
        wend = K.sb("p1b_wend", [128, 32], F32)
        dtw = K.sb("p1b_dtw", [128, 32], F32)
        dA = K.sb("p1b_dA", [128, 32], F32)
        xdt_pad = K.sb("p1b_xdtp", [128, 32, 128], BF16)
        xdtw = K.sb("p1b_xdtw", [128, 32, 64], BF16)
        btok = K.sb("p1b_btok", [128, 4, 128], BF16)
        state = K.sb("p1b_state", [128, 32, 64], F32)
        st_pad = K.sb("p1b_stp", [128, 32, 128], BF16)
        cbm = K.sb("p1b_cbm", [128, 4, 128], F32)
        seg_ring = Ring([K.sb("p1b_seg%d" % i, [128, 4, 128], F32) for i in range(2)])
        dec_ring = Ring([K.sb("p1b_dec%d" % i, [128, 4, 128], F32) for i in range(2)])
        ea_ring = Ring([K.sb("p1b_ea%d" % i, [128, 4, 128], F32) for i in range(2)])
        mt_ring = Ring([K.sb("p1b_mt%d" % i, [128, 4, 128], BF16) for i in range(3)])
        cs_ring = Ring([K.sb("p1b_cs%d" % i, [128, 4, 128], BF16) for i in range(3)])
        ytmp_ring = Ring([K.sb("p1b_yt%d" % i, [128, 128], F32) for i in range(2)])
        ygate = K.sb("p1b_yg", [128, 16, 128], F32)
        sq = K.sb("p1b_sq", [128, 16, 128], BF16)
        grs = K.sb("p1b_grs", [128, 4, 128], F32)
        yn_ring = Ring([K.sb("p1b_yn%d" % i, [128, 16, 128], BF16) for i in range(2)])
        pxs = K.ps("p1b_pxs", [128, 2048], BF16)
        pb = K.ps("p1b_pb", [128, 1024], BF16)
        psm = K.ps("p1b_psm", [128, 512], F32)
        pabc = Ring([K.ps("p1b_pabc%d" % i, [128, 512], F32) for i in range(2)])
        py_ring = Ring([K.ps("p1b_py%d" % i, [128, 512], F32) for i in range(2)])
        pupd = psm
        for t_ in (xdt_pad, st_pad, state):
            K.op(K.pool, lambda: nc.gpsimd.memset(t_.ap, 0.0), writes=[t_])
        szd = K.dram("sz_scr", sz_scr)
        xbd = K.dram("xbc_scr", xbc_scr)
        dtd = K.dram("dt_scr", dt_scr)
        ynd = K.dram("yn_scr", yn_scr)
        xbv = xbc_scr.rearrange("(q p) t -> p q t", p=128)
        szv = sz_scr.rearrange("(q p) t -> p q t", p=128)
        ynv = yn_scr.rearrange("(q p) t -> p q t", p=128)
        xdt4 = xdt_pad.ap.rearrange("p (q r) c -> p q r c", r=2)
        stp4 = st_pad.ap.rearrange("p (q r) c -> p q r c", r=2)
        for ch in range(nchunks):
            cols = slice(ch * 128, (ch + 1) * 128)
            xb = xb_ring.next()
            sz = sz_ring.next()
            dt = dt_ring.next()
            K.dma(K.sp, xb.ap, xbv[:, :, cols], reads=[xbd], writes=[xb])
            K.dma(K.sp, sz.ap, szv[:, :, cols], reads=[szd], writes=[sz])
            K.dma(K.sp, dt.ap, dt_scr[ch * 128:(ch + 1) * 128, :], reads=[dtd], writes=[dt])
            if _DBG_STOP <= 1:
                continue
            K.op(K.dve, lambda: nc.vector.tensor_tensor(a_ch.ap, dt.ap, C.A.ap, op=ALU.mult), reads=[dt, C.A], writes=[a_ch])
            K.op(K.pe, lambda: nc.tensor.matmul(psm.ap[:, 0:32], C.tri.ap, a_ch.ap, start=True, stop=True),
                 reads=[C.tri, a_ch], writes=[psm])
            K.op(K.pe, lambda: nc.tensor.matmul(psm.ap[:, 32:64], C.onesf.ap, a_ch.ap, start=True, stop=True),
                 reads=[C.onesf, a_ch], writes=[psm])
            K.op(K.act, lambda: nc.scalar.copy(acum.ap, psm.ap[:, 0:32]), reads=[psm], writes=[acum])
            K.op(K.dve, lambda: nc.vector.tensor_tensor(tmpw.ap, psm.ap[:, 32:64], acum.ap, op=ALU.subtract),
                 reads=[psm, acum], writes=[tmpw])
            K.op(K.act, lambda: nc.scalar.activation(wend.ap, tmpw.ap, AF.Exp), reads=[tmpw], writes=[wend])
            K.op(K.act, lambda: nc.scalar.activation(dA.ap, psm.ap[:, 32:64], AF.Exp), reads=[psm], writes=[dA])
            K.op(K.dve, lambda: nc.vector.tensor_tensor(dtw.ap, dt.ap, wend.ap, op=ALU.mult), reads=[dt, wend], writes=[dtw])
            if _DBG_STOP <= 2:
                continue
            for ci in range(16):
                K.op(K.pe, lambda: nc.tensor.transpose(pxs.ap[:, ci * 128:(ci + 1) * 128], xb.ap[:, ci, :], C.identb.ap),
                     reads=[xb, C.identb], writes=[pxs])
            for g in range(4):
                K.op(K.pe, lambda: nc.tensor.transpose(pb.ap[:, g * 128:(g + 1) * 128], xb.ap[:, 16 + g, :], C.identb.ap),
                     reads=[xb, C.identb], writes=[pb])
            pxs4 = pxs.ap.rearrange("p (q r c) -> p q r c", r=2, c=64)
            dt3 = dt.ap.rearrange("p (q r) -> p q r", r=2)
            for r in range(2):
                K.op(K.dve, lambda: nc.vector.tensor_tensor(xdt4[:, :, r, r * 64:(r + 1) * 64], pxs4[:, :, r, :],
                                                            dt3[:, :, r:r + 1].to_broadcast([128, 16, 64]), op=ALU.mult),
                     reads=[pxs, dt], writes=[xdt_pad])
            K.op(K.dve, lambda: nc.vector.tensor_tensor(xdtw.ap, pxs.ap.rearrange("p (h c) -> p h c", c=64),
                                                        dtw.ap.unsqueeze(2).to_broadcast([128, 32, 64]), op=ALU.mult),
                 reads=[pxs, dtw], writes=[xdtw])
            K.op(K.act, lambda: nc.scalar.copy(btok.ap.rearrange("p g n -> p (g n)"), pb.ap[:, 0:512]), reads=[pb], writes=[btok])
            if _DBG_STOP <= 3:
                continue
            for g in range(4):
                K.op(K.pe, lambda: nc.tensor.matmul(pupd.ap[:, g * 128:(g + 1) * 128],
                                                    xb.ap[:, 16 + g, :], xb.ap[:, 20 + g, :], start=True, stop=True),
                     reads=[xb], writes=[pupd])
            for g in range(4):
                K.op(K.dve, lambda: nc.vector.tensor_tensor(cbm.ap[:, g, :], pupd.ap[:, g * 128:(g + 1) * 128], C.tri.ap, op=ALU.mult),
                     reads=[pupd, C.tri], writes=[cbm])
            if _DBG_STOP <= 4:
                continue
            mts = {}
            css = {}
            pas = {}

            def emit_abc(hq):
                pa = pabc.next()
                for i in range(4):
                    h = hq * 4 + i
                    K.op(K.pe, lambda: nc.tensor.matmul(pa.ap[:, i * 128:(i + 1) * 128], a_ch.ap[:, h:h + 1].to_broadcast([128, 128]),
                                                        C.tri.ap, start=True, stop=True), reads=[a_ch, C.tri], writes=[pa])
                K.pe.signal_last()
                pas[hq] = pa

            emit_abc(0)
            for hq in range(8):
                g = hq // 2
                pa = pas[hq]
                seg = seg_ring.next()
                dec = dec_ring.next()
                ea = ea_ring.next()
                mt = mt_ring.next()
                cs = cs_ring.next()
                for i in range(4):
                    h = hq * 4 + i
                    K.op(K.dve, lambda: nc.vector.tensor_scalar(seg.ap[:, i, :], pa.ap[:, i * 128:(i + 1) * 128], acum.ap[:, h:h + 1], 0.0,
                                                                op0=ALU.subtract, op1=ALU.min), reads=[pa, acum], writes=[seg])
                K.op(K.act, lambda: nc.scalar.activation(dec.ap, seg.ap, AF.Exp), reads=[seg], writes=[dec])
                K.op(K.act, lambda: nc.scalar.activation(ea.ap.rearrange("p i l -> p (i l)"), pa.ap, AF.Exp), reads=[pa], writes=[ea])
                for i in range(4):
                    K.op(K.pool, lambda: nc.gpsimd.tensor_tensor(mt.ap[:, i, :], dec.ap[:, i, :], cbm.ap[:, g, :], op=ALU.mult),
                         reads=[dec, cbm], writes=[mt])
                    K.op(K.pool, lambda: nc.gpsimd.tensor_tensor(cs.ap[:, i, :], ea.ap[:, i, :], xb.ap[:, 20 + g, :], op=ALU.mult),
                         reads=[ea, xb], writes=[cs])
                if hq + 1 < 8:
                    emit_abc(hq + 1)
                for pi in range(2):
                    q = hq * 2 + pi
                    pyq = py_ring.next()
                    yo = pyq.ap[:, 0:128]
                    ops = [(xdt_pad, xdt_pad.ap[:, 2 * q, :], mt, mt.ap[:, 2 * pi, :]),
                           (xdt_pad, xdt_pad.ap[:, 2 * q + 1, :], mt, mt.ap[:, 2 * pi + 1, :]),
                           (st_pad, st_pad.ap[:, 2 * q, :], cs, cs.ap[:, 2 * pi, :]),
                           (st_pad, st_pad.ap[:, 2 * q + 1, :], cs, cs.ap[:, 2 * pi + 1, :])]
                    for k_, (lt, la, rt, ra) in enumerate(ops):
                        K.op(K.pe, lambda: nc.tensor.matmul(yo, la, ra, start=(k_ == 0), stop=(k_ == 3)), reads=[lt, rt], writes=[pyq])
                    yt = ytmp_ring.next()
                    K.op(K.dve, lambda: nc.vector.scalar_tensor_tensor(yt.ap, xb.ap[:, q, :], C.Dcol.ap[:, q:q + 1], yo,
                                                                       op0=ALU.mult, op1=ALU.add), reads=[xb, C.Dcol, pyq], writes=[yt])
                    K.op(K.pool, lambda: nc.gpsimd.tensor_tensor(ygate.ap[:, q, :], yt.ap, sz.ap[:, q, :], op=ALU.mult),
                         reads=[yt, sz], writes=[ygate])
            if _DBG_STOP <= 5:
                continue
            for g in range(4):
                K.op(K.pe, lambda: nc.tensor.matmul(pupd.ap, btok.ap[:, g, :], xdtw.ap[:, g * 8:(g + 1) * 8, :].rearrange("p h c -> p (h c)"),
                                                    start=True, stop=True), reads=[btok, xdtw], writes=[pupd])
                sv = state.ap[:, g * 8:(g + 1) * 8, :]
                K.op(K.dve, lambda: nc.vector.tensor_tensor(sv, sv, dA.ap[:, g * 8:(g + 1) * 8].unsqueeze(2).to_broadcast([128, 8, 64]),
                                                            op=ALU.mult), reads=[state, dA], writes=[state])
                K.op(K.dve, lambda: nc.vector.tensor_tensor(sv, sv, pupd.ap.rearrange("p (h c) -> p h c", c=64), op=ALU.add),
                     reads=[state, pupd], writes=[state])
            st4 = state.ap.rearrange("p (q r) c -> p q r c", r=2)
            for r in range(2):
                K.op(K.act, lambda: nc.scalar.copy(stp4[:, :, r, r * 64:(r + 1) * 64], st4[:, :, r, :]), reads=[state], writes=[st_pad])
            if _DBG_STOP <= 6:
                continue
            K.op(K.act, lambda: nc.scalar.activation(sq.ap, ygate.ap, AF.Square), reads=[ygate], writes=[sq])
            for G in range(4):
                for k_ in range(4):
                    K.op(K.pe, lambda: nc.tensor.matmul(psm.ap[:, G * 128:(G + 1) * 128], C.onesb.ap, sq.ap[:, G * 4 + k_, :],
                                                        start=(k_ == 0), stop=(k_ == 3)), reads=[C.onesb, sq], writes=[psm])
            K.op(K.act, lambda: nc.scalar.activation(grs.ap.rearrange("p g l -> p (g l)"), psm.ap, AF.Ln, bias=C.eps.ap, scale=1.0 / 512),
                 reads=[psm, C.eps], writes=[grs])
            K.op(K.act, lambda: nc.scalar.activation(grs.ap, grs.ap, AF.Exp, scale=-0.5), reads=[grs], writes=[grs])
            yn = yn_ring.next()
            for q in range(16):
                K.op(K.dve, lambda: nc.vector.scalar_tensor_tensor(yn.ap[:, q, :], ygate.ap[:, q, :], C.g_y[:, q:q + 1], grs.ap[:, q // 4, :],
                                                                   op0=ALU.mult, op1=ALU.mult), reads=[ygate, C.veca, grs], writes=[yn])
            K.dma(K.sp, ynv[:, :, cols], yn.ap, reads=[yn], writes=[ynd])
        K.end_phase()
        K.stack = K.es


def phase_out_ple(K, nc, C, name, a_scr, FC, w_o, h_in, p_in, g_ple, w_gate, w_proj, h_out):
    with ExitStack() as es:
        K.stack = es
        K.begin_phase()
        stage = Ring([K.sb(name + "_st%d" % i, [128, 1024], F32) for i in range(3)])
        Wo = load_weight(K, nc, es, name + "_wo", w_o, FC, 1024, stage, [K.pool, K.dve, K.act])
        Wg = load_weight(K, nc, es, name + "_wg", w_gate, 8, 1024, stage, [K.pool, K.dve, K.act])
        Wp = load_weight(K, nc, es, name + "_wp", w_proj, 2, 1024, stage, [K.pool, K.dve, K.act])
        at_ring = Ring([K.sb(name + "_at%d" % i, [128, FC, 512], BF16) for i in range(1)])
        h_ring = Ring([K.sb(name + "_h%d" % i, [128, 4, 1024], F32) for i in range(1)])
        p_ring = Ring([K.sb(name + "_p%d" % i, [128, 4, 256], F32) for i in range(2)])
        pbf = K.sb(name + "_pbf", [128, 4, 256], BF16)
        pT = K.sb(name + "_pT", [128, 2, 512], BF16)
        ss = K.sb(name + "_ss", [128, 4], F32)
        lnv = K.sb(name + "_ln", [128, 4], F32)
        rstd = K.sb(name + "_rstd", [128, 4], F32)
        junk = K.sb(name + "_junk", [128, 1024], BF16)
        xh = K.sb(name + "_xh", [128, 4, 1024], BF16)
        xT = K.sb(name + "_xT", [128, 8, 512], BF16)
        sig_ring = Ring([K.sb(name + "_sig%d" % i, [128, 512], F32) for i in range(2)])
        tmp_ring = Ring([K.sb(name + "_tmp%d" % i, [128, 512], F32) for i in range(2)])
        ho_ring = Ring([K.sb(name + "_ho%d" % i, [128, 4, 1024], F32) for i in range(1)])
        ptr = Ring([K.ps(name + "_pt%d" % i, [128, 1024], BF16) for i in range(2)])
        pmm = Ring([K.ps(name + "_pm%d" % i, [128, 512], F32) for i in range(2)])
        pg_ring = Ring([K.ps(name + "_pg%d" % i, [128, 512], F32) for i in range(2)])
        pp_ring = Ring([K.ps(name + "_pp%d" % i, [128, 512], F32) for i in range(2)])
        ad = K.dram(name + "_a", a_scr)
        hd = K.dram(name + "_hin", h_in)
        od = K.dram(name + "_hout", h_out)
        av = a_scr.rearrange("(c p) t -> p c t", p=128)
        hv = h_in.rearrange("(t a p) d -> t p a d", a=4, p=128)
        pv = p_in.rearrange("(t a p) d -> t p a d", a=4, p=128)
        ov = h_out.rearrange("(t a p) d -> t p a d", a=4, p=128)
        for ts in range(8):
            at = at_ring.next()
            ht = h_ring.next()
            pt_ = p_ring.next()
            K.dma(K.sp, at.ap, av[:, :, ts * 512:(ts + 1) * 512], reads=[ad], writes=[at])
            K.dma(K.sp, ht.ap, hv[ts], reads=[hd], writes=[ht])
            K.dma(K.sp, pt_.ap, pv[ts], writes=[pt_])
            for a in range(4):
                for half in range(2):
                    pm = pmm.next()
                    for c in range(FC):
                        K.op(K.pe, lambda: nc.tensor.matmul(pm.ap, at.ap[:, c, a * 128:(a + 1) * 128], Wo[c].ap[:, half * 512:(half + 1) * 512],
                                                            start=(c == 0), stop=(c == FC - 1)), reads=[at, Wo[c]], writes=[pm])
                    hs = ht.ap[:, a, half * 512:(half + 1) * 512]
                    K.op(K.dve, lambda: nc.vector.tensor_tensor(hs, hs, pm.ap, op=ALU.add), reads=[ht, pm], writes=[ht])
            rms_to_xT(K, nc, C, ht, 4, ss, lnv, rstd, junk, xh, ptr, [(xT, g_ple)])
            K.op(K.pool, lambda: nc.gpsimd.tensor_copy(pbf.ap, pt_.ap), reads=[pt_], writes=[pbf])
            for c2 in range(2):
                pt = ptr.next()
                for a in range(4):
                    K.op(K.pe, lambda: nc.tensor.transpose(pt.ap[:, a * 128:(a + 1) * 128], pbf.ap[:, a, c2 * 128:(c2 + 1) * 128], C.identb.ap),
                         reads=[pbf, C.identb], writes=[pt])
                K.op(K.act, lambda: nc.scalar.copy(pT.ap[:, c2, :], pt.ap[:, 0:512]), reads=[pt], writes=[pT])
            ho = ho_ring.next()
            for a in range(4):
                for half in range(2):
                    pg = pg_ring.next()
                    pp = pp_ring.next()
                    cs_ = slice(half * 512, (half + 1) * 512)
                    for c in range(8):
                        K.op(K.pe, lambda: nc.tensor.matmul(pg.ap, xT.ap[:, c, a * 128:(a + 1) * 128], Wg[c].ap[:, cs_],
                                                            start=(c == 0), stop=(c == 7)), reads=[xT, Wg[c]], writes=[pg])
                    for c in range(2):
                        K.op(K.pe, lambda: nc.tensor.matmul(pp.ap, pT.ap[:, c, a * 128:(a + 1) * 128], Wp[c].ap[:, cs_],
                                                            start=(c == 0), stop=(c == 1)), reads=[pT, Wp[c]], writes=[pp])
                    sg = sig_ring.next()
                    tm = tmp_ring.next()
                    K.op(K.act, lambda: nc.scalar.activation(sg.ap, pg.ap, AF.Sigmoid), reads=[pg], writes=[sg])
                    K.op(K.dve, lambda: nc.vector.tensor_tensor(tm.ap, pp.ap, sg.ap, op=ALU.mult), reads=[pp, sg], writes=[tm])
                    K.op(K.pool, lambda: nc.gpsimd.tensor_tensor(ho.ap[:, a, cs_], tm.ap, ht.ap[:, a, cs_], op=ALU.add),
                         reads=[tm, ht], writes=[ho])
            K.dma(K.sp, ov[ts], ho.ap, reads=[ho], writes=[od])
        K.end_phase()
        K.stack = K.es


def phase_p3(K, nc, C, h1, w_kv, s_in, kt_scr, v_scr, q_scr, sg_scr):
    with ExitStack() as es:
        K.stack = es
        K.begin_phase()
        stage = Ring([K.sb("p3_st%d" % i, [128, 2048], F32) for i in range(2)])
        Wkv = load_weight(K, nc, es, "p3_wkv", w_kv, 8, 2048, stage, [K.pool, K.dve, K.act])
        Wqg = load_weight(K, nc, es, "p3_wqg", s_in, 8, 2048, stage, [K.pool, K.dve, K.act])
        h_ring = Ring([K.sb("p3_h%d" % i, [128, 4, 1024], F32) for i in range(2)])
        ss = K.sb("p3_ss", [128, 4], F32)
        lnv = K.sb("p3_ln", [128, 4], F32)
        rstd = K.sb("p3_rstd", [128, 4], F32)
        junk = K.sb("p3_junk", [128, 1024], BF16)
        xh = K.sb("p3_xh", [128, 4, 1024], BF16)
        xTk = K.sb("p3_xTk", [128, 8, 512], BF16)
        xTq = K.sb("p3_xTq", [128, 8, 512], BF16)
        sq_ring = Ring([K.sb("p3_sq%d" % i, [128, 512], BF16) for i in range(2)])
        rs_ring = Ring([K.sb("p3_rs%d" % i, [128, 512], F32) for i in range(2)])
        kst = K.sb("p3_kst", [128, 8, 512], BF16)
        vst = K.sb("p3_vst", [128, 4, 1024], BF16)
        qst = K.sb("p3_qst", [128, 8, 512], BF16)
        gst = K.sb("p3_gst", [128, 8, 512], BF16)
        ptr = Ring([K.ps("p3_pt%d" % i, [128, 1024], BF16) for i in range(2)])
        pmm = Ring([K.ps("p3_pm%d" % i, [128, 512], F32) for i in range(3)])
        pn_ring = Ring([K.ps("p3_pn%d" % i, [128, 512], F32) for i in range(2)])
        hd = K.dram("p3_h1", h1)
        kd = K.dram("kt_scr", kt_scr)
        vd = K.dram("v_scr", v_scr)
        qd = K.dram("q_scr", q_scr)
        gd = K.dram("sg_scr", sg_scr)
        hv = h1.rearrange("(t a p) d -> t p a d", a=4, p=128)
        vv = v_scr.rearrange("(t a p) d -> t p a d", a=4, p=128)

        def headnorm(pm, gcol, out_ap, out_t):
            sq = sq_ring.next()
            rs = rs_ring.next()
            pn = pn_ring.next()
            K.op(K.act, lambda: nc.scalar.activation(sq.ap, pm.ap, AF.Square), reads=[pm], writes=[sq])
            K.op(K.pe, lambda: nc.tensor.matmul(pn.ap, C.blk.ap, sq.ap, start=True, stop=True), reads=[C.blk, sq], writes=[pn])
            K.op(K.act, lambda: nc.scalar.activation(rs.ap, pn.ap, AF.Ln, bias=C.eps.ap, scale=1.0 / 64), reads=[pn, C.eps], writes=[rs])
            K.op(K.act, lambda: nc.scalar.activation(rs.ap, rs.ap, AF.Exp, scale=-0.5), reads=[rs], writes=[rs])
            K.op(K.dve, lambda: nc.vector.scalar_tensor_tensor(out_ap, pm.ap, gcol, rs.ap, op0=ALU.mult, op1=ALU.mult),
                 reads=[pm, C.kq, rs], writes=[out_t])

        for ts in range(8):
            ht = h_ring.next()
            K.dma(K.sp, ht.ap, hv[ts], reads=[hd], writes=[ht])
            rms_to_xT(K, nc, C, ht, 4, ss, lnv, rstd, junk, xh, ptr, [(xTk, C.g_kv), (xTq, C.g_s)])
            tok = slice(ts * 512, (ts + 1) * 512)
            for fo in range(8):
                pm = pmm.next()
                for c in range(8):
                    K.op(K.pe, lambda: nc.tensor.matmul(pm.ap, Wkv[c].ap[:, fo * 128:(fo + 1) * 128], xTk.ap[:, c, :],
                                                        start=(c == 0), stop=(c == 7)), reads=[Wkv[c], xTk], writes=[pm])
                headnorm(pm, C.kq.ap[:, 0:1], kst.ap[:, fo, :], kst)
            for a in range(4):
                for half in range(2):
                    pm = pmm.next()
                    for c in range(8):
                        K.op(K.pe, lambda: nc.tensor.matmul(pm.ap, xTk.ap[:, c, a * 128:(a + 1) * 128],
                                                            Wkv[c].ap[:, 1024 + half * 512:1024 + (half + 1) * 512],
                                                            start=(c == 0), stop=(c == 7)), reads=[Wkv[c], xTk], writes=[pm])
                    K.op(K.act, lambda: nc.scalar.copy(vst.ap[:, a, half * 512:(half + 1) * 512], pm.ap), reads=[pm], writes=[vst])
            for fo in range(8):
                pm = pmm.next()
                for c in range(8):
                    K.op(K.pe, lambda: nc.tensor.matmul(pm.ap, Wqg[c].ap[:, fo * 128:(fo + 1) * 128], xTq.ap[:, c, :],
                                                        start=(c == 0), stop=(c == 7)), reads=[Wqg[c], xTq], writes=[pm])
                headnorm(pm, C.kq.ap[:, 1:2], qst.ap[:, fo, :], qst)
            for fo in range(8):
                pm = pmm.next()
                for c in range(8):
                    K.op(K.pe, lambda: nc.tensor.matmul(pm.ap, Wqg[c].ap[:, 1024 + fo * 128:1024 + (fo + 1) * 128], xTq.ap[:, c, :],
                                                        start=(c == 0), stop=(c == 7)), reads=[Wqg[c], xTq], writes=[pm])
                K.op(K.act, lambda: nc.scalar.activation(gst.ap[:, fo, :], pm.ap, AF.Silu), reads=[pm], writes=[gst])
            for hf in range(2):
                qs = slice(hf * 4, (hf + 1) * 4)
                K.dma(K.sp, kt_scr.rearrange("(q p) t -> p q t", p=128)[:, qs, tok], kst.ap[:, qs, :], reads=[kst], writes=[kd])
                K.dma(K.sp, q_scr.rearrange("(q p) t -> p q t", p=128)[:, qs, tok], qst.ap[:, qs, :], reads=[qst], writes=[qd])
                K.dma(K.sp, sg_scr.rearrange("(q p) t -> p q t", p=128)[:, qs, tok], gst.ap[:, qs, :], reads=[gst], writes=[gd])
            K.dma(K.sp, vv[ts], vst.ap, reads=[vst], writes=[vd])
        K.end_phase()
        K.stack = K.es


def phase_p4(K, nc, C, kt_scr, v_scr, q_scr, sg_scr, og_scr, nTB=8, nQ=8):
    with ExitStack() as es:
        K.stack = es
        K.begin_phase()
        ntri = K.sb("p4_ntri", [128, 128], BF16)
        ebig = K.sb("p4_ebig", [128, 255], BF16)
        nsel = K.sb("p4_nsel", [128, 32, 128], BF16)
        masks = K.sb("p4_masks", [128, 4, 512], BF16)
        with ExitStack() as est:
            K.stack = est
            negb = K.sb("p4_negb", [128, 128], BF16)
            negb3 = K.sb("p4_negb3", [128, 32, 128], BF16)
            oneb3 = K.sb("p4_oneb3", [128, 4, 512], BF16)
            K.op(K.pool, lambda: nc.gpsimd.memset(negb.ap, -1.0), writes=[negb])
            K.op(K.pool, lambda: nc.gpsimd.affine_select(ntri.ap, negb.ap, pattern=[[-1, 128]], compare_op=ALU.is_ge, fill=0.0,
                                                         base=0, channel_multiplier=1), reads=[negb], writes=[ntri])
            K.op(K.pool, lambda: nc.gpsimd.memset(ebig.ap, 0.0), writes=[ebig])
            K.op(K.pool, lambda: nc.gpsimd.memset(ebig.ap[:, 127:128], 1.0), writes=[ebig])
            K.op(K.pool, lambda: nc.gpsimd.memset(negb3.ap, -1.0), writes=[negb3])
            K.op(K.pool, lambda: nc.gpsimd.affine_select(nsel.ap, negb3.ap, pattern=[[-1, 32], [0, 128]], compare_op=ALU.is_ge, fill=0.0,
                                                         base=-1, channel_multiplier=1), reads=[negb3], writes=[nsel])
            K.op(K.pool, lambda: nc.gpsimd.memset(oneb3.ap, 1.0), writes=[oneb3])
            K.op(K.pool, lambda: nc.gpsimd.affine_select(masks.ap, oneb3.ap, pattern=[[-128, 4], [1, 512]], compare_op=ALU.is_gt, fill=0.0,
                                                         base=0, channel_multiplier=-1), reads=[oneb3], writes=[masks])
            K.barrier()
            K.stack = es
        qpad = [Ring([K.sb("p4_qp%d_%d" % (r, i), [128, 512], BF16) for i in range(2)]) for r in range(2)]
        for r in range(2):
            for t_ in qpad[r].t:
                K.op(K.pool, lambda: nc.gpsimd.memset(t_.ap, 0.0), writes=[t_])
        kt_ring = Ring([K.sb("p4_kt%d" % i, [128, S], BF16) for i in range(2)])
        v_ring = Ring([K.sb("p4_v%d" % i, [128, 32, 128], BF16) for i in range(2)])
        sgt_ring = Ring([K.sb("p4_sg%d" % i, [128, 512], BF16) for i in range(2)])
        ogst_ring = Ring([K.sb("p4_og%d" % i, [128, 512], BF16) for i in range(2)])
        SPs = [K.sb("p4_sp%d" % i, [128, 32, 512], BF16) for i in range(2)]
        SPvs = [[K.view("p4_sp%d_%d" % (b, i), SPs[b].ap[:, i, :]) for i in range(32)] for b in range(2)]
        e_ring = Ring([K.sb("p4_e%d" % i, [128, 512], F32) for i in range(3)])
        spt_ring = Ring([K.sb("p4_spt%d" % i, [128, 512], F32) for i in range(2)])
        csb_ring = Ring([K.sb("p4_csb%d" % i, [128, 512], BF16) for i in range(2)])
        wt_ring = Ring([K.sb("p4_wt%d" % i, [128, 512], BF16) for i in range(4)])
        wm_ring = Ring([K.sb("p4_wm%d" % i, [128, 512], BF16) for i in range(2)])
        pz = Ring([K.ps("p4_pz%d" % i, [128, 512], F32) for i in range(3)])
        pcs = K.ps("p4_pcs", [128, 512], F32)
        pgr = Ring([K.ps("p4_pg%d" % i, [128, 512], F32) for i in range(3)])
        po_ring = Ring([K.ps("p4_po%d" % i, [128, 512], F32) for i in range(1)])
        kd = K.dram("kt_scr", kt_scr)
        vd = K.dram("v_scr", v_scr)
        qd = K.dram("q_scr", q_scr)
        gd = K.dram("sg_scr", sg_scr)
        od = K.dram("og_scr", og_scr)
        vv = v_scr.rearrange("(jb p) (q c) -> q p jb c", p=128, c=128)
        heads = [(q, TB, r) for q in range(nQ) for TB in range(nTB) for r in range(2)]
        groups = {}
        qstate = {}
        hstate = {}

        def prepare(i):
            q, TB, r = heads[i]
            if q not in qstate:
                KT = kt_ring.next()
                V = v_ring.next()
                K.dma(K.sp, KT.ap, kt_scr[q * 128:(q + 1) * 128, :], reads=[kd], writes=[KT])
                K.dma(K.sp, V.ap, vv[q], reads=[vd], writes=[V])
                qstate[q] = (KT, V)
            if (q, TB) not in groups:
                tok = slice(TB * 512, (TB + 1) * 512)
                sgt = sgt_ring.next()
                K.dma(K.sp, sgt.ap, sg_scr[q * 128:(q + 1) * 128, tok], reads=[gd], writes=[sgt])
                ogst = ogst_ring.next()
                qps = []
                for r_ in range(2):
                    qp = qpad[r_].next()
                    K.dma(K.sp, qp.ap[r_ * 64:(r_ + 1) * 64, :], q_scr[q * 128 + r_ * 64:q * 128 + (r_ + 1) * 64, tok],
                          reads=[qd], writes=[qp])
                    qps.append(qp)
                groups[(q, TB)] = (sgt, ogst, qps, tok)
            hstate[i] = {"SPv": SPvs[i % 2]}

        def sweep1(i):
            q, TB, r = heads[i]
            KT, V = qstate[q]
            sgt, ogst, qps, tok = groups[(q, TB)]
            qp = qps[r]
            SPv = hstate[i]["SPv"]
            nkb = 4 * (TB + 1)
            zs, es_ = {}, {}

            def emit_z(jb):
                z = pz.next()
                K.op(K.pe, lambda: nc.tensor.matmul(z.ap, KT.ap[:, jb * 128:(jb + 1) * 128], qp.ap, start=True, stop=True),
                     reads=[KT, qp], writes=[z])
                K.pe.signal_last()
                zs[jb] = z

            def emit_exp(jb):
                e = e_ring.next()
                K.op(K.act, lambda: nc.scalar.activation(e.ap, zs[jb].ap, AF.Exp), reads=[zs[jb]], writes=[e])
                es_[jb] = e

            def emit_ln(jb):
                e = es_[jb]
                rr = jb - 4 * TB
                if rr < 0:
                    K.op(K.act, lambda: nc.scalar.activation(SPv[jb].ap, e.ap, AF.Ln, bias=C.one.ap, scale=1.0),
                         reads=[e, C.one], writes=[SPv[jb]])
                else:
                    spt = spt_ring.next()
                    K.op(K.act, lambda: nc.scalar.activation(spt.ap, e.ap, AF.Ln, bias=C.one.ap, scale=1.0),
                         reads=[e, C.one], writes=[spt])
                    K.op(K.dve, lambda: nc.vector.tensor_tensor(SPv[jb].ap, spt.ap, masks.ap[:, rr, :], op=ALU.mult),
                         reads=[spt, masks], writes=[SPv[jb]])

            def emit_cs(jb):
                K.op(K.pe, lambda: nc.tensor.matmul(pcs.ap, ebig.ap[:, 127 - jb:255 - jb], SPv[jb].ap,
                                                    start=(jb == 0), stop=(jb == nkb - 1)), reads=[ebig, SPv[jb]], writes=[pcs])

            emit_z(0)
            if nkb > 1:
                emit_z(1)
            emit_exp(0)
            for jb in range(nkb):
                if jb + 1 < nkb:
                    emit_exp(jb + 1)
                emit_ln(jb)
                if jb + 2 < nkb:
                    emit_z(jb + 2)
                emit_cs(jb)
            csb = csb_ring.next()
            K.op(K.dve, lambda: nc.vector.tensor_copy(csb.ap, pcs.ap), reads=[pcs], writes=[csb])
            hstate[i]["csb"] = csb

        def sweep2(i):
            q, TB, r = heads[i]
            KT, V = qstate[q]
            sgt, ogst, qps, tok = groups[(q, TB)]
            qp = qps[r]
            SPv = hstate[i]["SPv"]
            csb = hstate[i]["csb"]
            nkb = 4 * (TB + 1)
            rows = slice(r * 64, (r + 1) * 64)
            po = po_ring.next()
            gs = {}

            def emit_G(jb):
                gq = pgr.next()
                K.op(K.pe, lambda: nc.tensor.matmul(gq.ap, ntri.ap, SPv[jb].ap, start=True, stop=False),
                     reads=[ntri, SPv[jb]], writes=[gq])
                K.op(K.pe, lambda: nc.tensor.matmul(gq.ap, KT.ap[:, jb * 128:(jb + 1) * 128], qp.ap, start=False, stop=False),
                     reads=[KT, qp], writes=[gq])
                K.op(K.pe, lambda: nc.tensor.matmul(gq.ap, nsel.ap[:, jb, :], csb.ap, start=False, stop=True),
                     reads=[nsel, csb], writes=[gq])
                K.pe.signal_last()
                gs[jb] = gq

            emit_G(0)
            if nkb > 1:
                emit_G(1)
            for jb in range(nkb):
                gq = gs[jb]
                wt = wt_ring.next()
                K.op(K.act, lambda: nc.scalar.activation(wt.ap, gq.ap, AF.Exp), reads=[gq], writes=[wt])
                rr = jb - 4 * TB
                if rr >= 0:
                    wm = wm_ring.next()
                    K.op(K.dve, lambda: nc.vector.tensor_tensor(wm.ap, wt.ap, masks.ap[:, rr, :], op=ALU.mult),
                         reads=[wt, masks], writes=[wm])
                    wt = wm
                if jb + 2 < nkb:
                    emit_G(jb + 2)
                K.op(K.pe, lambda: nc.tensor.matmul(po.ap, V.ap[:, jb, :], wt.ap, start=(jb == 0), stop=(jb == nkb - 1)),
                     reads=[V, wt], writes=[po])
            K.op(K.dve, lambda: nc.vector.tensor_tensor(ogst.ap[rows, :], po.ap[rows, :], sgt.ap[rows, :], op=ALU.mult),
                 reads=[po, sgt], writes=[ogst])
            if r == 1:
                K.dma(K.sp, og_scr[q * 128:(q + 1) * 128, tok], ogst.ap, reads=[ogst], writes=[od])
            del hstate[i]

        prepare(0)
        sweep1(0)
        for i in range(len(heads)):
            if i + 1 < len(heads):
                prepare(i + 1)
                sweep1(i + 1)
            sweep2(i)
        K.end_phase()
        K.stack = K.es


PARAM_SHAPES = {
    "m_norm": [1, 1024], "m_in": [1, 1024, 5152], "m_conv_w": [1, 4, 3072], "m_conv_b": [1, 3072],
    "m_dt_bias": [1, 32], "m_A_log": [1, 32], "m_D": [1, 32], "m_ynorm": [1, 2048], "m_out": [1, 2048, 1024],
    "kv_norm": [1024], "w_kv": [1024, 2048], "k_norm": [64], "s_norm": [1, 1024], "s_in": [1, 1024, 2048],
    "q_norm": [1, 64], "s_out": [1, 1024, 1024], "ple_norm": [2, 1024], "ple_gate": [2, 1024, 1024],
    "ple_proj": [2, 256, 1024],
}

SCRATCH = {
    "sz_scr": ([2048, S], BF16), "xbc_scr": ([3072, S], BF16), "dt_scr": ([S, 32], F32),
    "yn_scr": ([2048, S], BF16), "h1_scr": ([S, D], F32), "q_scr": ([1024, S], BF16),
    "sg_scr": ([1024, S], BF16), "og_scr": ([1024, S], BF16),
    "kt_scr": ([1024, S], BF16), "v_scr": ([S, 1024], BF16),
}


def build(phases=("p1a", "p1b", "p2", "p3", "p4", "p5"), ext_in=(), ext_out=(), p1b_chunks=32, p4_tb=8, p4_q=8):
    nc = bass.Bass("TRN2", target_bir_lowering=False)
    x = nc.dram_tensor("x", [S, D], F32, kind="ExternalInput").ap()
    p0 = nc.dram_tensor("p0", [S, 256], F32, kind="ExternalInput").ap()
    p1 = nc.dram_tensor("p1", [S, 256], F32, kind="ExternalInput").ap()
    prm = {k: nc.dram_tensor(k, shp, F32, kind="ExternalInput").ap() for k, shp in PARAM_SHAPES.items()}
    out = nc.dram_tensor("out", [S, D], F32, kind="ExternalOutput").ap()
    scr = {}
    for k, (shp, dt_) in SCRATCH.items():
        kind = "ExternalInput" if k in ext_in else ("ExternalOutput" if k in ext_out else "Internal")
        scr[k] = nc.dram_tensor(k, shp, dt_, kind=kind).ap()
    with ExitStack() as es:
        K = Ctx(nc, es)
        C = make_consts(K, nc, prm)
        if "p1a" in phases:
            phase_p1a(K, nc, C, x, prm["m_in"][0], scr["sz_scr"], scr["xbc_scr"], scr["dt_scr"])
        if "p1b" in phases:
            phase_p1b(K, nc, C, scr["sz_scr"], scr["xbc_scr"], scr["dt_scr"], scr["yn_scr"], nchunks=p1b_chunks)
        if "p2" in phases:
            phase_out_ple(K, nc, C, "p2", scr["yn_scr"], 16, prm["m_out"][0], x, p0, C.g_p0,
                          prm["ple_gate"][0], prm["ple_proj"][0], scr["h1_scr"])
        if "p3" in phases:
            phase_p3(K, nc, C, scr["h1_scr"], prm["w_kv"], prm["s_in"][0], scr["kt_scr"], scr["v_scr"], scr["q_scr"], scr["sg_scr"])
        if "p4" in phases:
            phase_p4(K, nc, C, scr["kt_scr"], scr["v_scr"], scr["q_scr"], scr["sg_scr"], scr["og_scr"], nTB=p4_tb, nQ=p4_q)
        if "p5" in phases:
            phase_out_ple(K, nc, C, "p5", scr["og_scr"], 8, prm["s_out"][0], scr["h1_scr"], p1, C.g_p1,
                          prm["ple_gate"][1], prm["ple_proj"][1], out)
        K.finish()
    return nc


_NC_CACHE = {}


def kernel(**inputs):
    if "full" not in _NC_CACHE:
        _NC_CACHE["full"] = build()
    nc = _NC_CACHE["full"]
    x = np.asarray(inputs["x"], dtype=np.float32)
    p = np.asarray(inputs["p"], dtype=np.float32)
    in_maps = []
    for b in range(NCORES):
        m = {"x": np.ascontiguousarray(x[b]), "p0": np.ascontiguousarray(p[0, b]), "p1": np.ascontiguousarray(p[1, b])}
        for k in PARAM_SHAPES:
            m[k] = np.ascontiguousarray(np.asarray(inputs[k], dtype=np.float32))
        in_maps.append(m)
    res = run_bass_kernel_spmd(nc, in_maps, core_ids=list(range(NCORES)))
    return np.stack([np.asarray(res.results[b]["out"], dtype=np.float32) for b in range(NCORES)], axis=0)
```

```python
from bisect import bisect_left
from contextlib import ExitStack
import os
import numpy as np
import concourse.bass as bass
import concourse.mybir as mybir
from concourse.alu_op_type import AluOpType as ALU
from concourse.bass_utils import run_bass_kernel_spmd

F32 = mybir.dt.float32
BF16 = mybir.dt.bfloat16
AF = mybir.ActivationFunctionType

S = 4096
D = 1024
_DBG_STOP = int(os.environ.get('P1B_STOP', '99'))
NCORES = 8
EPS = 1e-6


class Eng:
    def __init__(self, name, eng, sem, eager):
        self.name, self.eng, self.sem, self.eager = name, eng, sem, eager
        self.n = 0
        self.count = 0
        self.sig_idx = []
        self.sig_cnt = []
        self.last = None
        self.last_signaled = True
        self.known = {}

    def signal_last(self):
        if not self.last_signaled:
            self.last.then_inc(self.sem, 1)
            self.count += 1
            self.sig_idx.append(self.n)
            self.sig_cnt.append(self.count)
            self.last_signaled = True

    def count_for(self, idx):
        i = bisect_left(self.sig_idx, idx)
        if i == len(self.sig_idx):
            self.signal_last()
            i = len(self.sig_idx) - 1
        assert self.sig_idx[i] >= idx
        return self.sig_cnt[i]


class DSem:
    def __init__(self, sem):
        self.sem = sem
        self.issued = 0


class T:
    def __init__(self, name, ap, space):
        self.name, self.ap, self.space = name, ap, space
        self.w = None
        self.r = {}
        self.dsem = None

    def __getitem__(self, k):
        return self.ap[k]


class Ring:
    def __init__(self, tiles):
        self.t = tiles
        self.i = 0

    def next(self):
        t = self.t[self.i % len(self.t)]
        self.i += 1
        return t


class Ctx:
    def __init__(self, nc, es):
        self.nc, self.es = nc, es
        mk = lambda n: es.enter_context(nc.semaphore(n))
        self.pe = Eng("pe", nc.tensor, mk("s_pe"), False)
        self.act = Eng("act", nc.scalar, mk("s_act"), True)
        self.dve = Eng("dve", nc.vector, mk("s_dve"), True)
        self.pool = Eng("pool", nc.gpsimd, mk("s_pool"), True)
        self.sp = Eng("sp", nc.sync, mk("s_sp"), True)
        self.engs = [self.pe, self.act, self.dve, self.pool, self.sp]
        self.dsems = []
        self.free_dsems = []
        self.nsem = 0
        self.stack = es

    def sb(self, name, shape, dtype, es=None):
        t = (es or self.stack).enter_context(self.nc.sbuf_tensor(name, shape, dtype))
        return T(name, t.ap(), "sb")

    def ps(self, name, shape, dtype, es=None):
        t = (es or self.stack).enter_context(self.nc.psum_tensor(name, shape, dtype))
        return T(name, t.ap(), "ps")

    def dram(self, name, ap):
        return T(name, ap, "dram")

    def view(self, name, ap, space="sb"):
        return T(name, ap, space)

    def begin_phase(self):
        self.phase_dsems = []

    def end_phase(self):
        self.barrier()
        self.free_dsems.extend(self.phase_dsems)
        self.phase_dsems = None

    def new_dsem(self):
        if self.free_dsems:
            d = self.free_dsems.pop()
        else:
            self.nsem += 1
            d = DSem(self.es.enter_context(self.nc.semaphore("s_d%d" % self.nsem)))
            self.dsems.append(d)
        if getattr(self, "phase_dsems", None) is not None:
            self.phase_dsems.append(d)
        return d

    def _waits(self, E, reads, writes):
        need = {}

        def add(ev, same_ok):
            if ev is None:
                return
            if ev[0] == "e":
                Dn, idx = ev[1], ev[2]
                if Dn is E and same_ok:
                    return
                c = Dn.count_for(idx)
                sem = Dn.sem
            else:
                c = ev[1].issued * 16
                sem = ev[1].sem
            key = id(sem)
            if need.get(key, (None, 0))[1] < c:
                need[key] = (sem, c)

        for t in reads:
            add(t.w, E is self.pe)
        for t in writes:
            add(t.w, True)
            for ev in t.r.values():
                add(ev, True)
        for key, (sem, c) in need.items():
            if E.known.get(key, 0) >= c:
                continue
            E.eng.wait_ge(sem, c)
            E.known[key] = c

    def op(self, E, make, reads=(), writes=()):
        writes = list(writes) + [t for t in reads if t.space == "ps"]
        reads = [t for t in reads if t.space != "ps"]
        self._waits(E, reads, writes)
        inst = make()
        E.n += 1
        E.last = inst
        E.last_signaled = False
        if E.eager:
            E.signal_last()
        ev = ("e", E, E.n)
        for t in reads:
            t.r[id(E)] = ev
        for t in writes:
            t.w = ev
            t.r = {}
        return inst

    def dma(self, Q, out_ap, in_ap, reads=(), writes=(), dsem=None, **kw):
        self._waits(Q, reads, writes)
        if dsem is None:
            cand = ([t for t in writes if t.space != "dram"] or [t for t in reads if t.space != "dram"]
                    or list(writes) or list(reads))
            t0 = cand[0]
            if t0.dsem is None:
                t0.dsem = self.new_dsem()
            dsem = t0.dsem
        inst = Q.eng.dma_start(out=out_ap, in_=in_ap, **kw)
        inst.then_inc(dsem.sem, 16)
        dsem.issued += 1
        ev = ("d", dsem)
        for t in reads:
            t.r[id(dsem)] = ev
        for t in writes:
            t.w = ev
            t.r = {}
        return inst

    def barrier(self):
        for E in self.engs:
            E.signal_last()
        for E in self.engs:
            for Dn in self.engs:
                if Dn is E or Dn.count == 0:
                    continue
                key = id(Dn.sem)
                if E.known.get(key, 0) < Dn.count:
                    E.eng.wait_ge(Dn.sem, Dn.count)
                    E.known[key] = Dn.count
            for d in self.dsems:
                if d.issued:
                    key = id(d.sem)
                    if E.known.get(key, 0) < d.issued * 16:
                        E.eng.wait_ge(d.sem, d.issued * 16)
                        E.known[key] = d.issued * 16

    def finish(self):
        for d in self.dsems:
            if d.issued:
                self.sp.eng.wait_ge(d.sem, d.issued * 16)


class Consts:
    pass


def make_consts(K, nc, prm):
    C = Consts()
    onesf = K.sb("c_onesf", [128, 128], F32)
    K.op(K.pool, lambda: nc.gpsimd.memset(onesf.ap, 1.0), writes=[onesf])
    C.onesf = onesf
    identf = K.sb("c_identf", [128, 128], F32)
    K.op(K.pool, lambda: nc.gpsimd.affine_select(identf.ap, onesf.ap, pattern=[[1, 128]], compare_op=ALU.is_equal,
                                                 fill=0.0, base=0, channel_multiplier=-1), reads=[onesf], writes=[identf])
    C.identf = identf
    identb = K.sb("c_identb", [128, 128], BF16)
    K.op(K.pool, lambda: nc.gpsimd.tensor_copy(identb.ap, identf.ap), reads=[identf], writes=[identb])
    C.identb = identb
    onesb = K.sb("c_onesb", [128, 128], BF16)
    K.op(K.pool, lambda: nc.gpsimd.memset(onesb.ap, 1.0), writes=[onesb])
    C.onesb = onesb
    tri = K.sb("c_tri", [128, 128], F32)
    K.op(K.pool, lambda: nc.gpsimd.affine_select(tri.ap, onesf.ap, pattern=[[1, 128]], compare_op=ALU.is_ge,
                                                 fill=0.0, base=0, channel_multiplier=-1), reads=[onesf], writes=[tri])
    C.tri = tri
    epsc = K.sb("c_eps", [128, 1], F32)
    K.op(K.pool, lambda: nc.gpsimd.memset(epsc.ap, EPS), writes=[epsc])
    C.eps = epsc
    onec = K.sb("c_one", [128, 1], F32)
    K.op(K.pool, lambda: nc.gpsimd.memset(onec.ap, 1.0), writes=[onec])
    C.one = onec
    blk = K.sb("c_blk", [128, 128], BF16)
    K.op(K.pool, lambda: nc.gpsimd.memset(blk.ap, 0.0), writes=[blk])
    K.op(K.pool, lambda: nc.gpsimd.memset(blk.ap[0:64, 0:64], 1.0), writes=[blk])
    K.op(K.pool, lambda: nc.gpsimd.memset(blk.ap[64:128, 64:128], 1.0), writes=[blk])
    C.blk = blk

    def vec_cols(name, items):
        R = sum(n for _, n in items)
        st = K.sb("vst_" + name, [R, 128], F32)
        r0 = 0
        for ap1, n in items:
            K.dma(K.sp, st.ap[r0:r0 + n, :], ap1.rearrange("(c p) -> c p", p=128), writes=[st])
            r0 += n
        pt = K.ps("vps_" + name, [128, 512], F32)
        K.op(K.pe, lambda: nc.tensor.transpose(pt.ap[:, 0:R], st.ap, identf.ap[0:R, 0:R]), reads=[st, identf], writes=[pt])
        out = K.sb("vec_" + name, [128, R], F32)
        K.op(K.dve, lambda: nc.vector.tensor_copy(out.ap, pt.ap[:, 0:R]), reads=[pt], writes=[out])
        return out

    veca = K.sb("c_veca", [128, 80], F32)
    vecb = K.sb("c_vecb", [128, 96], F32)
    with ExitStack() as es2:
        K.stack = es2
        va = vec_cols("a", [(prm["m_norm"][0], 8), (prm["kv_norm"], 8), (prm["s_norm"][0], 8),
                            (prm["ple_norm"][0], 8), (prm["ple_norm"][1], 8), (prm["m_ynorm"][0], 16),
                            (prm["m_conv_b"][0], 24)])
        vb = vec_cols("b", [(prm["m_conv_w"][0, j], 24) for j in range(4)])
        K.op(K.dve, lambda: nc.vector.tensor_copy(veca.ap, va.ap), reads=[va], writes=[veca])
        K.op(K.dve, lambda: nc.vector.tensor_copy(vecb.ap, vb.ap), reads=[vb], writes=[vecb])
        K.barrier()
        K.stack = K.es
    C.g_m, C.g_kv, C.g_s = veca.ap[:, 0:8], veca.ap[:, 8:16], veca.ap[:, 16:24]
    C.g_p0, C.g_p1 = veca.ap[:, 24:32], veca.ap[:, 32:40]
    C.g_y, C.conv_b = veca.ap[:, 40:56], veca.ap[:, 56:80]
    C.conv_w = vecb.ap
    C.veca, C.vecb = veca, vecb

    kq = K.sb("c_kq", [128, 2], F32)
    for half in range(2):
        K.dma(K.sp, kq.ap[half * 64:(half + 1) * 64, 0:1], prm["k_norm"].rearrange("(p o) -> p o", o=1), writes=[kq])
        K.dma(K.sp, kq.ap[half * 64:(half + 1) * 64, 1:2], prm["q_norm"][0].rearrange("(p o) -> p o", o=1), writes=[kq])
    kq2 = K.sb("c_kq2", [128, 2], F32)
    K.op(K.dve, lambda: nc.vector.tensor_copy(kq2.ap[:, 0:1], kq.ap[:, 0:1]), reads=[kq], writes=[kq2])
    K.op(K.dve, lambda: nc.vector.tensor_scalar(kq2.ap[:, 1:2], kq.ap[:, 1:2], 0.125, None, op0=ALU.mult), reads=[kq], writes=[kq2])
    C.kq = kq2
    bc = K.sb("c_bc", [128, 3, 32], F32)
    K.dma(K.sp, bc.ap[:, 0, :], prm["m_dt_bias"][0:1, :].to_broadcast([128, 32]), writes=[bc])
    K.dma(K.sp, bc.ap[:, 1, :], prm["m_A_log"][0:1, :].to_broadcast([128, 32]), writes=[bc])
    K.dma(K.sp, bc.ap[:, 2, :], prm["m_D"][0:1, :].to_broadcast([128, 32]), writes=[bc])
    C.bc = bc
    Abc = K.sb("c_A", [128, 32], F32)
    K.op(K.act, lambda: nc.scalar.activation(Abc.ap, bc.ap[:, 1, :], AF.Exp), reads=[bc], writes=[Abc])
    K.op(K.dve, lambda: nc.vector.tensor_scalar(Abc.ap, Abc.ap, -1.0, None, op0=ALU.mult), reads=[Abc], writes=[Abc])
    C.A = Abc
    Dcol = K.sb("c_Dcol", [128, 16], F32)
    dv = bc.ap[:, 2, :].rearrange("p (q r) -> p q r", r=2)
    K.op(K.dve, lambda: nc.vector.tensor_copy(Dcol.ap[0:64, :], dv[0:64, :, 0]), reads=[bc], writes=[Dcol])
    K.op(K.dve, lambda: nc.vector.tensor_copy(Dcol.ap[64:128, :], dv[64:128, :, 1]), reads=[bc], writes=[Dcol])
    C.Dcol = Dcol
    return C


def load_weight(K, nc, es, name, src, C_, N, stage_ring, cast_engs):
    w = K.sb(name, [128, C_, N], BF16, es)
    views = [K.view("%s_%d" % (name, c), w.ap[:, c, :]) for c in range(C_)]
    SW = stage_ring.t[0].ap.shape[1]
    i = 0
    for c in range(C_):
        for n0 in range(0, N, SW):
            n1 = min(N, n0 + SW)
            st = stage_ring.next()
            K.dma(K.sp, st.ap[:, 0:n1 - n0], src[c * 128:(c + 1) * 128, n0:n1], writes=[st])
            E = cast_engs[i % len(cast_engs)]
            i += 1
            if E is K.act:
                K.op(E, lambda: nc.scalar.copy(views[c].ap[:, n0:n1], st.ap[:, 0:n1 - n0]), reads=[st], writes=[views[c]])
            else:
                K.op(E, lambda: E.eng.tensor_copy(views[c].ap[:, n0:n1], st.ap[:, 0:n1 - n0]), reads=[st], writes=[views[c]])
    return views


def rms_to_xT(K, nc, C, ht, na, ss, lnv, rstd, junk, xh, ptr_ring, outs):
    for a in range(na):
        K.op(K.act, lambda: nc.scalar.activation(junk.ap, ht.ap[:, a, :], AF.Square, accum_out=ss.ap[:, a:a + 1]),
             reads=[ht], writes=[junk, ss])
    K.op(K.act, lambda: nc.scalar.copy(junk.ap[:, 0:8], junk.ap[:, 8:16]), reads=[junk], writes=[junk])
    K.op(K.act, lambda: nc.scalar.activation(lnv.ap[:, 0:na], ss.ap[:, 0:na], AF.Ln, bias=C.eps.ap, scale=1.0 / D),
         reads=[ss, C.eps], writes=[lnv])
    K.op(K.act, lambda: nc.scalar.activation(rstd.ap[:, 0:na], lnv.ap[:, 0:na], AF.Exp, scale=-0.5), reads=[lnv], writes=[rstd])
    for a in range(na):
        K.op(K.dve, lambda: nc.vector.tensor_scalar(xh.ap[:, a, :], ht.ap[:, a, :], rstd.ap[:, a:a + 1], None, op0=ALU.mult),
             reads=[ht, rstd], writes=[xh])
    k = 0
    for c in range(8):
        pt = ptr_ring.next()
        for a in range(na):
            K.op(K.pe, lambda: nc.tensor.transpose(pt.ap[:, a * 128:(a + 1) * 128], xh.ap[:, a, c * 128:(c + 1) * 128], C.identb.ap),
                 reads=[xh, C.identb], writes=[pt])
        for (xT, g) in outs:
            if k % 2 == 0:
                K.op(K.act, lambda: nc.scalar.activation(xT.ap[:, c, :], pt.ap[:, 0:na * 128], AF.Identity, scale=g[:, c:c + 1]),
                     reads=[pt, C.veca], writes=[xT])
            else:
                K.op(K.dve, lambda: nc.vector.tensor_scalar(xT.ap[:, c, :], pt.ap[:, 0:na * 128], g[:, c:c + 1], None, op0=ALU.mult),
                     reads=[pt, C.veca], writes=[xT])
            k += 1


def phase_p1a(K, nc, C, x_d, w_in, sz_scr, xbc_scr, dt_scr):
    with ExitStack() as es:
        K.stack = es
        K.begin_phase()
        stage = Ring([K.sb("p1a_st%d" % i, [128, 1288], F32) for i in range(2)])
        W = load_weight(K, nc, es, "p1a_w", w_in, 8, 5152, stage, [K.pool, K.dve, K.act])
        xt_ring = Ring([K.sb("p1a_x%d" % i, [128, 4, 1024], F32) for i in range(1)])
        ss = K.sb("p1a_ss", [128, 4], F32)
        lnv = K.sb("p1a_ln", [128, 4], F32)
        rstd = K.sb("p1a_rstd", [128, 4], F32)
        junk = K.sb("p1a_junk", [128, 1024], BF16)
        xh = K.sb("p1a_xh", [128, 4, 1024], BF16)
        xT = K.sb("p1a_xT", [128, 8, 512], BF16)
        ptr = Ring([K.ps("p1a_pt%d" % i, [128, 1024], BF16) for i in range(2)])
        pmm = Ring([K.ps("p1a_pm%d" % i, [128, 512], F32) for i in range(4)])
        pdt = K.ps("p1a_pdt", [128, 512], F32)
        ost_ring = Ring([K.sb("p1a_ost%d" % i, [128, 512], BF16) for i in range(6)])
        raw_ring = Ring([K.sb("p1a_raw%d" % i, [128, 515], F32) for i in range(3)])
        acc_ring = Ring([K.sb("p1a_acc%d" % i, [128, 512], F32) for i in range(2)])
        halo = K.sb("p1a_halo", [128, 24, 3], F32)
        halos = [K.view("p1a_halo%d" % i, halo.ap[:, i, :]) for i in range(24)]
        K.op(K.pool, lambda: nc.gpsimd.memset(halo.ap, 0.0), writes=halos)
        dtx = K.sb("p1a_dtx", [128, 4, 32], F32)
        dta = K.sb("p1a_dta", [128, 4, 32], F32)
        dte = K.sb("p1a_dte", [128, 4, 32], F32)
        dtm = K.sb("p1a_dtm", [128, 4, 32], F32)
        dto = K.sb("p1a_dto", [128, 4, 32], F32)
        szd = K.dram("sz_scr", sz_scr)
        xbd = K.dram("xbc_scr", xbc_scr)
        dtd = K.dram("dt_scr", dt_scr)
        xv = x_d.rearrange("(t a p) d -> t p a d", a=4, p=128)
        for ts in range(8):
            xt = xt_ring.next()
            K.dma(K.sp, xt.ap, xv[ts], writes=[xt])
            rms_to_xT(K, nc, C, xt, 4, ss, lnv, rstd, junk, xh, ptr, [(xT, C.g_m)])
            tok = slice(ts * 512, (ts + 1) * 512)
            for ft in range(40):
                pm = pmm.next()
                for c in range(8):
                    K.op(K.pe, lambda: nc.tensor.matmul(pm.ap, W[c].ap[:, ft * 128:(ft + 1) * 128], xT.ap[:, c, :],
                                                        start=(c == 0), stop=(c == 7)), reads=[W[c], xT], writes=[pm])
                ost = ost_ring.next()
                if ft < 16:
                    K.op(K.act, lambda: nc.scalar.activation(ost.ap, pm.ap, AF.Silu), reads=[pm], writes=[ost])
                    K.dma(K.sp, sz_scr[ft * 128:(ft + 1) * 128, tok], ost.ap, reads=[ost], writes=[szd])
                else:
                    ci = ft - 16
                    raw = raw_ring.next()
                    acc = acc_ring.next()
                    K.op(K.pool, lambda: nc.gpsimd.tensor_copy(raw.ap[:, 0:3], halos[ci].ap), reads=[halos[ci]], writes=[raw])
                    K.op(K.act, lambda: nc.scalar.copy(raw.ap[:, 3:515], pm.ap), reads=[pm], writes=[raw])
                    K.op(K.pool, lambda: nc.gpsimd.tensor_copy(halos[ci].ap, raw.ap[:, 512:515]), reads=[raw], writes=[halos[ci]])
                    cw = lambda j: C.conv_w[:, j * 24 + ci:j * 24 + ci + 1]
                    K.op(K.dve, lambda: nc.vector.tensor_scalar(acc.ap, raw.ap[:, 0:512], cw(0), None, op0=ALU.mult),
                         reads=[raw, C.vecb], writes=[acc])
                    for j in range(1, 4):
                        K.op(K.dve, lambda: nc.vector.scalar_tensor_tensor(acc.ap, raw.ap[:, j:j + 512], cw(j), acc.ap,
                                                                           op0=ALU.mult, op1=ALU.add),
                             reads=[raw, acc, C.vecb], writes=[acc])
                    K.op(K.act, lambda: nc.scalar.activation(ost.ap, acc.ap, AF.Silu, bias=C.conv_b[:, ci:ci + 1]),
                         reads=[acc, C.veca], writes=[ost])
                    K.dma(K.sp, xbc_scr[ci * 128:(ci + 1) * 128, tok], ost.ap, reads=[ost], writes=[xbd])
            for a in range(4):
                for c in range(8):
                    K.op(K.pe, lambda: nc.tensor.matmul(pdt.ap[:, a * 32:(a + 1) * 32], xT.ap[:, c, a * 128:(a + 1) * 128],
                                                        W[c].ap[:, 5120:5152], start=(c == 0), stop=(c == 7)),
                         reads=[W[c], xT], writes=[pdt])
            pv = pdt.ap[:, 0:128].rearrange("p (a h) -> p a h", a=4)
            K.op(K.dve, lambda: nc.vector.tensor_tensor(dtx.ap, pv, C.bc.ap[:, 0:1, :].to_broadcast([128, 4, 32]), op=ALU.add),
                 reads=[pdt, C.bc], writes=[dtx])
            K.op(K.act, lambda: nc.scalar.activation(dta.ap, dtx.ap, AF.Abs), reads=[dtx], writes=[dta])
            K.op(K.act, lambda: nc.scalar.activation(dte.ap, dta.ap, AF.Exp, scale=-1.0), reads=[dta], writes=[dte])
            K.op(K.act, lambda: nc.scalar.activation(dte.ap, dte.ap, AF.Ln, bias=C.one.ap, scale=1.0), reads=[dte, C.one], writes=[dte])
            K.op(K.dve, lambda: nc.vector.tensor_scalar(dtm.ap, dtx.ap, 0.0, None, op0=ALU.max), reads=[dtx], writes=[dtm])
            K.op(K.dve, lambda: nc.vector.tensor_tensor(dto.ap, dtm.ap, dte.ap, op=ALU.add), reads=[dtm, dte], writes=[dto])
            K.dma(K.sp, dt_scr.rearrange("(t a p) h -> t p a h", a=4, p=128)[ts], dto.ap, reads=[dto], writes=[dtd])
        K.end_phase()
        K.stack = K.es


def phase_p1b(K, nc, C, sz_scr, xbc_scr, dt_scr, yn_scr, nchunks=32):
    with ExitStack() as es:
        K.stack = es
        K.begin_phase()
        xb_ring = Ring([K.sb("p1b_xb%d" % i, [128, 24, 128], BF16) for i in range(2)])
        sz_ring = Ring([K.sb("p1b_sz%d" % i, [128, 16, 128], BF16) for i in range(2)])
        dt_ring = Ring([K.sb("p1b_dt%d" % i, [128, 32], F32) for i in range(2)])
        a_ch = K.sb("p1b_a", [128, 32], F32)
        acum = K.sb("p1b_acum", [128, 32], F32)
        tmpw = K.sb("p1b_tmpw", [128, 32], F32)
        wend = K.sb("p1b_wend", [128, 32], F32)
        dtw = K.sb("p1b_dtw", [128, 32], F32)
        dA = K.sb("p1b_dA", [128, 32], F32)
        xdt_pad = K.sb("p1b_xdtp", [128, 32, 128], BF16)
        xdtw = K.sb("p1b_xdtw", [128, 32, 64], BF16)
        btok = K.sb("p1b_btok", [128, 4, 128], BF16)
        state = K.sb("p1b_state", [128, 32, 64], F32)
        st_pad = K.sb("p1b_stp", [128, 32, 128], BF16)
        cbm = K.sb("p1b_cbm", [128, 4, 128], F32)
        seg_ring = Ring([K.sb("p1b_seg%d" % i, [128, 4, 128], F32) for i in range(2)])
        dec_ring = Ring([K.sb("p1b_dec%d" % i, [128, 4, 128], F32) for i in range(2)])
        ea_ring = Ring([K.sb("p1b_ea%d" % i, [128, 4, 128], F32) for i in range(2)])
        mt_ring = Ring([K.sb("p1b_mt%d" % i, [128, 4, 128], BF16) for i in range(3)])
        cs_ring = Ring([K.sb("p1b_cs%d" % i, [128, 4, 128], BF16) for i in range(3)])
        ytmp_ring = Ring([K.sb("p1b_yt%d" % i, [128, 128], F32) for i in range(2)])
        ygate = K.sb("p1b_yg", [128, 16, 128], F32)
        sq = K.sb("p1b_sq", [128, 16, 128], BF16)
        grs = K.sb("p1b_grs", [128, 4, 128], F32)
        yn_ring = Ring([K.sb("p1b_yn%d" % i, [128, 16, 128], BF16) for i in range(2)])
        pxs = K.ps("p1b_pxs", [128, 2048], BF16)
        pb = K.ps("p1b_pb", [128, 1024], BF16)
        psm = K.ps("p1b_psm", [128, 512], F32)
        pabc = Ring([K.ps("p1b_pabc%d" % i, [128, 512], F32) for i in range(2)])
        py_ring = Ring([K.ps("p1b_py%d" % i, [128, 512], F32) for i in range(2)])
        pupd = psm
        for t_ in (xdt_pad, st_pad, state):
            K.op(K.pool, lambda: nc.gpsimd.memset(t_.ap, 0.0), writes=[t_])
        szd = K.dram("sz_scr", sz_scr)
        xbd = K.dram("xbc_scr", xbc_scr)
        dtd = K.dram("dt_scr", dt_scr)
        ynd = K.dram("yn_scr", yn_scr)
        xbv = xbc_scr.rearrange("(q p) t -> p q t", p=128)
        szv = sz_scr.rearrange("(q p) t -> p q t", p=128)
        ynv = yn_scr.rearrange("(q p) t -> p q t", p=128)
        xdt4 = xdt_pad.ap.rearrange("p (q r) c -> p q r c", r=2)
        stp4 = st_pad.ap.rearrange("p (q r) c -> p q r c", r=2)
        for ch in range(nchunks):
            cols = slice(ch * 128, (ch + 1) * 128)
            xb = xb_ring.next()
            sz = sz_ring.next()
            dt = dt_ring.next()
            K.dma(K.sp, xb.ap, xbv[:, :, cols], reads=[xbd], writes=[xb])
            K.dma(K.sp, sz.ap, szv[:, :, cols], reads=[szd], writes=[sz])
            K.dma(K.sp, dt.ap, dt_scr[ch * 128:(ch + 1) * 128, :], reads=[dtd], writes=[dt])
            if _DBG_STOP <= 1:
                continue
            K.op(K.dve, lambda: nc.vector.tensor_tensor(a_ch.ap, dt.ap, C.A.ap, op=ALU.mult), reads=[dt, C.A], writes=[a_ch])
            K.op(K.pe, lambda: nc.tensor.matmul(psm.ap[:, 0:32], C.tri.ap, a_ch.ap, start=True, stop=True),
                 reads=[C.tri, a_ch], writes=[psm])
            K.op(K.pe, lambda: nc.tensor.matmul(psm.ap[:, 32:64], C.onesf.ap, a_ch.ap, start=True, stop=True),
                 reads=[C.onesf, a_ch], writes=[psm])
            K.op(K.act, lambda: nc.scalar.copy(acum.ap, psm.ap[:, 0:32]), reads=[psm], writes=[acum])
            K.op(K.dve, lambda: nc.vector.tensor_tensor(tmpw.ap, psm.ap[:, 32:64], acum.ap, op=ALU.subtract),
                 reads=[psm, acum], writes=[tmpw])
            K.op(K.act, lambda: nc.scalar.activation(wend.ap, tmpw.ap, AF.Exp), reads=[tmpw], writes=[wend])
            K.op(K.act, lambda: nc.scalar.activation(dA.ap, psm.ap[:, 32:64], AF.Exp), reads=[psm], writes=[dA])
            K.op(K.dve, lambda: nc.vector.tensor_tensor(dtw.ap, dt.ap, wend.ap, op=ALU.mult), reads=[dt, wend], writes=[dtw])
            if _DBG_STOP <= 2:
                continue
            for ci in range(16):
                K.op(K.pe, lambda: nc.tensor.transpose(pxs.ap[:, ci * 128:(ci + 1) * 128], xb.ap[:, ci, :], C.identb.ap),
                     reads=[xb, C.identb], writes=[pxs])
            for g in range(4):
                K.op(K.pe, lambda: nc.tensor.transpose(pb.ap[:, g * 128:(g + 1) * 128], xb.ap[:, 16 + g, :], C.identb.ap),
                     reads=[xb, C.identb], writes=[pb])
            pxs4 = pxs.ap.rearrange("p (q r c) -> p q r c", r=2, c=64)
            dt3 = dt.ap.rearrange("p (q r) -> p q r", r=2)
            for r in range(2):
                K.op(K.dve, lambda: nc.vector.tensor_tensor(xdt4[:, :, r, r * 64:(r + 1) * 64], pxs4[:, :, r, :],
                                                            dt3[:, :, r:r + 1].to_broadcast([128, 16, 64]), op=ALU.mult),
                     reads=[pxs, dt], writes=[xdt_pad])
            K.op(K.dve, lambda: nc.vector.tensor_tensor(xdtw.ap, pxs.ap.rearrange("p (h c) -> p h c", c=64),
                                                        dtw.ap.unsqueeze(2).to_broadcast([128, 32, 64]), op=ALU.mult),
                 reads=[pxs, dtw], writes=[xdtw])
            K.op(K.act, lambda: nc.scalar.copy(btok.ap.rearrange("p g n -> p (g n)"), pb.ap[:, 0:512]), reads=[pb], writes=[btok])
            if _DBG_STOP <= 3:
                continue
            for g in range(4):
                K.op(K.pe, lambda: nc.tensor.matmul(pupd.ap[:, g * 128:(g + 1) * 128],
                                                    xb.ap[:, 16 + g, :], xb.ap[:, 20 + g, :], start=True, stop=True),
                     reads=[xb], writes=[pupd])
            for g in range(4):
                K.op(K.dve, lambda: nc.vector.tensor_tensor(cbm.ap[:, g, :], pupd.ap[:, g * 128:(g + 1) * 128], C.tri.ap, op=ALU.mult),
                     reads=[pupd, C.tri], writes=[cbm])
            if _DBG_STOP <= 4:
                continue
            mts = {}
            css = {}
            pas = {}

            def emit_abc(hq):
                pa = pabc.next()
                for i in range(4):
                    h = hq * 4 + i
                    K.op(K.pe, lambda: nc.tensor.matmul(pa.ap[:, i * 128:(i + 1) * 128], a_ch.ap[:, h:h + 1].to_broadcast([128, 128]),
                                                        C.tri.ap, start=True, stop=True), reads=[a_ch, C.tri], writes=[pa])
                K.pe.signal_last()
                pas[hq] = pa

            emit_abc(0)
            for hq in range(8):
                g = hq // 2
                pa = pas[hq]
                seg = seg_ring.next()
                dec = dec_ring.next()
                ea = ea_ring.next()
                mt = mt_ring.next()
                cs = cs_ring.next()
                for i in range(4):
                    h = hq * 4 + i
                    K.op(K.dve, lambda: nc.vector.tensor_scalar(seg.ap[:, i, :], pa.ap[:, i * 128:(i + 1) * 128], acum.ap[:, h:h + 1], 0.0,
                                                                op0=ALU.subtract, op1=ALU.min), reads=[pa, acum], writes=[seg])
                K.op(K.act, lambda: nc.scalar.activation(dec.ap, seg.ap, AF.Exp), reads=[seg], writes=[dec])
                K.op(K.act, lambda: nc.scalar.activation(ea.ap.rearrange("p i l -> p (i l)"), pa.ap, AF.Exp), reads=[pa], writes=[ea])
                for i in range(4):
                    K.op(K.pool, lambda: nc.gpsimd.tensor_tensor(mt.ap[:, i, :], dec.ap[:, i, :], cbm.ap[:, g, :], op=ALU.mult),
                         reads=[dec, cbm], writes=[mt])
                    K.op(K.pool, lambda: nc.gpsimd.tensor_tensor(cs.ap[:, i, :], ea.ap[:, i, :], xb.ap[:, 20 + g, :], op=ALU.mult),
                         reads=[ea, xb], writes=[cs])
                if hq + 1 < 8:
                    emit_abc(hq + 1)
                for pi in range(2):
                    q = hq * 2 + pi
                    pyq = py_ring.next()
                    yo = pyq.ap[:, 0:128]
                    ops = [(xdt_pad, xdt_pad.ap[:, 2 * q, :], mt, mt.ap[:, 2 * pi, :]),
                           (xdt_pad, xdt_pad.ap[:, 2 * q + 1, :], mt, mt.ap[:, 2 * pi + 1, :]),
                           (st_pad, st_pad.ap[:, 2 * q, :], cs, cs.ap[:, 2 * pi, :]),
                           (st_pad, st_pad.ap[:, 2 * q + 1, :], cs, cs.ap[:, 2 * pi + 1, :])]
                    for k_, (lt, la, rt, ra) in enumerate(ops):
                        K.op(K.pe, lambda: nc.tensor.matmul(yo, la, ra, start=(k_ == 0), stop=(k_ == 3)), reads=[lt, rt], writes=[pyq])
                    yt = ytmp_ring.next()
                    K.op(K.dve, lambda: nc.vector.scalar_tensor_tensor(yt.ap, xb.ap[:, q, :], C.Dcol.ap[:, q:q + 1], yo,
                                                                       op0=ALU.mult, op1=ALU.add), reads=[xb, C.Dcol, pyq], writes=[yt])
                    K.op(K.pool, lambda: nc.gpsimd.tensor_tensor(ygate.ap[:, q, :], yt.ap, sz.ap[:, q, :], op=ALU.mult),
                         reads=[yt, sz], writes=[ygate])
            if _DBG_STOP <= 5:
                continue
            for g in range(4):
                K.op(K.pe, lambda: nc.tensor.matmul(pupd.ap, btok.ap[:, g, :], xdtw.ap[:, g * 8:(g + 1) * 8, :].rearrange("p h c -> p (h c)"),
                                                    start=True, stop=True), reads=[btok, xdtw], writes=[pupd])
                sv = state.ap[:, g * 8:(g + 1) * 8, :]
                K.op(K.dve, lambda: nc.vector.tensor_tensor(sv, sv, dA.ap[:, g * 8:(g + 1) * 8].unsqueeze(2).to_broadcast([128, 8, 64]),
                                                            op=ALU.mult), reads=[state, dA], writes=[state])
                K.op(K.dve, lambda: nc.vector.tensor_tensor(sv, sv, pupd.ap.rearrange("p (h c) -> p h c", c=64), op=ALU.add),
                     reads=[state, pupd], writes=[state])
            st4 = state.ap.rearrange("p (q r) c -> p q r c", r=2)
            for r in range(2):
                K.op(K.act, lambda: nc.scalar.copy(stp4[:, :, r, r * 64:(r + 1) * 64], st4[:, :, r, :]), reads=[state], writes=[st_pad])
            if _DBG_STOP <= 6:
                continue
            K.op(K.act, lambda: nc.scalar.activation(sq.ap, ygate.ap, AF.Square), reads=[ygate], writes=[sq])
            for G in range(4):
                for k_ in range(4):
                    K.op(K.pe, lambda: nc.tensor.matmul(psm.ap[:, G * 128:(G + 1) * 128], C.onesb.ap, sq.ap[:, G * 4 + k_, :],
                                                        start=(k_ == 0), stop=(k_ == 3)), reads=[C.onesb, sq], writes=[psm])
            K.op(K.act, lambda: nc.scalar.activation(grs.ap.rearrange("p g l -> p (g l)"), psm.ap, AF.Ln, bias=C.eps.ap, scale=1.0 / 512),
                 reads=[psm, C.eps], writes=[grs])
            K.op(K.act, lambda: nc.scalar.activation(grs.ap, grs.ap, AF.Exp, scale=-0.5), reads=[grs], writes=[grs])
            yn = yn_ring.next()
            for q in range(16):
                K.op(K.dve, lambda: nc.vector.scalar_tensor_tensor(yn.ap[:, q, :], ygate.ap[:, q, :], C.g_y[:, q:q + 1], grs.ap[:, q // 4, :],
                                                                   op0=ALU.mult, op1=ALU.mult), reads=[ygate, C.veca, grs], writes=[yn])
            K.dma(K.sp, ynv[:, :, cols], yn.ap, reads=[yn], writes=[ynd])
        K.end_phase()
        K.stack = K.es


def phase_out_ple(K, nc, C, name, a_scr, FC, w_o, h_in, p_in, g_ple, w_gate, w_proj, h_out):
    with ExitStack() as es:
        K.stack = es
        K.begin_phase()
        stage = Ring([K.sb(name + "_st%d" % i, [128, 1024], F32) for i in range(3)])
        Wo = load_weight(K, nc, es, name + "_wo", w_o, FC, 1024, stage, [K.pool, K.dve, K.act])
        Wg = load_weight(K, nc, es, name + "_wg", w_gate, 8, 1024, stage, [K.pool, K.dve, K.act])
        Wp = load_weight(K, nc, es, name + "_wp", w_proj, 2, 1024, stage, [K.pool, K.dve, K.act])
        at_ring = Ring([K.sb(name + "_at%d" % i, [128, FC, 512], BF16) for i in range(1)])
        h_ring = Ring([K.sb(name + "_h%d" % i, [128, 4, 1024], F32) for i in range(1)])
        p_ring = Ring([K.sb(name + "_p%d" % i, [128, 4, 256], F32) for i in range(2)])
        pbf = K.sb(name + "_pbf", [128, 4, 256], BF16)
        pT = K.sb(name + "_pT", [128, 2, 512], BF16)
        ss = K.sb(name + "_ss", [128, 4], F32)
        lnv = K.sb(name + "_ln", [128, 4], F32)
        rstd = K.sb(name + "_rstd", [128, 4], F32)
        junk = K.sb(name + "_junk", [128, 1024], BF16)
        xh = K.sb(name + "_xh", [128, 4, 1024], BF16)
        xT = K.sb(name + "_xT", [128, 8, 512], BF16)
        sig_ring = Ring([K.sb(name + "_sig%d" % i, [128, 512], F32) for i in range(2)])
        tmp_ring = Ring([K.sb(name + "_tmp%d" % i, [128, 512], F32) for i in range(2)])
        ho_ring = Ring([K.sb(name + "_ho%d" % i, [128, 4, 1024], F32) for i in range(1)])
        ptr = Ring([K.ps(name + "_pt%d" % i, [128, 1024], BF16) for i in range(2)])
        pmm = Ring([K.ps(name + "_pm%d" % i, [128, 512], F32) for i in range(2)])
        pg_ring = Ring([K.ps(name + "_pg%d" % i, [128, 512], F32) for i in range(2)])
        pp_ring = Ring([K.ps(name + "_pp%d" % i, [128, 512], F32) for i in range(2)])
        ad = K.dram(name + "_a", a_scr)
        hd = K.dram(name + "_hin", h_in)
        od = K.dram(name + "_hout", h_out)
        av = a_scr.rearrange("(c p) t -> p c t", p=128)
        hv = h_in.rearrange("(t a p) d -> t p a d", a=4, p=128)
        pv = p_in.rearrange("(t a p) d -> t p a d", a=4, p=128)
        ov = h_out.rearrange("(t a p) d -> t p a d", a=4, p=128)
        for ts in range(8):
            at = at_ring.next()
            ht = h_ring.next()
            pt_ = p_ring.next()
            K.dma(K.sp, at.ap, av[:, :, ts * 512:(ts + 1) * 512], reads=[ad], writes=[at])
            K.dma(K.sp, ht.ap, hv[ts], reads=[hd], writes=[ht])
            K.dma(K.sp, pt_.ap, pv[ts], writes=[pt_])
            for a in range(4):
                for half in range(2):
                    pm = pmm.next()
                    for c in range(FC):
                        K.op(K.pe, lambda: nc.tensor.matmul(pm.ap, at.ap[:, c, a * 128:(a + 1) * 128], Wo[c].ap[:, half * 512:(half + 1) * 512],
                                                            start=(c == 0), stop=(c == FC - 1)), reads=[at, Wo[c]], writes=[pm])
                    hs = ht.ap[:, a, half * 512:(half + 1) * 512]
                    K.op(K.dve, lambda: nc.vector.tensor_tensor(hs, hs, pm.ap, op=ALU.add), reads=[ht, pm], writes=[ht])
            rms_to_xT(K, nc, C, ht, 4, ss, lnv, rstd, junk, xh, ptr, [(xT, g_ple)])
            K.op(K.pool, lambda: nc.gpsimd.tensor_copy(pbf.ap, pt_.ap), reads=[pt_], writes=[pbf])
            for c2 in range(2):
                pt = ptr.next()
                for a in range(4):
                    K.op(K.pe, lambda: nc.tensor.transpose(pt.ap[:, a * 128:(a + 1) * 128], pbf.ap[:, a, c2 * 128:(c2 + 1) * 128], C.identb.ap),
                         reads=[pbf, C.identb], writes=[pt])
                K.op(K.act, lambda: nc.scalar.copy(pT.ap[:, c2, :], pt.ap[:, 0:512]), reads=[pt], writes=[pT])
            ho = ho_ring.next()
            for a in range(4):
                for half in range(2):
                    pg = pg_ring.next()
                    pp = pp_ring.next()
                    cs_ = slice(half * 512, (half + 1) * 512)
                    for c in range(8):
                        K.op(K.pe, lambda: nc.tensor.matmul(pg.ap, xT.ap[:, c, a * 128:(a + 1) * 128], Wg[c].ap[:, cs_],
                                                            start=(c == 0), stop=(c == 7)), reads=[xT, Wg[c]], writes=[pg])
                    for c in range(2):
                        K.op(K.pe, lambda: nc.tensor.matmul(pp.ap, pT.ap[:, c, a * 128:(a + 1) * 128], Wp[c].ap[:, cs_],
                                                            start=(c == 0), stop=(c == 1)), reads=[pT, Wp[c]], writes=[pp])
                    sg = sig_ring.next()
                    tm = tmp_ring.next()
                    K.op(K.act, lambda: nc.scalar.activation(sg.ap, pg.ap, AF.Sigmoid), reads=[pg], writes=[sg])
                    K.op(K.dve, lambda: nc.vector.tensor_tensor(tm.ap, pp.ap, sg.ap, op=ALU.mult), reads=[pp, sg], writes=[tm])
                    K.op(K.pool, lambda: nc.gpsimd.tensor_tensor(ho.ap[:, a, cs_], tm.ap, ht.ap[:, a, cs_], op=ALU.add),
                         reads=[tm, ht], writes=[ho])
            K.dma(K.sp, ov[ts], ho.ap, reads=[ho], writes=[od])
        K.end_phase()
        K.stack = K.es


def phase_p3(K, nc, C, h1, w_kv, s_in, kt_scr, v_scr, q_scr, sg_scr):
    with ExitStack() as es:
        K.stack = es
        K.begin_phase()
        stage = Ring([K.sb("p3_st%d" % i, [128, 2048], F32) for i in range(2)])
        Wkv = load_weight(K, nc, es, "p3_wkv", w_kv, 8, 2048, stage, [K.pool, K.dve, K.act])
        Wqg = load_weight(K, nc, es, "p3_wqg", s_in, 8, 2048, stage, [K.pool, K.dve, K.act])
        h_ring = Ring([K.sb("p3_h%d" % i, [128, 4, 1024], F32) for i in range(2)])
        ss = K.sb("p3_ss", [128, 4], F32)
        lnv = K.sb("p3_ln", [128, 4], F32)
        rstd = K.sb("p3_rstd", [128, 4], F32)
        junk = K.sb("p3_junk", [128, 1024], BF16)
        xh = K.sb("p3_xh", [128, 4, 1024], BF16)
        xTk = K.sb("p3_xTk", [128, 8, 512], BF16)
        xTq = K.sb("p3_xTq", [128, 8, 512], BF16)
        sq_ring = Ring([K.sb("p3_sq%d" % i, [128, 512], BF16) for i in range(2)])
        rs_ring = Ring([K.sb("p3_rs%d" % i, [128, 512], F32) for i in range(2)])
        kst = K.sb("p3_kst", [128, 8, 512], BF16)
        vst = K.sb("p3_vst", [128, 4, 1024], BF16)
        qst = K.sb("p3_qst", [128, 8, 512], BF16)
        gst = K.sb("p3_gst", [128, 8, 512], BF16)
        ptr = Ring([K.ps("p3_pt%d" % i, [128, 1024], BF16) for i in range(2)])
        pmm = Ring([K.ps("p3_pm%d" % i, [128, 512], F32) for i in range(3)])
        pn_ring = Ring([K.ps("p3_pn%d" % i, [128, 512], F32) for i in range(2)])
        hd = K.dram("p3_h1", h1)
        kd = K.dram("kt_scr", kt_scr)
        vd = K.dram("v_scr", v_scr)
        qd = K.dram("q_scr", q_scr)
        gd = K.dram("sg_scr", sg_scr)
        hv = h1.rearrange("(t a p) d -> t p a d", a=4, p=128)
        vv = v_scr.rearrange("(t a p) d -> t p a d", a=4, p=128)

        def headnorm(pm, gcol, out_ap, out_t):
            sq = sq_ring.next()
            rs = rs_ring.next()
            pn = pn_ring.next()
            K.op(K.act, lambda: nc.scalar.activation(sq.ap, pm.ap, AF.Square), reads=[pm], writes=[sq])
            K.op(K.pe, lambda: nc.tensor.matmul(pn.ap, C.blk.ap, sq.ap, start=True, stop=True), reads=[C.blk, sq], writes=[pn])
            K.op(K.act, lambda: nc.scalar.activation(rs.ap, pn.ap, AF.Ln, bias=C.eps.ap, scale=1.0 / 64), reads=[pn, C.eps], writes=[rs])
            K.op(K.act, lambda: nc.scalar.activation(rs.ap, rs.ap, AF.Exp, scale=-0.5), reads=[rs], writes=[rs])
            K.op(K.dve, lambda: nc.vector.scalar_tensor_tensor(out_ap, pm.ap, gcol, rs.ap, op0=ALU.mult, op1=ALU.mult),
                 reads=[pm, C.kq, rs], writes=[out_t])

        for ts in range(8):
            ht = h_ring.next()
            K.dma(K.sp, ht.ap, hv[ts], reads=[hd], writes=[ht])
            rms_to_xT(K, nc, C, ht, 4, ss, lnv, rstd, junk, xh, ptr, [(xTk, C.g_kv), (xTq, C.g_s)])
            tok = slice(ts * 512, (ts + 1) * 512)
            for fo in range(8):
                pm = pmm.next()
                for c in range(8):
                    K.op(K.pe, lambda: nc.tensor.matmul(pm.ap, Wkv[c].ap[:, fo * 128:(fo + 1) * 128], xTk.ap[:, c, :],
                                                        start=(c == 0), stop=(c == 7)), reads=[Wkv[c], xTk], writes=[pm])
                headnorm(pm, C.kq.ap[:, 0:1], kst.ap[:, fo, :], kst)
            for a in range(4):
                for half in range(2):
                    pm = pmm.next()
                    for c in range(8):
                        K.op(K.pe, lambda: nc.tensor.matmul(pm.ap, xTk.ap[:, c, a * 128:(a + 1) * 128],
                                                            Wkv[c].ap[:, 1024 + half * 512:1024 + (half + 1) * 512],
                                                            start=(c == 0), stop=(c == 7)), reads=[Wkv[c], xTk], writes=[pm])
                    K.op(K.act, lambda: nc.scalar.copy(vst.ap[:, a, half * 512:(half + 1) * 512], pm.ap), reads=[pm], writes=[vst])
            for fo in range(8):
                pm = pmm.next()
                for c in range(8):
                    K.op(K.pe, lambda: nc.tensor.matmul(pm.ap, Wqg[c].ap[:, fo * 128:(fo + 1) * 128], xTq.ap[:, c, :],
                                                        start=(c == 0), stop=(c == 7)), reads=[Wqg[c], xTq], writes=[pm])
                headnorm(pm, C.kq.ap[:, 1:2], qst.ap[:, fo, :], qst)
            for fo in range(8):
                pm = pmm.next()
                for c in range(8):
                    K.op(K.pe, lambda: nc.tensor.matmul(pm.ap, Wqg[c].ap[:, 1024 + fo * 128:1024 + (fo + 1) * 128], xTq.ap[:, c, :],
                                                        start=(c == 0), stop=(c == 7)), reads=[Wqg[c], xTq], writes=[pm])
                K.op(K.act, lambda: nc.scalar.activation(gst.ap[:, fo, :], pm.ap, AF.Silu), reads=[pm], writes=[gst])
            for hf in range(2):
                qs = slice(hf * 4, (hf + 1) * 4)
                K.dma(K.sp, kt_scr.rearrange("(q p) t -> p q t", p=128)[:, qs, tok], kst.ap[:, qs, :], reads=[kst], writes=[kd])
                K.dma(K.sp, q_scr.rearrange("(q p) t -> p q t", p=128)[:, qs, tok], qst.ap[:, qs, :], reads=[qst], writes=[qd])
                K.dma(K.sp, sg_scr.rearrange("(q p) t -> p q t", p=128)[:, qs, tok], gst.ap[:, qs, :], reads=[gst], writes=[gd])
            K.dma(K.sp, vv[ts], vst.ap, reads=[vst], writes=[vd])
        K.end_phase()
        K.stack = K.es


def phase_p4(K, nc, C, kt_scr, v_scr, q_scr, sg_scr, og_scr, nTB=8, nQ=8):
    with ExitStack() as es:
        K.stack = es
        K.begin_phase()
        ntri = K.sb("p4_ntri", [128, 128], BF16)
        ebig = K.sb("p4_ebig", [128, 255], BF16)
        nsel = K.sb("p4_nsel", [128, 32, 128], BF16)
        masks = K.sb("p4_masks", [128, 4, 512], BF16)
        with ExitStack() as est:
            K.stack = est
            negb = K.sb("p4_negb", [128, 128], BF16)
            negb3 = K.sb("p4_negb3", [128, 32, 128], BF16)
            oneb3 = K.sb("p4_oneb3", [128, 4, 512], BF16)
            K.op(K.pool, lambda: nc.gpsimd.memset(negb.ap, -1.0), writes=[negb])
            K.op(K.pool, lambda: nc.gpsimd.affine_select(ntri.ap, negb.ap, pattern=[[-1, 128]], compare_op=ALU.is_ge, fill=0.0,
                                                         base=0, channel_multiplier=1), reads=[negb], writes=[ntri])
            K.op(K.pool, lambda: nc.gpsimd.memset(ebig.ap, 0.0), writes=[ebig])
            K.op(K.pool, lambda: nc.gpsimd.memset(ebig.ap[:, 127:128], 1.0), writes=[ebig])
            K.op(K.pool, lambda: nc.gpsimd.memset(negb3.ap, -1.0), writes=[negb3])
            K.op(K.pool, lambda: nc.gpsimd.affine_select(nsel.ap, negb3.ap, pattern=[[-1, 32], [0, 128]], compare_op=ALU.is_ge, fill=0.0,
                                                         base=-1, channel_multiplier=1), reads=[negb3], writes=[nsel])
            K.op(K.pool, lambda: nc.gpsimd.memset(oneb3.ap, 1.0), writes=[oneb3])
            K.op(K.pool, lambda: nc.gpsimd.affine_select(masks.ap, oneb3.ap, pattern=[[-128, 4], [1, 512]], compare_op=ALU.is_gt, fill=0.0,
                                                         base=0, channel_multiplier=-1), reads=[oneb3], writes=[masks])
            K.barrier()
            K.stack = es
        qpad = [Ring([K.sb("p4_qp%d_%d" % (r, i), [128, 512], BF16) for i in range(2)]) for r in range(2)]
        for r in range(2):
            for t_ in qpad[r].t:
                K.op(K.pool, lambda: nc.gpsimd.memset(t_.ap, 0.0), writes=[t_])
        kt_ring = Ring([K.sb("p4_kt%d" % i, [128, S], BF16) for i in range(2)])
        v_ring = Ring([K.sb("p4_v%d" % i, [128, 32, 128], BF16) for i in range(2)])
        sgt_ring = Ring([K.sb("p4_sg%d" % i, [128, 512], BF16) for i in range(2)])
        ogst_ring = Ring([K.sb("p4_og%d" % i, [128, 512], BF16) for i in range(2)])
        SPs = [K.sb("p4_sp%d" % i, [128, 32, 512], BF16) for i in range(2)]
        SPvs = [[K.view("p4_sp%d_%d" % (b, i), SPs[b].ap[:, i, :]) for i in range(32)] for b in range(2)]
        e_ring = Ring([K.sb("p4_e%d" % i, [128, 512], F32) for i in range(3)])
        spt_ring = Ring([K.sb("p4_spt%d" % i, [128, 512], F32) for i in range(2)])
        csb_ring = Ring([K.sb("p4_csb%d" % i, [128, 512], BF16) for i in range(2)])
        wt_ring = Ring([K.sb("p4_wt%d" % i, [128, 512], BF16) for i in range(4)])
        wm_ring = Ring([K.sb("p4_wm%d" % i, [128, 512], BF16) for i in range(2)])
        pz = Ring([K.ps("p4_pz%d" % i, [128, 512], F32) for i in range(3)])
        pcs = K.ps("p4_pcs", [128, 512], F32)
        pgr = Ring([K.ps("p4_pg%d" % i, [128, 512], F32) for i in range(3)])
        po_ring = Ring([K.ps("p4_po%d" % i, [128, 512], F32) for i in range(1)])
        kd = K.dram("kt_scr", kt_scr)
        vd = K.dram("v_scr", v_scr)
        qd = K.dram("q_scr", q_scr)
        gd = K.dram("sg_scr", sg_scr)
        od = K.dram("og_scr", og_scr)
        vv = v_scr.rearrange("(jb p) (q c) -> q p jb c", p=128, c=128)
        heads = [(q, TB, r) for q in range(nQ) for TB in range(nTB) for r in range(2)]
        groups = {}
        qstate = {}
        hstate = {}

        def prepare(i):
            q, TB, r = heads[i]
            if q not in qstate:
                KT = kt_ring.next()
                V = v_ring.next()
                K.dma(K.sp, KT.ap, kt_scr[q * 128:(q + 1) * 128, :], reads=[kd], writes=[KT])
                K.dma(K.sp, V.ap, vv[q], reads=[vd], writes=[V])
                qstate[q] = (KT, V)
            if (q, TB) not in groups:
                tok = slice(TB * 512, (TB + 1) * 512)
                sgt = sgt_ring.next()
                K.dma(K.sp, sgt.ap, sg_scr[q * 128:(q + 1) * 128, tok], reads=[gd], writes=[sgt])
                ogst = ogst_ring.next()
                qps = []
                for r_ in range(2):
                    qp = qpad[r_].next()
                    K.dma(K.sp, qp.ap[r_ * 64:(r_ + 1) * 64, :], q_scr[q * 128 + r_ * 64:q * 128 + (r_ + 1) * 64, tok],
                          reads=[qd], writes=[qp])
                    qps.append(qp)
                groups[(q, TB)] = (sgt, ogst, qps, tok)
            hstate[i] = {"SPv": SPvs[i % 2]}

        def sweep1(i):
            q, TB, r = heads[i]
            KT, V = qstate[q]
            sgt, ogst, qps, tok = groups[(q, TB)]
            qp = qps[r]
            SPv = hstate[i]["SPv"]
            nkb = 4 * (TB + 1)
            zs, es_ = {}, {}

            def emit_z(jb):
                z = pz.next()
                c0 = max(0, jb - 4 * TB) * 128
                K.op(K.pe, lambda: nc.tensor.matmul(z.ap[:, c0:], KT.ap[:, jb * 128:(jb + 1) * 128], qp.ap[:, c0:], start=True, stop=True),
                     reads=[KT, qp], writes=[z])
                K.pe.signal_last()
                zs[jb] = z

            def emit_exp(jb):
                e = e_ring.next()
                c0 = max(0, jb - 4 * TB) * 128
                K.op(K.act, lambda: nc.scalar.activation(e.ap[:, c0:], zs[jb].ap[:, c0:], AF.Exp), reads=[zs[jb]], writes=[e])
                es_[jb] = e

            def emit_ln(jb):
                e = es_[jb]
                rr = jb - 4 * TB
                if rr < 0:
                    K.op(K.act, lambda: nc.scalar.activation(SPv[jb].ap, e.ap, AF.Ln, bias=C.one.ap, scale=1.0),
                         reads=[e, C.one], writes=[SPv[jb]])
                else:
                    spt = spt_ring.next()
                    c0 = rr * 128
                    K.op(K.act, lambda: nc.scalar.activation(spt.ap[:, c0:], e.ap[:, c0:], AF.Ln, bias=C.one.ap, scale=1.0),
                         reads=[e, C.one], writes=[spt])
                    K.op(K.dve, lambda: nc.vector.tensor_tensor(SPv[jb].ap[:, c0:], spt.ap[:, c0:], masks.ap[:, rr, c0:], op=ALU.mult),
                         reads=[spt, masks], writes=[SPv[jb]])

            def emit_cs(jb):
                c0 = max(0, jb - 4 * TB) * 128
                K.op(K.pe, lambda: nc.tensor.matmul(pcs.ap[:, c0:], ebig.ap[:, 127 - jb:255 - jb], SPv[jb].ap[:, c0:],
                                                    start=(jb == 0), stop=(jb == nkb - 1)), reads=[ebig, SPv[jb]], writes=[pcs])

            emit_z(0)
            if nkb > 1:
                emit_z(1)
            emit_exp(0)
            for jb in range(nkb):
                if jb + 1 < nkb:
                    emit_exp(jb + 1)
                emit_ln(jb)
                if jb + 2 < nkb:
                    emit_z(jb + 2)
                emit_cs(jb)
            csb = csb_ring.next()
            K.op(K.dve, lambda: nc.vector.tensor_copy(csb.ap, pcs.ap), reads=[pcs], writes=[csb])
            hstate[i]["csb"] = csb

        def sweep2(i):
            q, TB, r = heads[i]
            KT, V = qstate[q]
            sgt, ogst, qps, tok = groups[(q, TB)]
            qp = qps[r]
            SPv = hstate[i]["SPv"]
            csb = hstate[i]["csb"]
            nkb = 4 * (TB + 1)
            rows = slice(r * 64, (r + 1) * 64)
            po = po_ring.next()
            gs = {}

            def emit_G(jb):
                gq = pgr.next()
                c0 = max(0, jb - 4 * TB) * 128
                K.op(K.pe, lambda: nc.tensor.matmul(gq.ap[:, c0:], ntri.ap, SPv[jb].ap[:, c0:], start=True, stop=False),
                     reads=[ntri, SPv[jb]], writes=[gq])
                K.op(K.pe, lambda: nc.tensor.matmul(gq.ap[:, c0:], KT.ap[:, jb * 128:(jb + 1) * 128], qp.ap[:, c0:], start=False, stop=False),
                     reads=[KT, qp], writes=[gq])
                K.op(K.pe, lambda: nc.tensor.matmul(gq.ap[:, c0:], nsel.ap[:, jb, :], csb.ap[:, c0:], start=False, stop=True),
                     reads=[nsel, csb], writes=[gq])
                K.pe.signal_last()
                gs[jb] = gq

            emit_G(0)
            if nkb > 1:
                emit_G(1)
            for jb in range(nkb):
                gq = gs[jb]
                wt = wt_ring.next()
                rr = jb - 4 * TB
                c0 = max(0, rr) * 128
                K.op(K.act, lambda: nc.scalar.activation(wt.ap[:, c0:], gq.ap[:, c0:], AF.Exp), reads=[gq], writes=[wt])
                if rr >= 0:
                    wm = wm_ring.next()
                    K.op(K.dve, lambda: nc.vector.tensor_tensor(wm.ap[:, c0:], wt.ap[:, c0:], masks.ap[:, rr, c0:], op=ALU.mult),
                         reads=[wt, masks], writes=[wm])
                    wt = wm
                if jb + 2 < nkb:
                    emit_G(jb + 2)
                K.op(K.pe, lambda: nc.tensor.matmul(po.ap[:, c0:], V.ap[:, jb, :], wt.ap[:, c0:], start=(jb == 0), stop=(jb == nkb - 1)),
                     reads=[V, wt], writes=[po])
            K.op(K.dve, lambda: nc.vector.tensor_tensor(ogst.ap[rows, :], po.ap[rows, :], sgt.ap[rows, :], op=ALU.mult),
                 reads=[po, sgt], writes=[ogst])
            if r == 1:
                K.dma(K.sp, og_scr[q * 128:(q + 1) * 128, tok], ogst.ap, reads=[ogst], writes=[od])
            del hstate[i]

        prepare(0)
        sweep1(0)
        for i in range(len(heads)):
            if i + 1 < len(heads):
                prepare(i + 1)
                sweep1(i + 1)
            sweep2(i)
        K.end_phase()
        K.stack = K.es


PARAM_SHAPES = {
    "m_norm": [1, 1024], "m_in": [1, 1024, 5152], "m_conv_w": [1, 4, 3072], "m_conv_b": [1, 3072],
    "m_dt_bias": [1, 32], "m_A_log": [1, 32], "m_D": [1, 32], "m_ynorm": [1, 2048], "m_out": [1, 2048, 1024],
    "kv_norm": [1024], "w_kv": [1024, 2048], "k_norm": [64], "s_norm": [1, 1024], "s_in": [1, 1024, 2048],
    "q_norm": [1, 64], "s_out": [1, 1024, 1024], "ple_norm": [2, 1024], "ple_gate": [2, 1024, 1024],
    "ple_proj": [2, 256, 1024],
}

SCRATCH = {
    "sz_scr": ([2048, S], BF16), "xbc_scr": ([3072, S], BF16), "dt_scr": ([S, 32], F32),
    "yn_scr": ([2048, S], BF16), "h1_scr": ([S, D], F32), "q_scr": ([1024, S], BF16),
    "sg_scr": ([1024, S], BF16), "og_scr": ([1024, S], BF16),
    "kt_scr": ([1024, S], BF16), "v_scr": ([S, 1024], BF16),
}


def build(phases=("p1a", "p1b", "p2", "p3", "p4", "p5"), ext_in=(), ext_out=(), p1b_chunks=32, p4_tb=8, p4_q=8):
    nc = bass.Bass("TRN2", target_bir_lowering=False)
    x = nc.dram_tensor("x", [S, D], F32, kind="ExternalInput").ap()
    p0 = nc.dram_tensor("p0", [S, 256], F32, kind="ExternalInput").ap()
    p1 = nc.dram_tensor("p1", [S, 256], F32, kind="ExternalInput").ap()
    prm = {k: nc.dram_tensor(k, shp, F32, kind="ExternalInput").ap() for k, shp in PARAM_SHAPES.items()}
    out = nc.dram_tensor("out", [S, D], F32, kind="ExternalOutput").ap()
    scr = {}
    for k, (shp, dt_) in SCRATCH.items():
        kind = "ExternalInput" if k in ext_in else ("ExternalOutput" if k in ext_out else "Internal")
        scr[k] = nc.dram_tensor(k, shp, dt_, kind=kind).ap()
    with ExitStack() as es:
        K = Ctx(nc, es)
        C = make_consts(K, nc, prm)
        if "p1a" in phases:
            phase_p1a(K, nc, C, x, prm["m_in"][0], scr["sz_scr"], scr["xbc_scr"], scr["dt_scr"])
        if "p1b" in phases:
            phase_p1b(K, nc, C, scr["sz_scr"], scr["xbc_scr"], scr["dt_scr"], scr["yn_scr"], nchunks=p1b_chunks)
        if "p2" in phases:
            phase_out_ple(K, nc, C, "p2", scr["yn_scr"], 16, prm["m_out"][0], x, p0, C.g_p0,
                          prm["ple_gate"][0], prm["ple_proj"][0], scr["h1_scr"])
        if "p3" in phases:
            phase_p3(K, nc, C, scr["h1_scr"], prm["w_kv"], prm["s_in"][0], scr["kt_scr"], scr["v_scr"], scr["q_scr"], scr["sg_scr"])
        if "p4" in phases:
            phase_p4(K, nc, C, scr["kt_scr"], scr["v_scr"], scr["q_scr"], scr["sg_scr"], scr["og_scr"], nTB=p4_tb, nQ=p4_q)
        if "p5" in phases:
            phase_out_ple(K, nc, C, "p5", scr["og_scr"], 8, prm["s_out"][0], scr["h1_scr"], p1, C.g_p1,
                          prm["ple_gate"][1], prm["ple_proj"][1], out)
        K.finish()
    return nc


_NC_CACHE = {}


def kernel(**inputs):
    if "full" not in _NC_CACHE:
        _NC_CACHE["full"] = build()
    nc = _NC_CACHE["full"]
    x = np.asarray(inputs["x"], dtype=np.float32)
    p = np.asarray(inputs["p"], dtype=np.float32)
    in_maps = []
    for b in range(NCORES):
        m = {"x": np.ascontiguousarray(x[b]), "p0": np.ascontiguousarray(p[0, b]), "p1": np.ascontiguousarray(p[1, b])}
        for k in PARAM_SHAPES:
            m[k] = np.ascontiguousarray(np.asarray(inputs[k], dtype=np.float32))
        in_maps.append(m)
    res = run_bass_kernel_spmd(nc, in_maps, core_ids=list(range(NCORES)))
    return np.stack([np.asarray(res.results[b]["out"], dtype=np.float32) for b in range(NCORES)], axis=0)
```

```python
from bisect import bisect_left
from contextlib import ExitStack
import os
import numpy as np
import concourse.bass as bass
import concourse.mybir as mybir
from concourse.alu_op_type import AluOpType as ALU
from concourse.bass_utils import run_bass_kernel_spmd

F32 = mybir.dt.float32
BF16 = mybir.dt.bfloat16
AF = mybir.ActivationFunctionType

S = 4096
D = 1024
_DBG_STOP = int(os.environ.get('P1B_STOP', '99'))
NCORES = 8
EPS = 1e-6


class Eng:
    def __init__(self, name, eng, sem, eager):
        self.name, self.eng, self.sem, self.eager = name, eng, sem, eager
        self.n = 0
        self.count = 0
        self.sig_idx = []
        self.sig_cnt = []
        self.last = None
        self.last_signaled = True
        self.known = {}

    def signal_last(self):
        if not self.last_signaled:
            self.last.then_inc(self.sem, 1)
            self.count += 1
            self.sig_idx.append(self.n)
            self.sig_cnt.append(self.count)
            self.last_signaled = True

    def count_for(self, idx):
        i = bisect_left(self.sig_idx, idx)
        if i == len(self.sig_idx):
            self.signal_last()
            i = len(self.sig_idx) - 1
        assert self.sig_idx[i] >= idx
        return self.sig_cnt[i]


class DSem:
    def __init__(self, sem):
        self.sem = sem
        self.issued = 0


class T:
    def __init__(self, name, ap, space):
        self.name, self.ap, self.space = name, ap, space
        self.w = None
        self.r = {}
        self.dsem = None

    def __getitem__(self, k):
        return self.ap[k]


class Ring:
    def __init__(self, tiles):
        self.t = tiles
        self.i = 0

    def next(self):
        t = self.t[self.i % len(self.t)]
        self.i += 1
        return t


class Ctx:
    def __init__(self, nc, es):
        self.nc, self.es = nc, es
        mk = lambda n: es.enter_context(nc.semaphore(n))
        self.pe = Eng("pe", nc.tensor, mk("s_pe"), False)
        self.act = Eng("act", nc.scalar, mk("s_act"), True)
        self.dve = Eng("dve", nc.vector, mk("s_dve"), True)
        self.pool = Eng("pool", nc.gpsimd, mk("s_pool"), True)
        self.sp = Eng("sp", nc.sync, mk("s_sp"), True)
        self.engs = [self.pe, self.act, self.dve, self.pool, self.sp]
        self.dsems = []
        self.free_dsems = []
        self.nsem = 0
        self.stack = es

    def sb(self, name, shape, dtype, es=None):
        t = (es or self.stack).enter_context(self.nc.sbuf_tensor(name, shape, dtype))
        return T(name, t.ap(), "sb")

    def ps(self, name, shape, dtype, es=None):
        t = (es or self.stack).enter_context(self.nc.psum_tensor(name, shape, dtype))
        return T(name, t.ap(), "ps")

    def dram(self, name, ap):
        return T(name, ap, "dram")

    def view(self, name, ap, space="sb"):
        return T(name, ap, space)

    def begin_phase(self):
        self.phase_dsems = []

    def end_phase(self):
        self.barrier()
        self.free_dsems.extend(self.phase_dsems)
        self.phase_dsems = None

    def new_dsem(self):
        if self.free_dsems:
            d = self.free_dsems.pop()
        else:
            self.nsem += 1
            d = DSem(self.es.enter_context(self.nc.semaphore("s_d%d" % self.nsem)))
            self.dsems.append(d)
        if getattr(self, "phase_dsems", None) is not None:
            self.phase_dsems.append(d)
        return d

    def _waits(self, E, reads, writes):
        need = {}

        def add(ev, same_ok):
            if ev is None:
                return
            if ev[0] == "e":
                Dn, idx = ev[1], ev[2]
                if Dn is E and same_ok:
                    return
                c = Dn.count_for(idx)
                sem = Dn.sem
            else:
                c = ev[1].issued * 16
                sem = ev[1].sem
            key = id(sem)
            if need.get(key, (None, 0))[1] < c:
                need[key] = (sem, c)

        for t in reads:
            add(t.w, E is self.pe)
        for t in writes:
            add(t.w, True)
            for ev in t.r.values():
                add(ev, True)
        for key, (sem, c) in need.items():
            if E.known.get(key, 0) >= c:
                continue
            E.eng.wait_ge(sem, c)
            E.known[key] = c

    def op(self, E, make, reads=(), writes=()):
        writes = list(writes) + [t for t in reads if t.space == "ps"]
        reads = [t for t in reads if t.space != "ps"]
        self._waits(E, reads, writes)
        inst = make()
        E.n += 1
        E.last = inst
        E.last_signaled = False
        if E.eager:
            E.signal_last()
        ev = ("e", E, E.n)
        for t in reads:
            t.r[id(E)] = ev
        for t in writes:
            t.w = ev
            t.r = {}
        return inst

    def dma(self, Q, out_ap, in_ap, reads=(), writes=(), dsem=None, **kw):
        self._waits(Q, reads, writes)
        if dsem is None:
            cand = ([t for t in writes if t.space != "dram"] or [t for t in reads if t.space != "dram"]
                    or list(writes) or list(reads))
            t0 = cand[0]
            if t0.dsem is None:
                t0.dsem = self.new_dsem()
            dsem = t0.dsem
        inst = Q.eng.dma_start(out=out_ap, in_=in_ap, **kw)
        inst.then_inc(dsem.sem, 16)
        dsem.issued += 1
        ev = ("d", dsem)
        for t in reads:
            t.r[id(dsem)] = ev
        for t in writes:
            t.w = ev
            t.r = {}
        return inst

    def barrier(self):
        for E in self.engs:
            E.signal_last()
        for E in self.engs:
            for Dn in self.engs:
                if Dn is E or Dn.count == 0:
                    continue
                key = id(Dn.sem)
                if E.known.get(key, 0) < Dn.count:
                    E.eng.wait_ge(Dn.sem, Dn.count)
                    E.known[key] = Dn.count
            for d in self.dsems:
                if d.issued:
                    key = id(d.sem)
                    if E.known.get(key, 0) < d.issued * 16:
                        E.eng.wait_ge(d.sem, d.issued * 16)
                        E.known[key] = d.issued * 16

    def finish(self):
        for d in self.dsems:
            if d.issued:
                self.sp.eng.wait_ge(d.sem, d.issued * 16)


class Consts:
    pass


def make_consts(K, nc, prm):
    C = Consts()
    onesf = K.sb("c_onesf", [128, 128], F32)
    K.op(K.pool, lambda: nc.gpsimd.memset(onesf.ap, 1.0), writes=[onesf])
    C.onesf = onesf
    identf = K.sb("c_identf", [128, 128], F32)
    K.op(K.pool, lambda: nc.gpsimd.affine_select(identf.ap, onesf.ap, pattern=[[1, 128]], compare_op=ALU.is_equal,
                                                 fill=0.0, base=0, channel_multiplier=-1), reads=[onesf], writes=[identf])
    C.identf = identf
    identb = K.sb("c_identb", [128, 128], BF16)
    K.op(K.pool, lambda: nc.gpsimd.tensor_copy(identb.ap, identf.ap), reads=[identf], writes=[identb])
    C.identb = identb
    onesb = K.sb("c_onesb", [128, 128], BF16)
    K.op(K.pool, lambda: nc.gpsimd.memset(onesb.ap, 1.0), writes=[onesb])
    C.onesb = onesb
    tri = K.sb("c_tri", [128, 128], F32)
    K.op(K.pool, lambda: nc.gpsimd.affine_select(tri.ap, onesf.ap, pattern=[[1, 128]], compare_op=ALU.is_ge,
                                                 fill=0.0, base=0, channel_multiplier=-1), reads=[onesf], writes=[tri])
    C.tri = tri
    epsc = K.sb("c_eps", [128, 1], F32)
    K.op(K.pool, lambda: nc.gpsimd.memset(epsc.ap, EPS), writes=[epsc])
    C.eps = epsc
    onec = K.sb("c_one", [128, 1], F32)
    K.op(K.pool, lambda: nc.gpsimd.memset(onec.ap, 1.0), writes=[onec])
    C.one = onec
    blk = K.sb("c_blk", [128, 128], BF16)
    K.op(K.pool, lambda: nc.gpsimd.memset(blk.ap, 0.0), writes=[blk])
    K.op(K.pool, lambda: nc.gpsimd.memset(blk.ap[0:64, 0:64], 1.0), writes=[blk])
    K.op(K.pool, lambda: nc.gpsimd.memset(blk.ap[64:128, 64:128], 1.0), writes=[blk])
    C.blk = blk

    def vec_cols(name, items):
        R = sum(n for _, n in items)
        st = K.sb("vst_" + name, [R, 128], F32)
        r0 = 0
        for ap1, n in items:
            K.dma(K.sp, st.ap[r0:r0 + n, :], ap1.rearrange("(c p) -> c p", p=128), writes=[st])
            r0 += n
        pt = K.ps("vps_" + name, [128, 512], F32)
        K.op(K.pe, lambda: nc.tensor.transpose(pt.ap[:, 0:R], st.ap, identf.ap[0:R, 0:R]), reads=[st, identf], writes=[pt])
        out = K.sb("vec_" + name, [128, R], F32)
        K.op(K.dve, lambda: nc.vector.tensor_copy(out.ap, pt.ap[:, 0:R]), reads=[pt], writes=[out])
        return out

    veca = K.sb("c_veca", [128, 80], F32)
    vecb = K.sb("c_vecb", [128, 96], F32)
    with ExitStack() as es2:
        K.stack = es2
        va = vec_cols("a", [(prm["m_norm"][0], 8), (prm["kv_norm"], 8), (prm["s_norm"][0], 8),
                            (prm["ple_norm"][0], 8), (prm["ple_norm"][1], 8), (prm["m_ynorm"][0], 16),
                            (prm["m_conv_b"][0], 24)])
        vb = vec_cols("b", [(prm["m_conv_w"][0, j], 24) for j in range(4)])
        K.op(K.dve, lambda: nc.vector.tensor_copy(veca.ap, va.ap), reads=[va], writes=[veca])
        K.op(K.dve, lambda: nc.vector.tensor_copy(vecb.ap, vb.ap), reads=[vb], writes=[vecb])
        K.barrier()
        K.stack = K.es
    C.g_m, C.g_kv, C.g_s = veca.ap[:, 0:8], veca.ap[:, 8:16], veca.ap[:, 16:24]
    C.g_p0, C.g_p1 = veca.ap[:, 24:32], veca.ap[:, 32:40]
    C.g_y, C.conv_b = veca.ap[:, 40:56], veca.ap[:, 56:80]
    C.conv_w = vecb.ap
    C.veca, C.vecb = veca, vecb

    kq = K.sb("c_kq", [128, 2], F32)
    for half in range(2):
        K.dma(K.sp, kq.ap[half * 64:(half + 1) * 64, 0:1], prm["k_norm"].rearrange("(p o) -> p o", o=1), writes=[kq])
        K.dma(K.sp, kq.ap[half * 64:(half + 1) * 64, 1:2], prm["q_norm"][0].rearrange("(p o) -> p o", o=1), writes=[kq])
    kq2 = K.sb("c_kq2", [128, 2], F32)
    K.op(K.dve, lambda: nc.vector.tensor_copy(kq2.ap[:, 0:1], kq.ap[:, 0:1]), reads=[kq], writes=[kq2])
    K.op(K.dve, lambda: nc.vector.tensor_scalar(kq2.ap[:, 1:2], kq.ap[:, 1:2], 0.125, None, op0=ALU.mult), reads=[kq], writes=[kq2])
    C.kq = kq2
    bc = K.sb("c_bc", [128, 3, 32], F32)
    K.dma(K.sp, bc.ap[:, 0, :], prm["m_dt_bias"][0:1, :].to_broadcast([128, 32]), writes=[bc])
    K.dma(K.sp, bc.ap[:, 1, :], prm["m_A_log"][0:1, :].to_broadcast([128, 32]), writes=[bc])
    K.dma(K.sp, bc.ap[:, 2, :], prm["m_D"][0:1, :].to_broadcast([128, 32]), writes=[bc])
    C.bc = bc
    Abc = K.sb("c_A", [128, 32], F32)
    K.op(K.act, lambda: nc.scalar.activation(Abc.ap, bc.ap[:, 1, :], AF.Exp), reads=[bc], writes=[Abc])
    K.op(K.dve, lambda: nc.vector.tensor_scalar(Abc.ap, Abc.ap, -1.0, None, op0=ALU.mult), reads=[Abc], writes=[Abc])
    C.A = Abc
    Dcol = K.sb("c_Dcol", [128, 16], F32)
    dv = bc.ap[:, 2, :].rearrange("p (q r) -> p q r", r=2)
    K.op(K.dve, lambda: nc.vector.tensor_copy(Dcol.ap[0:64, :], dv[0:64, :, 0]), reads=[bc], writes=[Dcol])
    K.op(K.dve, lambda: nc.vector.tensor_copy(Dcol.ap[64:128, :], dv[64:128, :, 1]), reads=[bc], writes=[Dcol])
    C.Dcol = Dcol
    return C


def load_weight(K, nc, es, name, src, C_, N, stage_ring, cast_engs):
    w = K.sb(name, [128, C_, N], BF16, es)
    views = [K.view("%s_%d" % (name, c), w.ap[:, c, :]) for c in range(C_)]
    SW = stage_ring.t[0].ap.shape[1]
    i = 0
    for c in range(C_):
        for n0 in range(0, N, SW):
            n1 = min(N, n0 + SW)
            st = stage_ring.next()
            K.dma(K.sp, st.ap[:, 0:n1 - n0], src[c * 128:(c + 1) * 128, n0:n1], writes=[st])
            E = cast_engs[i % len(cast_engs)]
            i += 1
            if E is K.act:
                K.op(E, lambda: nc.scalar.copy(views[c].ap[:, n0:n1], st.ap[:, 0:n1 - n0]), reads=[st], writes=[views[c]])
            else:
                K.op(E, lambda: E.eng.tensor_copy(views[c].ap[:, n0:n1], st.ap[:, 0:n1 - n0]), reads=[st], writes=[views[c]])
    return views


def rms_to_xT(K, nc, C, ht, na, ss, lnv, rstd, junk, xh, ptr_ring, outs):
    for a in range(na):
        K.op(K.act, lambda: nc.scalar.activation(junk.ap, ht.ap[:, a, :], AF.Square, accum_out=ss.ap[:, a:a + 1]),
             reads=[ht], writes=[junk, ss])
    K.op(K.act, lambda: nc.scalar.copy(junk.ap[:, 0:8], junk.ap[:, 8:16]), reads=[junk], writes=[junk])
    K.op(K.act, lambda: nc.scalar.activation(lnv.ap[:, 0:na], ss.ap[:, 0:na], AF.Ln, bias=C.eps.ap, scale=1.0 / D),
         reads=[ss, C.eps], writes=[lnv])
    K.op(K.act, lambda: nc.scalar.activation(rstd.ap[:, 0:na], lnv.ap[:, 0:na], AF.Exp, scale=-0.5), reads=[lnv], writes=[rstd])
    for a in range(na):
        K.op(K.dve, lambda: nc.vector.tensor_scalar(xh.ap[:, a, :], ht.ap[:, a, :], rstd.ap[:, a:a + 1], None, op0=ALU.mult),
             reads=[ht, rstd], writes=[xh])
    k = 0
    for c in range(8):
        pt = ptr_ring.next()
        for a in range(na):
            K.op(K.pe, lambda: nc.tensor.transpose(pt.ap[:, a * 128:(a + 1) * 128], xh.ap[:, a, c * 128:(c + 1) * 128], C.identb.ap),
                 reads=[xh, C.identb], writes=[pt])
        for (xT, g) in outs:
            if k % 2 == 0:
                K.op(K.act, lambda: nc.scalar.activation(xT.ap[:, c, :], pt.ap[:, 0:na * 128], AF.Identity, scale=g[:, c:c + 1]),
                     reads=[pt, C.veca], writes=[xT])
            else:
                K.op(K.dve, lambda: nc.vector.tensor_scalar(xT.ap[:, c, :], pt.ap[:, 0:na * 128], g[:, c:c + 1], None, op0=ALU.mult),
                     reads=[pt, C.veca], writes=[xT])
            k += 1


def phase_p1a(K, nc, C, x_d, w_in, sz_scr, xbc_scr, dt_scr):
    with ExitStack() as es:
        K.stack = es
        K.begin_phase()
        stage = Ring([K.sb("p1a_st%d" % i, [128, 1288], F32) for i in range(2)])
        W = load_weight(K, nc, es, "p1a_w", w_in, 8, 5152, stage, [K.pool, K.dve, K.act])
        xt_ring = Ring([K.sb("p1a_x%d" % i, [128, 4, 1024], F32) for i in range(1)])
        ss = K.sb("p1a_ss", [128, 4], F32)
        lnv = K.sb("p1a_ln", [128, 4], F32)
        rstd = K.sb("p1a_rstd", [128, 4], F32)
        junk = K.sb("p1a_junk", [128, 1024], BF16)
        xh = K.sb("p1a_xh", [128, 4, 1024], BF16)
        xT = K.sb("p1a_xT", [128, 8, 512], BF16)
        ptr = Ring([K.ps("p1a_pt%d" % i, [128, 1024], BF16) for i in range(2)])
        pmm = Ring([K.ps("p1a_pm%d" % i, [128, 512], F32) for i in range(4)])
        pdt = K.ps("p1a_pdt", [128, 512], F32)
        ost_ring = Ring([K.sb("p1a_ost%d" % i, [128, 512], BF16) for i in range(6)])
        raw_ring = Ring([K.sb("p1a_raw%d" % i, [128, 515], F32) for i in range(3)])
        acc_ring = Ring([K.sb("p1a_acc%d" % i, [128, 512], F32) for i in range(2)])
        halo = K.sb("p1a_halo", [128, 24, 3], F32)
        halos = [K.view("p1a_halo%d" % i, halo.ap[:, i, :]) for i in range(24)]
        K.op(K.pool, lambda: nc.gpsimd.memset(halo.ap, 0.0), writes=halos)
        dtx = K.sb("p1a_dtx", [128, 4, 32], F32)
        dta = K.sb("p1a_dta", [128, 4, 32], F32)
        dte = K.sb("p1a_dte", [128, 4, 32], F32)
        dtm = K.sb("p1a_dtm", [128, 4, 32], F32)
        dto = K.sb("p1a_dto", [128, 4, 32], F32)
        szd = K.dram("sz_scr", sz_scr)
        xbd = K.dram("xbc_scr", xbc_scr)
        dtd = K.dram("dt_scr", dt_scr)
        xv = x_d.rearrange("(t a p) d -> t p a d", a=4, p=128)
        for ts in range(8):
            xt = xt_ring.next()
            K.dma(K.sp, xt.ap, xv[ts], writes=[xt])
            rms_to_xT(K, nc, C, xt, 4, ss, lnv, rstd, junk, xh, ptr, [(xT, C.g_m)])
            tok = slice(ts * 512, (ts + 1) * 512)
            for ft in range(40):
                pm = pmm.next()
                for c in range(8):
                    K.op(K.pe, lambda: nc.tensor.matmul(pm.ap, W[c].ap[:, ft * 128:(ft + 1) * 128], xT.ap[:, c, :],
                                                        start=(c == 0), stop=(c == 7)), reads=[W[c], xT], writes=[pm])
                ost = ost_ring.next()
                if ft < 16:
                    K.op(K.act, lambda: nc.scalar.activation(ost.ap, pm.ap, AF.Silu), reads=[pm], writes=[ost])
                    K.dma(K.sp, sz_scr[ft * 128:(ft + 1) * 128, tok], ost.ap, reads=[ost], writes=[szd])
                else:
                    ci = ft - 16
                    raw = raw_ring.next()
                    acc = acc_ring.next()
                    K.op(K.pool, lambda: nc.gpsimd.tensor_copy(raw.ap[:, 0:3], halos[ci].ap), reads=[halos[ci]], writes=[raw])
                    K.op(K.act, lambda: nc.scalar.copy(raw.ap[:, 3:515], pm.ap), reads=[pm], writes=[raw])
                    K.op(K.pool, lambda: nc.gpsimd.tensor_copy(halos[ci].ap, raw.ap[:, 512:515]), reads=[raw], writes=[halos[ci]])
                    cw = lambda j: C.conv_w[:, j * 24 + ci:j * 24 + ci + 1]
                    K.op(K.dve, lambda: nc.vector.tensor_scalar(acc.ap, raw.ap[:, 0:512], cw(0), None, op0=ALU.mult),
                         reads=[raw, C.vecb], writes=[acc])
                    for j in range(1, 4):
                        K.op(K.dve, lambda: nc.vector.scalar_tensor_tensor(acc.ap, raw.ap[:, j:j + 512], cw(j), acc.ap,
                                                                           op0=ALU.mult, op1=ALU.add),
                             reads=[raw, acc, C.vecb], writes=[acc])
                    K.op(K.act, lambda: nc.scalar.activation(ost.ap, acc.ap, AF.Silu, bias=C.conv_b[:, ci:ci + 1]),
                         reads=[acc, C.veca], writes=[ost])
                    K.dma(K.sp, xbc_scr[ci * 128:(ci + 1) * 128, tok], ost.ap, reads=[ost], writes=[xbd])
            for a in range(4):
                for c in range(8):
                    K.op(K.pe, lambda: nc.tensor.matmul(pdt.ap[:, a * 32:(a + 1) * 32], xT.ap[:, c, a * 128:(a + 1) * 128],
                                                        W[c].ap[:, 5120:5152], start=(c == 0), stop=(c == 7)),
                         reads=[W[c], xT], writes=[pdt])
            pv = pdt.ap[:, 0:128].rearrange("p (a h) -> p a h", a=4)
            K.op(K.dve, lambda: nc.vector.tensor_tensor(dtx.ap, pv, C.bc.ap[:, 0:1, :].to_broadcast([128, 4, 32]), op=ALU.add),
                 reads=[pdt, C.bc], writes=[dtx])
            K.op(K.act, lambda: nc.scalar.activation(dta.ap, dtx.ap, AF.Abs), reads=[dtx], writes=[dta])
            K.op(K.act, lambda: nc.scalar.activation(dte.ap, dta.ap, AF.Exp, scale=-1.0), reads=[dta], writes=[dte])
            K.op(K.act, lambda: nc.scalar.activation(dte.ap, dte.ap, AF.Ln, bias=C.one.ap, scale=1.0), reads=[dte, C.one], writes=[dte])
            K.op(K.dve, lambda: nc.vector.tensor_scalar(dtm.ap, dtx.ap, 0.0, None, op0=ALU.max), reads=[dtx], writes=[dtm])
            K.op(K.dve, lambda: nc.vector.tensor_tensor(dto.ap, dtm.ap, dte.ap, op=ALU.add), reads=[dtm, dte], writes=[dto])
            K.dma(K.sp, dt_scr.rearrange("(t a p) h -> t p a h", a=4, p=128)[ts], dto.ap, reads=[dto], writes=[dtd])
        K.end_phase()
        K.stack = K.es


def phase_p1b(K, nc, C, sz_scr, xbc_scr, dt_scr, yn_scr, nchunks=32):
    with ExitStack() as es:
        K.stack = es
        K.begin_phase()
        xb_ring = Ring([K.sb("p1b_xb%d" % i, [128, 24, 128], BF16) for i in range(2)])
        sz_ring = Ring([K.sb("p1b_sz%d" % i, [128, 16, 128], BF16) for i in range(2)])
        dt_ring = Ring([K.sb("p1b_dt%d" % i, [128, 32], F32) for i in range(2)])
        a_ch = K.sb("p1b_a", [128, 32], F32)
        acum = K.sb("p1b_acum", [128, 32], F32)
        tmpw = K.sb("p1b_tmpw", [128, 32], F32)
        wend = K.sb("p1b_wend", [128, 32], F32)
        dtw = K.sb("p1b_dtw", [128, 32], F32)
        dA = K.sb("p1b_dA", [128, 32], F32)
        xdt_pad = K.sb("p1b_xdtp", [128, 32, 128], BF16)
        xdtw = K.sb("p1b_xdtw", [128, 32, 64], BF16)
        btok = K.sb("p1b_btok", [128, 4, 128], BF16)
        state = K.sb("p1b_state", [128, 32, 64], F32)
        st_pad = K.sb("p1b_stp", [128, 32, 128], BF16)
        cbm = K.sb("p1b_cbm", [128, 4, 128], F32)
        seg_ring = Ring([K.sb("p1b_seg%d" % i, [128, 4, 128], F32) for i in range(2)])
        dec_ring = Ring([K.sb("p1b_dec%d" % i, [128, 4, 128], F32) for i in range(2)])
        ea_ring = Ring([K.sb("p1b_ea%d" % i, [128, 4, 128], F32) for i in range(2)])
        mt_ring = Ring([K.sb("p1b_mt%d" % i, [128, 4, 128], BF16) for i in range(3)])
        cs_ring = Ring([K.sb("p1b_cs%d" % i, [128, 4, 128], BF16) for i in range(3)])
        ytmp_ring = Ring([K.sb("p1b_yt%d" % i, [128, 128], F32) for i in range(2)])
        ygate = K.sb("p1b_yg", [128, 16, 128], F32)
        sq = K.sb("p1b_sq", [128, 16, 128], BF16)
        grs = K.sb("p1b_grs", [128, 4, 128], F32)
        yn_ring = Ring([K.sb("p1b_yn%d" % i, [128, 16, 128], BF16) for i in range(2)])
        pxs = K.ps("p1b_pxs", [128, 2048], BF16)
        pb = K.ps("p1b_pb", [128, 1024], BF16)
        psm = K.ps("p1b_psm", [128, 512], F32)
        pabc = Ring([K.ps("p1b_pabc%d" % i, [128, 512], F32) for i in range(2)])
        py_ring = Ring([K.ps("p1b_py%d" % i, [128, 512], F32) for i in range(2)])
        pupd = psm
        for t_ in (xdt_pad, st_pad, state):
            K.op(K.pool, lambda: nc.gpsimd.memset(t_.ap, 0.0), writes=[t_])
        szd = K.dram("sz_scr", sz_scr)
        xbd = K.dram("xbc_scr", xbc_scr)
        dtd = K.dram("dt_scr", dt_scr)
        ynd = K.dram("yn_scr", yn_scr)
        xbv = xbc_scr.rearrange("(q p) t -> p q t", p=128)
        szv = sz_scr.rearrange("(q p) t -> p q t", p=128)
        ynv = yn_scr.rearrange("(q p) t -> p q t", p=128)
        xdt4 = xdt_pad.ap.rearrange("p (q r) c -> p q r c", r=2)
        stp4 = st_pad.ap.rearrange("p (q r) c -> p q r c", r=2)
        for ch in range(nchunks):
            cols = slice(ch * 128, (ch + 1) * 128)
            xb = xb_ring.next()
            sz = sz_ring.next()
            dt = dt_ring.next()
            K.dma(K.sp, xb.ap, xbv[:, :, cols], reads=[xbd], writes=[xb])
            K.dma(K.sp, sz.ap, szv[:, :, cols], reads=[szd], writes=[sz])
            K.dma(K.sp, dt.ap, dt_scr[ch * 128:(ch + 1) * 128, :], reads=[dtd], writes=[dt])
            if _DBG_STOP <= 1:
                continue
            K.op(K.dve, lambda: nc.vector.tensor_tensor(a_ch.ap, dt.ap, C.A.ap, op=ALU.mult), reads=[dt, C.A], writes=[a_ch])
            K.op(K.pe, lambda: nc.tensor.matmul(psm.ap[:, 0:32], C.tri.ap, a_ch.ap, start=True, stop=True),
                 reads=[C.tri, a_ch], writes=[psm])
            K.op(K.pe, lambda: nc.tensor.matmul(psm.ap[:, 32:64], C.onesf.ap, a_ch.ap, start=True, stop=True),
                 reads=[C.onesf, a_ch], writes=[psm])
            K.op(K.act, lambda: nc.scalar.copy(acum.ap, psm.ap[:, 0:32]), reads=[psm], writes=[acum])
            K.op(K.dve, lambda: nc.vector.tensor_tensor(tmpw.ap, psm.ap[:, 32:64], acum.ap, op=ALU.subtract),
                 reads=[psm, acum], writes=[tmpw])
            K.op(K.act, lambda: nc.scalar.activation(wend.ap, tmpw.ap, AF.Exp), reads=[tmpw], writes=[wend])
            K.op(K.act, lambda: nc.scalar.activation(dA.ap, psm.ap[:, 32:64], AF.Exp), reads=[psm], writes=[dA])
            K.op(K.dve, lambda: nc.vector.tensor_tensor(dtw.ap, dt.ap, wend.ap, op=ALU.mult), reads=[dt, wend], writes=[dtw])
            if _DBG_STOP <= 2:
                continue
            for ci in range(16):
                K.op(K.pe, lambda: nc.tensor.transpose(pxs.ap[:, ci * 128:(ci + 1) * 128], xb.ap[:, ci, :], C.identb.ap),
                     reads=[xb, C.identb], writes=[pxs])
            for g in range(4):
                K.op(K.pe, lambda: nc.tensor.transpose(pb.ap[:, g * 128:(g + 1) * 128], xb.ap[:, 16 + g, :], C.identb.ap),
                     reads=[xb, C.identb], writes=[pb])
            pxs4 = pxs.ap.rearrange("p (q r c) -> p q r c", r=2, c=64)
            dt3 = dt.ap.rearrange("p (q r) -> p q r", r=2)
            for r in range(2):
                K.op(K.dve, lambda: nc.vector.tensor_tensor(xdt4[:, :, r, r * 64:(r + 1) * 64], pxs4[:, :, r, :],
                                                            dt3[:, :, r:r + 1].to_broadcast([128, 16, 64]), op=ALU.mult),
                     reads=[pxs, dt], writes=[xdt_pad])
            K.op(K.dve, lambda: nc.vector.tensor_tensor(xdtw.ap, pxs.ap.rearrange("p (h c) -> p h c", c=64),
                                                        dtw.ap.unsqueeze(2).to_broadcast([128, 32, 64]), op=ALU.mult),
                 reads=[pxs, dtw], writes=[xdtw])
            K.op(K.act, lambda: nc.scalar.copy(btok.ap.rearrange("p g n -> p (g n)"), pb.ap[:, 0:512]), reads=[pb], writes=[btok])
            if _DBG_STOP <= 3:
                continue
            for g in range(4):
                K.op(K.pe, lambda: nc.tensor.matmul(pupd.ap[:, g * 128:(g + 1) * 128],
                                                    xb.ap[:, 16 + g, :], xb.ap[:, 20 + g, :], start=True, stop=True),
                     reads=[xb], writes=[pupd])
            for g in range(4):
                K.op(K.dve, lambda: nc.vector.tensor_tensor(cbm.ap[:, g, :], pupd.ap[:, g * 128:(g + 1) * 128], C.tri.ap, op=ALU.mult),
                     reads=[pupd, C.tri], writes=[cbm])
            if _DBG_STOP <= 4:
                continue
            mts = {}
            css = {}
            pas = {}

            def emit_abc(hq):
                pa = pabc.next()
                for i in range(4):
                    h = hq * 4 + i
                    K.op(K.pe, lambda: nc.tensor.matmul(pa.ap[:, i * 128:(i + 1) * 128], a_ch.ap[:, h:h + 1].to_broadcast([128, 128]),
                                                        C.tri.ap, start=True, stop=True), reads=[a_ch, C.tri], writes=[pa])
                K.pe.signal_last()
                pas[hq] = pa

            emit_abc(0)
            for hq in range(8):
                g = hq // 2
                pa = pas[hq]
                seg = seg_ring.next()
                dec = dec_ring.next()
                ea = ea_ring.next()
                mt = mt_ring.next()
                cs = cs_ring.next()
                for i in range(4):
                    h = hq * 4 + i
                    K.op(K.dve, lambda: nc.vector.tensor_scalar(seg.ap[:, i, :], pa.ap[:, i * 128:(i + 1) * 128], acum.ap[:, h:h + 1], 0.0,
                                                                op0=ALU.subtract, op1=ALU.min), reads=[pa, acum], writes=[seg])
                K.op(K.act, lambda: nc.scalar.activation(dec.ap, seg.ap, AF.Exp), reads=[seg], writes=[dec])
                K.op(K.act, lambda: nc.scalar.activation(ea.ap.rearrange("p i l -> p (i l)"), pa.ap, AF.Exp), reads=[pa], writes=[ea])
                for i in range(4):
                    K.op(K.pool, lambda: nc.gpsimd.tensor_tensor(mt.ap[:, i, :], dec.ap[:, i, :], cbm.ap[:, g, :], op=ALU.mult),
                         reads=[dec, cbm], writes=[mt])
                    K.op(K.pool, lambda: nc.gpsimd.tensor_tensor(cs.ap[:, i, :], ea.ap[:, i, :], xb.ap[:, 20 + g, :], op=ALU.mult),
                         reads=[ea, xb], writes=[cs])
                if hq + 1 < 8:
                    emit_abc(hq + 1)
                for pi in range(2):
                    q = hq * 2 + pi
                    pyq = py_ring.next()
                    yo = pyq.ap[:, 0:128]
                    ops = [(xdt_pad, xdt_pad.ap[:, 2 * q, :], mt, mt.ap[:, 2 * pi, :]),
                           (xdt_pad, xdt_pad.ap[:, 2 * q + 1, :], mt, mt.ap[:, 2 * pi + 1, :]),
                           (st_pad, st_pad.ap[:, 2 * q, :], cs, cs.ap[:, 2 * pi, :]),
                           (st_pad, st_pad.ap[:, 2 * q + 1, :], cs, cs.ap[:, 2 * pi + 1, :])]
                    for k_, (lt, la, rt, ra) in enumerate(ops):
                        K.op(K.pe, lambda: nc.tensor.matmul(yo, la, ra, start=(k_ == 0), stop=(k_ == 3)), reads=[lt, rt], writes=[pyq])
                    yt = ytmp_ring.next()
                    K.op(K.dve, lambda: nc.vector.scalar_tensor_tensor(yt.ap, xb.ap[:, q, :], C.Dcol.ap[:, q:q + 1], yo,
                                                                       op0=ALU.mult, op1=ALU.add), reads=[xb, C.Dcol, pyq], writes=[yt])
                    K.op(K.pool, lambda: nc.gpsimd.tensor_tensor(ygate.ap[:, q, :], yt.ap, sz.ap[:, q, :], op=ALU.mult),
                         reads=[yt, sz], writes=[ygate])
            if _DBG_STOP <= 5:
                continue
            for g in range(4):
                K.op(K.pe, lambda: nc.tensor.matmul(pupd.ap, btok.ap[:, g, :], xdtw.ap[:, g * 8:(g + 1) * 8, :].rearrange("p h c -> p (h c)"),
                                                    start=True, stop=True), reads=[btok, xdtw], writes=[pupd])
                sv = state.ap[:, g * 8:(g + 1) * 8, :]
                K.op(K.dve, lambda: nc.vector.tensor_tensor(sv, sv, dA.ap[:, g * 8:(g + 1) * 8].unsqueeze(2).to_broadcast([128, 8, 64]),
                                                            op=ALU.mult), reads=[state, dA], writes=[state])
                K.op(K.dve, lambda: nc.vector.tensor_tensor(sv, sv, pupd.ap.rearrange("p (h c) -> p h c", c=64), op=ALU.add),
                     reads=[state, pupd], writes=[state])
            st4 = state.ap.rearrange("p (q r) c -> p q r c", r=2)
            for r in range(2):
                K.op(K.act, lambda: nc.scalar.copy(stp4[:, :, r, r * 64:(r + 1) * 64], st4[:, :, r, :]), reads=[state], writes=[st_pad])
            if _DBG_STOP <= 6:
                continue
            K.op(K.act, lambda: nc.scalar.activation(sq.ap, ygate.ap, AF.Square), reads=[ygate], writes=[sq])
            for G in range(4):
                for k_ in range(4):
                    K.op(K.pe, lambda: nc.tensor.matmul(psm.ap[:, G * 128:(G + 1) * 128], C.onesb.ap, sq.ap[:, G * 4 + k_, :],
                                                        start=(k_ == 0), stop=(k_ == 3)), reads=[C.onesb, sq], writes=[psm])
            K.op(K.act, lambda: nc.scalar.activation(grs.ap.rearrange("p g l -> p (g l)"), psm.ap, AF.Ln, bias=C.eps.ap, scale=1.0 / 512),
                 reads=[psm, C.eps], writes=[grs])
            K.op(K.act, lambda: nc.scalar.activation(grs.ap, grs.ap, AF.Exp, scale=-0.5), reads=[grs], writes=[grs])
            yn = yn_ring.next()
            for q in range(16):
                K.op(K.dve, lambda: nc.vector.scalar_tensor_tensor(yn.ap[:, q, :], ygate.ap[:, q, :], C.g_y[:, q:q + 1], grs.ap[:, q // 4, :],
                                                                   op0=ALU.mult, op1=ALU.mult), reads=[ygate, C.veca, grs], writes=[yn])
            K.dma(K.sp, ynv[:, :, cols], yn.ap, reads=[yn], writes=[ynd])
        K.end_phase()
        K.stack = K.es


def phase_out_ple(K, nc, C, name, a_scr, FC, w_o, h_in, p_in, g_ple, w_gate, w_proj, h_out):
    with ExitStack() as es:
        K.stack = es
        K.begin_phase()
        stage = Ring([K.sb(name + "_st%d" % i, [128, 1024], F32) for i in range(3)])
        Wo = load_weight(K, nc, es, name + "_wo", w_o, FC, 1024, stage, [K.pool, K.dve, K.act])
        Wg = load_weight(K, nc, es, name + "_wg", w_gate, 8, 1024, stage, [K.pool, K.dve, K.act])
        Wp = load_weight(K, nc, es, name + "_wp", w_proj, 2, 1024, stage, [K.pool, K.dve, K.act])
        at_ring = Ring([K.sb(name + "_at%d" % i, [128, FC, 512], BF16) for i in range(2)])
        h_ring = Ring([K.sb(name + "_h%d" % i, [128, 4, 1024], F32) for i in range(2)])
        p_ring = Ring([K.sb(name + "_p%d" % i, [128, 4, 256], F32) for i in range(2)])
        pbf = K.sb(name + "_pbf", [128, 4, 256], BF16)
        pT = K.sb(name + "_pT", [128, 2, 512], BF16)
        ss = K.sb(name + "_ss", [128, 4], F32)
        lnv = K.sb(name + "_ln", [128, 4], F32)
        rstd = K.sb(name + "_rstd", [128, 4], F32)
        junk = K.sb(name + "_junk", [128, 1024], BF16)
        xh = K.sb(name + "_xh", [128, 4, 1024], BF16)
        xT = K.sb(name + "_xT", [128, 8, 512], BF16)
        sig_ring = Ring([K.sb(name + "_sig%d" % i, [128, 512], F32) for i in range(2)])
        tmp_ring = Ring([K.sb(name + "_tmp%d" % i, [128, 512], F32) for i in range(2)])
        ho_ring = Ring([K.sb(name + "_ho%d" % i, [128, 4, 1024], F32) for i in range(1)])
        ptr = Ring([K.ps(name + "_pt%d" % i, [128, 1024], BF16) for i in range(2)])
        pmm = Ring([K.ps(name + "_pm%d" % i, [128, 512], F32) for i in range(2)])
        pg_ring = Ring([K.ps(name + "_pg%d" % i, [128, 512], F32) for i in range(2)])
        pp_ring = Ring([K.ps(name + "_pp%d" % i, [128, 512], F32) for i in range(2)])
        ad = K.dram(name + "_a", a_scr)
        hd = K.dram(name + "_hin", h_in)
        od = K.dram(name + "_hout", h_out)
        av = a_scr.rearrange("(c p) t -> p c t", p=128)
        hv = h_in.rearrange("(t a p) d -> t p a d", a=4, p=128)
        pv = p_in.rearrange("(t a p) d -> t p a d", a=4, p=128)
        ov = h_out.rearrange("(t a p) d -> t p a d", a=4, p=128)
        loaded = {}

        def issue_loads(ts):
            at = at_ring.next()
            ht = h_ring.next()
            pt_ = p_ring.next()
            K.dma(K.sp, at.ap, av[:, :, ts * 512:(ts + 1) * 512], reads=[ad], writes=[at])
            K.dma(K.sp, ht.ap, hv[ts], reads=[hd], writes=[ht])
            K.dma(K.sp, pt_.ap, pv[ts], writes=[pt_])
            loaded[ts] = (at, ht, pt_)

        issue_loads(0)
        for ts in range(8):
            if ts + 1 < 8:
                issue_loads(ts + 1)
            at, ht, pt_ = loaded.pop(ts)
            for a in range(4):
                for half in range(2):
                    pm = pmm.next()
                    for c in range(FC):
                        K.op(K.pe, lambda: nc.tensor.matmul(pm.ap, at.ap[:, c, a * 128:(a + 1) * 128], Wo[c].ap[:, half * 512:(half + 1) * 512],
                                                            start=(c == 0), stop=(c == FC - 1)), reads=[at, Wo[c]], writes=[pm])
                    hs = ht.ap[:, a, half * 512:(half + 1) * 512]
                    K.op(K.dve, lambda: nc.vector.tensor_tensor(hs, hs, pm.ap, op=ALU.add), reads=[ht, pm], writes=[ht])
            rms_to_xT(K, nc, C, ht, 4, ss, lnv, rstd, junk, xh, ptr, [(xT, g_ple)])
            K.op(K.pool, lambda: nc.gpsimd.tensor_copy(pbf.ap, pt_.ap), reads=[pt_], writes=[pbf])
            for c2 in range(2):
                pt = ptr.next()
                for a in range(4):
                    K.op(K.pe, lambda: nc.tensor.transpose(pt.ap[:, a * 128:(a + 1) * 128], pbf.ap[:, a, c2 * 128:(c2 + 1) * 128], C.identb.ap),
                         reads=[pbf, C.identb], writes=[pt])
                K.op(K.act, lambda: nc.scalar.copy(pT.ap[:, c2, :], pt.ap[:, 0:512]), reads=[pt], writes=[pT])
            ho = ho_ring.next()
            for a in range(4):
                for half in range(2):
                    pg = pg_ring.next()
                    pp = pp_ring.next()
                    cs_ = slice(half * 512, (half + 1) * 512)
                    for c in range(8):
                        K.op(K.pe, lambda: nc.tensor.matmul(pg.ap, xT.ap[:, c, a * 128:(a + 1) * 128], Wg[c].ap[:, cs_],
                                                            start=(c == 0), stop=(c == 7)), reads=[xT, Wg[c]], writes=[pg])
                    for c in range(2):
                        K.op(K.pe, lambda: nc.tensor.matmul(pp.ap, pT.ap[:, c, a * 128:(a + 1) * 128], Wp[c].ap[:, cs_],
                                                            start=(c == 0), stop=(c == 1)), reads=[pT, Wp[c]], writes=[pp])
                    sg = sig_ring.next()
                    tm = tmp_ring.next()
                    K.op(K.act, lambda: nc.scalar.activation(sg.ap, pg.ap, AF.Sigmoid), reads=[pg], writes=[sg])
                    K.op(K.dve, lambda: nc.vector.tensor_tensor(tm.ap, pp.ap, sg.ap, op=ALU.mult), reads=[pp, sg], writes=[tm])
                    K.op(K.pool, lambda: nc.gpsimd.tensor_tensor(ho.ap[:, a, cs_], tm.ap, ht.ap[:, a, cs_], op=ALU.add),
                         reads=[tm, ht], writes=[ho])
            K.dma(K.sp, ov[ts], ho.ap, reads=[ho], writes=[od])
        K.end_phase()
        K.stack = K.es


def phase_p3(K, nc, C, h1, w_kv, s_in, kt_scr, v_scr, q_scr, sg_scr):
    with ExitStack() as es:
        K.stack = es
        K.begin_phase()
        stage = Ring([K.sb("p3_st%d" % i, [128, 2048], F32) for i in range(2)])
        Wkv = load_weight(K, nc, es, "p3_wkv", w_kv, 8, 2048, stage, [K.pool, K.dve, K.act])
        Wqg = load_weight(K, nc, es, "p3_wqg", s_in, 8, 2048, stage, [K.pool, K.dve, K.act])
        h_ring = Ring([K.sb("p3_h%d" % i, [128, 4, 1024], F32) for i in range(2)])
        ss = K.sb("p3_ss", [128, 4], F32)
        lnv = K.sb("p3_ln", [128, 4], F32)
        rstd = K.sb("p3_rstd", [128, 4], F32)
        junk = K.sb("p3_junk", [128, 1024], BF16)
        xh = K.sb("p3_xh", [128, 4, 1024], BF16)
        xTk = K.sb("p3_xTk", [128, 8, 512], BF16)
        xTq = K.sb("p3_xTq", [128, 8, 512], BF16)
        sq_ring = Ring([K.sb("p3_sq%d" % i, [128, 512], BF16) for i in range(2)])
        rs_ring = Ring([K.sb("p3_rs%d" % i, [128, 512], F32) for i in range(2)])
        kst = K.sb("p3_kst", [128, 8, 512], BF16)
        vst = K.sb("p3_vst", [128, 4, 1024], BF16)
        qst = K.sb("p3_qst", [128, 8, 512], BF16)
        gst = K.sb("p3_gst", [128, 8, 512], BF16)
        ptr = Ring([K.ps("p3_pt%d" % i, [128, 1024], BF16) for i in range(2)])
        pmm = Ring([K.ps("p3_pm%d" % i, [128, 512], F32) for i in range(3)])
        pn_ring = Ring([K.ps("p3_pn%d" % i, [128, 512], F32) for i in range(2)])
        hd = K.dram("p3_h1", h1)
        kd = K.dram("kt_scr", kt_scr)
        vd = K.dram("v_scr", v_scr)
        qd = K.dram("q_scr", q_scr)
        gd = K.dram("sg_scr", sg_scr)
        hv = h1.rearrange("(t a p) d -> t p a d", a=4, p=128)
        vv = v_scr.rearrange("(t a p) d -> t p a d", a=4, p=128)

        def headnorm(pm, gcol, out_ap, out_t):
            sq = sq_ring.next()
            rs = rs_ring.next()
            pn = pn_ring.next()
            K.op(K.act, lambda: nc.scalar.activation(sq.ap, pm.ap, AF.Square), reads=[pm], writes=[sq])
            K.op(K.pe, lambda: nc.tensor.matmul(pn.ap, C.blk.ap, sq.ap, start=True, stop=True), reads=[C.blk, sq], writes=[pn])
            K.op(K.act, lambda: nc.scalar.activation(rs.ap, pn.ap, AF.Ln, bias=C.eps.ap, scale=1.0 / 64), reads=[pn, C.eps], writes=[rs])
            K.op(K.act, lambda: nc.scalar.activation(rs.ap, rs.ap, AF.Exp, scale=-0.5), reads=[rs], writes=[rs])
            K.op(K.dve, lambda: nc.vector.scalar_tensor_tensor(out_ap, pm.ap, gcol, rs.ap, op0=ALU.mult, op1=ALU.mult),
                 reads=[pm, C.kq, rs], writes=[out_t])

        for ts in range(8):
            ht = h_ring.next()
            K.dma(K.sp, ht.ap, hv[ts], reads=[hd], writes=[ht])
            rms_to_xT(K, nc, C, ht, 4, ss, lnv, rstd, junk, xh, ptr, [(xTk, C.g_kv), (xTq, C.g_s)])
            tok = slice(ts * 512, (ts + 1) * 512)
            for fo in range(8):
                pm = pmm.next()
                for c in range(8):
                    K.op(K.pe, lambda: nc.tensor.matmul(pm.ap, Wkv[c].ap[:, fo * 128:(fo + 1) * 128], xTk.ap[:, c, :],
                                                        start=(c == 0), stop=(c == 7)), reads=[Wkv[c], xTk], writes=[pm])
                headnorm(pm, C.kq.ap[:, 0:1], kst.ap[:, fo, :], kst)
            for a in range(4):
                for half in range(2):
                    pm = pmm.next()
                    for c in range(8):
                        K.op(K.pe, lambda: nc.tensor.matmul(pm.ap, xTk.ap[:, c, a * 128:(a + 1) * 128],
                                                            Wkv[c].ap[:, 1024 + half * 512:1024 + (half + 1) * 512],
                                                            start=(c == 0), stop=(c == 7)), reads=[Wkv[c], xTk], writes=[pm])
                    K.op(K.act, lambda: nc.scalar.copy(vst.ap[:, a, half * 512:(half + 1) * 512], pm.ap), reads=[pm], writes=[vst])
            for fo in range(8):
                pm = pmm.next()
                for c in range(8):
                    K.op(K.pe, lambda: nc.tensor.matmul(pm.ap, Wqg[c].ap[:, fo * 128:(fo + 1) * 128], xTq.ap[:, c, :],
                                                        start=(c == 0), stop=(c == 7)), reads=[Wqg[c], xTq], writes=[pm])
                headnorm(pm, C.kq.ap[:, 1:2], qst.ap[:, fo, :], qst)
            for fo in range(8):
                pm = pmm.next()
                for c in range(8):
                    K.op(K.pe, lambda: nc.tensor.matmul(pm.ap, Wqg[c].ap[:, 1024 + fo * 128:1024 + (fo + 1) * 128], xTq.ap[:, c, :],
                                                        start=(c == 0), stop=(c == 7)), reads=[Wqg[c], xTq], writes=[pm])
                K.op(K.act, lambda: nc.scalar.activation(gst.ap[:, fo, :], pm.ap, AF.Silu), reads=[pm], writes=[gst])
            for hf in range(2):
                qs = slice(hf * 4, (hf + 1) * 4)
                K.dma(K.sp, kt_scr.rearrange("(q p) t -> p q t", p=128)[:, qs, tok], kst.ap[:, qs, :], reads=[kst], writes=[kd])
                K.dma(K.sp, q_scr.rearrange("(q p) t -> p q t", p=128)[:, qs, tok], qst.ap[:, qs, :], reads=[qst], writes=[qd])
                K.dma(K.sp, sg_scr.rearrange("(q p) t -> p q t", p=128)[:, qs, tok], gst.ap[:, qs, :], reads=[gst], writes=[gd])
            K.dma(K.sp, vv[ts], vst.ap, reads=[vst], writes=[vd])
        K.end_phase()
        K.stack = K.es


def phase_p4(K, nc, C, kt_scr, v_scr, q_scr, sg_scr, og_scr, nTB=8, nQ=8):
    with ExitStack() as es:
        K.stack = es
        K.begin_phase()
        ntri = K.sb("p4_ntri", [128, 128], BF16)
        ebig = K.sb("p4_ebig", [128, 255], BF16)
        nsel = K.sb("p4_nsel", [128, 32, 128], BF16)
        masks = K.sb("p4_masks", [128, 4, 512], BF16)
        with ExitStack() as est:
            K.stack = est
            negb = K.sb("p4_negb", [128, 128], BF16)
            negb3 = K.sb("p4_negb3", [128, 32, 128], BF16)
            oneb3 = K.sb("p4_oneb3", [128, 4, 512], BF16)
            K.op(K.pool, lambda: nc.gpsimd.memset(negb.ap, -1.0), writes=[negb])
            K.op(K.pool, lambda: nc.gpsimd.affine_select(ntri.ap, negb.ap, pattern=[[-1, 128]], compare_op=ALU.is_ge, fill=0.0,
                                                         base=0, channel_multiplier=1), reads=[negb], writes=[ntri])
            K.op(K.pool, lambda: nc.gpsimd.memset(ebig.ap, 0.0), writes=[ebig])
            K.op(K.pool, lambda: nc.gpsimd.memset(ebig.ap[:, 127:128], 1.0), writes=[ebig])
            K.op(K.pool, lambda: nc.gpsimd.memset(negb3.ap, -1.0), writes=[negb3])
            K.op(K.pool, lambda: nc.gpsimd.affine_select(nsel.ap, negb3.ap, pattern=[[-1, 32], [0, 128]], compare_op=ALU.is_ge, fill=0.0,
                                                         base=-1, channel_multiplier=1), reads=[negb3], writes=[nsel])
            K.op(K.pool, lambda: nc.gpsimd.memset(oneb3.ap, 1.0), writes=[oneb3])
            K.op(K.pool, lambda: nc.gpsimd.affine_select(masks.ap, oneb3.ap, pattern=[[-128, 4], [1, 512]], compare_op=ALU.is_gt, fill=0.0,
                                                         base=0, channel_multiplier=-1), reads=[oneb3], writes=[masks])
            K.barrier()
            K.stack = es
        qpad = [Ring([K.sb("p4_qp%d_%d" % (r, i), [128, 512], BF16) for i in range(2)]) for r in range(2)]
        for r in range(2):
            for t_ in qpad[r].t:
                K.op(K.pool, lambda: nc.gpsimd.memset(t_.ap, 0.0), writes=[t_])
        kt_ring = Ring([K.sb("p4_kt%d" % i, [128, S], BF16) for i in range(2)])
        v_ring = Ring([K.sb("p4_v%d" % i, [128, 32, 128], BF16) for i in range(2)])
        sgt_ring = Ring([K.sb("p4_sg%d" % i, [128, 512], BF16) for i in range(2)])
        ogst_ring = Ring([K.sb("p4_og%d" % i, [128, 512], BF16) for i in range(2)])
        SPs = [K.sb("p4_sp%d" % i, [128, 32, 512], BF16) for i in range(2)]
        SPvs = [[K.view("p4_sp%d_%d" % (b, i), SPs[b].ap[:, i, :]) for i in range(32)] for b in range(2)]
        e_ring = Ring([K.sb("p4_e%d" % i, [128, 512], F32) for i in range(3)])
        spt_ring = Ring([K.sb("p4_spt%d" % i, [128, 512], F32) for i in range(2)])
        csb_ring = Ring([K.sb("p4_csb%d" % i, [128, 512], BF16) for i in range(2)])
        wt_ring = Ring([K.sb("p4_wt%d" % i, [128, 512], BF16) for i in range(4)])
        wm_ring = Ring([K.sb("p4_wm%d" % i, [128, 512], BF16) for i in range(2)])
        pz = Ring([K.ps("p4_pz%d" % i, [128, 512], F32) for i in range(3)])
        pcs = K.ps("p4_pcs", [128, 512], F32)
        pgr = Ring([K.ps("p4_pg%d" % i, [128, 512], F32) for i in range(3)])
        po_ring = Ring([K.ps("p4_po%d" % i, [128, 512], F32) for i in range(1)])
        kd = K.dram("kt_scr", kt_scr)
        vd = K.dram("v_scr", v_scr)
        qd = K.dram("q_scr", q_scr)
        gd = K.dram("sg_scr", sg_scr)
        od = K.dram("og_scr", og_scr)
        vv = v_scr.rearrange("(jb p) (q c) -> q p jb c", p=128, c=128)
        heads = [(q, TB, r) for q in range(nQ) for TB in range(nTB) for r in range(2)]
        groups = {}
        qstate = {}
        hstate = {}

        def prepare(i):
            q, TB, r = heads[i]
            if q not in qstate:
                KT = kt_ring.next()
                V = v_ring.next()
                K.dma(K.sp, KT.ap, kt_scr[q * 128:(q + 1) * 128, :], reads=[kd], writes=[KT])
                K.dma(K.sp, V.ap, vv[q], reads=[vd], writes=[V])
                qstate[q] = (KT, V)
            if (q, TB) not in groups:
                tok = slice(TB * 512, (TB + 1) * 512)
                sgt = sgt_ring.next()
                K.dma(K.sp, sgt.ap, sg_scr[q * 128:(q + 1) * 128, tok], reads=[gd], writes=[sgt])
                ogst = ogst_ring.next()
                qps = []
                for r_ in range(2):
                    qp = qpad[r_].next()
                    K.dma(K.sp, qp.ap[r_ * 64:(r_ + 1) * 64, :], q_scr[q * 128 + r_ * 64:q * 128 + (r_ + 1) * 64, tok],
                          reads=[qd], writes=[qp])
                    qps.append(qp)
                groups[(q, TB)] = (sgt, ogst, qps, tok)
            hstate[i] = {"SPv": SPvs[i % 2]}

        def sweep1(i):
            q, TB, r = heads[i]
            KT, V = qstate[q]
            sgt, ogst, qps, tok = groups[(q, TB)]
            qp = qps[r]
            SPv = hstate[i]["SPv"]
            nkb = 4 * (TB + 1)
            zs, es_ = {}, {}

            def emit_z(jb):
                z = pz.next()
                c0 = max(0, jb - 4 * TB) * 128
                K.op(K.pe, lambda: nc.tensor.matmul(z.ap[:, c0:], KT.ap[:, jb * 128:(jb + 1) * 128], qp.ap[:, c0:], start=True, stop=True),
                     reads=[KT, qp], writes=[z])
                K.pe.signal_last()
                zs[jb] = z

            def emit_exp(jb):
                e = e_ring.next()
                c0 = max(0, jb - 4 * TB) * 128
                K.op(K.act, lambda: nc.scalar.activation(e.ap[:, c0:], zs[jb].ap[:, c0:], AF.Exp), reads=[zs[jb]], writes=[e])
                es_[jb] = e

            def emit_ln(jb):
                e = es_[jb]
                rr = jb - 4 * TB
                if rr < 0:
                    K.op(K.act, lambda: nc.scalar.activation(SPv[jb].ap, e.ap, AF.Ln, bias=C.one.ap, scale=1.0),
                         reads=[e, C.one], writes=[SPv[jb]])
                else:
                    spt = spt_ring.next()
                    c0 = rr * 128
                    K.op(K.act, lambda: nc.scalar.activation(spt.ap[:, c0:], e.ap[:, c0:], AF.Ln, bias=C.one.ap, scale=1.0),
                         reads=[e, C.one], writes=[spt])
                    K.op(K.dve, lambda: nc.vector.tensor_tensor(SPv[jb].ap[:, c0:], spt.ap[:, c0:], masks.ap[:, rr, c0:], op=ALU.mult),
                         reads=[spt, masks], writes=[SPv[jb]])

            def emit_cs(jb):
                c0 = max(0, jb - 4 * TB) * 128
                K.op(K.pe, lambda: nc.tensor.matmul(pcs.ap[:, c0:], ebig.ap[:, 127 - jb:255 - jb], SPv[jb].ap[:, c0:],
                                                    start=(jb == 0), stop=(jb == nkb - 1)), reads=[ebig, SPv[jb]], writes=[pcs])

            emit_z(0)
            if nkb > 1:
                emit_z(1)
            emit_exp(0)
            for jb in range(nkb):
                if jb + 1 < nkb:
                    emit_exp(jb + 1)
                emit_ln(jb)
                if jb + 2 < nkb:
                    emit_z(jb + 2)
                emit_cs(jb)
            csb = csb_ring.next()
            K.op(K.dve, lambda: nc.vector.tensor_copy(csb.ap, pcs.ap), reads=[pcs], writes=[csb])
            hstate[i]["csb"] = csb

        def sweep2(i):
            q, TB, r = heads[i]
            KT, V = qstate[q]
            sgt, ogst, qps, tok = groups[(q, TB)]
            qp = qps[r]
            SPv = hstate[i]["SPv"]
            csb = hstate[i]["csb"]
            nkb = 4 * (TB + 1)
            rows = slice(r * 64, (r + 1) * 64)
            po = po_ring.next()
            gs = {}

            def emit_G(jb):
                gq = pgr.next()
                c0 = max(0, jb - 4 * TB) * 128
                K.op(K.pe, lambda: nc.tensor.matmul(gq.ap[:, c0:], ntri.ap, SPv[jb].ap[:, c0:], start=True, stop=False),
                     reads=[ntri, SPv[jb]], writes=[gq])
                K.op(K.pe, lambda: nc.tensor.matmul(gq.ap[:, c0:], KT.ap[:, jb * 128:(jb + 1) * 128], qp.ap[:, c0:], start=False, stop=False),
                     reads=[KT, qp], writes=[gq])
                K.op(K.pe, lambda: nc.tensor.matmul(gq.ap[:, c0:], nsel.ap[:, jb, :], csb.ap[:, c0:], start=False, stop=True),
                     reads=[nsel, csb], writes=[gq])
                K.pe.signal_last()
                gs[jb] = gq

            emit_G(0)
            if nkb > 1:
                emit_G(1)
            for jb in range(nkb):
                gq = gs[jb]
                wt = wt_ring.next()
                rr = jb - 4 * TB
                c0 = max(0, rr) * 128
                K.op(K.act, lambda: nc.scalar.activation(wt.ap[:, c0:], gq.ap[:, c0:], AF.Exp), reads=[gq], writes=[wt])
                if rr >= 0:
                    wm = wm_ring.next()
                    K.op(K.dve, lambda: nc.vector.tensor_tensor(wm.ap[:, c0:], wt.ap[:, c0:], masks.ap[:, rr, c0:], op=ALU.mult),
                         reads=[wt, masks], writes=[wm])
                    wt = wm
                if jb + 2 < nkb:
                    emit_G(jb + 2)
                K.op(K.pe, lambda: nc.tensor.matmul(po.ap[:, c0:], V.ap[:, jb, :], wt.ap[:, c0:], start=(jb == 0), stop=(jb == nkb - 1)),
                     reads=[V, wt], writes=[po])
            K.op(K.dve, lambda: nc.vector.tensor_tensor(ogst.ap[rows, :], po.ap[rows, :], sgt.ap[rows, :], op=ALU.mult),
                 reads=[po, sgt], writes=[ogst])
            if r == 1:
                K.dma(K.sp, og_scr[q * 128:(q + 1) * 128, tok], ogst.ap, reads=[ogst], writes=[od])
            del hstate[i]

        prepare(0)
        sweep1(0)
        for i in range(len(heads)):
            if i + 1 < len(heads):
                prepare(i + 1)
                sweep1(i + 1)
            sweep2(i)
        K.end_phase()
        K.stack = K.es


PARAM_SHAPES = {
    "m_norm": [1, 1024], "m_in": [1, 1024, 5152], "m_conv_w": [1, 4, 3072], "m_conv_b": [1, 3072],
    "m_dt_bias": [1, 32], "m_A_log": [1, 32], "m_D": [1, 32], "m_ynorm": [1, 2048], "m_out": [1, 2048, 1024],
    "kv_norm": [1024], "w_kv": [1024, 2048], "k_norm": [64], "s_norm": [1, 1024], "s_in": [1, 1024, 2048],
    "q_norm": [1, 64], "s_out": [1, 1024, 1024], "ple_norm": [2, 1024], "ple_gate": [2, 1024, 1024],
    "ple_proj": [2, 256, 1024],
}

SCRATCH = {
    "sz_scr": ([2048, S], BF16), "xbc_scr": ([3072, S], BF16), "dt_scr": ([S, 32], F32),
    "yn_scr": ([2048, S], BF16), "h1_scr": ([S, D], F32), "q_scr": ([1024, S], BF16),
    "sg_scr": ([1024, S], BF16), "og_scr": ([1024, S], BF16),
    "kt_scr": ([1024, S], BF16), "v_scr": ([S, 1024], BF16),
}


def build(phases=("p1a", "p1b", "p2", "p3", "p4", "p5"), ext_in=(), ext_out=(), p1b_chunks=32, p4_tb=8, p4_q=8):
    nc = bass.Bass("TRN2", target_bir_lowering=False)
    x = nc.dram_tensor("x", [S, D], F32, kind="ExternalInput").ap()
    p0 = nc.dram_tensor("p0", [S, 256], F32, kind="ExternalInput").ap()
    p1 = nc.dram_tensor("p1", [S, 256], F32, kind="ExternalInput").ap()
    prm = {k: nc.dram_tensor(k, shp, F32, kind="ExternalInput").ap() for k, shp in PARAM_SHAPES.items()}
    out = nc.dram_tensor("out", [S, D], F32, kind="ExternalOutput").ap()
    scr = {}
    for k, (shp, dt_) in SCRATCH.items():
        kind = "ExternalInput" if k in ext_in else ("ExternalOutput" if k in ext_out else "Internal")
        scr[k] = nc.dram_tensor(k, shp, dt_, kind=kind).ap()
    with ExitStack() as es:
        K = Ctx(nc, es)
        C = make_consts(K, nc, prm)
        if "p1a" in phases:
            phase_p1a(K, nc, C, x, prm["m_in"][0], scr["sz_scr"], scr["xbc_scr"], scr["dt_scr"])
        if "p1b" in phases:
            phase_p1b(K, nc, C, scr["sz_scr"], scr["xbc_scr"], scr["dt_scr"], scr["yn_scr"], nchunks=p1b_chunks)
        if "p2" in phases:
            phase_out_ple(K, nc, C, "p2", scr["yn_scr"], 16, prm["m_out"][0], x, p0, C.g_p0,
                          prm["ple_gate"][0], prm["ple_proj"][0], scr["h1_scr"])
        if "p3" in phases:
            phase_p3(K, nc, C, scr["h1_scr"], prm["w_kv"], prm["s_in"][0], scr["kt_scr"], scr["v_scr"], scr["q_scr"], scr["sg_scr"])
        if "p4" in phases:
            phase_p4(K, nc, C, scr["kt_scr"], scr["v_scr"], scr["q_scr"], scr["sg_scr"], scr["og_scr"], nTB=p4_tb, nQ=p4_q)
        if "p5" in phases:
            phase_out_ple(K, nc, C, "p5", scr["og_scr"], 8, prm["s_out"][0], scr["h1_scr"], p1, C.g_p1,
                          prm["ple_gate"][1], prm["ple_proj"][1], out)
        K.finish()
    return nc


_NC_CACHE = {}


def kernel(**inputs):
    if "full" not in _NC_CACHE:
        _NC_CACHE["full"] = build()
    nc = _NC_CACHE["full"]
    x = np.asarray(inputs["x"], dtype=np.float32)
    p = np.asarray(inputs["p"], dtype=np.float32)
    in_maps = []
    for b in range(NCORES):
        m = {"x": np.ascontiguousarray(x[b]), "p0": np.ascontiguousarray(p[0, b]), "p1": np.ascontiguousarray(p[1, b])}
        for k in PARAM_SHAPES:
            m[k] = np.ascontiguousarray(np.asarray(inputs[k], dtype=np.float32))
        in_maps.append(m)
    res = run_bass_kernel_spmd(nc, in_maps, core_ids=list(range(NCORES)))
    return np.stack([np.asarray(res.results[b]["out"], dtype=np.float32) for b in range(NCORES)], axis=0)
```
